# Optimizing a Trainium2 kernel written in Bass

```python
import math, functools
import jax, jax.numpy as jnp
from jax import lax
import numpy as np

D_MODEL = 1024
BATCH = 32
SEQ = 256
DEPTH = 4
DEC_BATCH = 2
DEC_SEQ = 4096
PAST_LEN = 512

GRID_W = 64
N_MIXERS = 2
N_A_LAYERS = (DEPTH + 1) // 2
N_B_LAYERS = DEPTH // 2
EXPAND = 2
E_WIDTH = EXPAND * D_MODEL
CHUNK_A = 128
G_A = 8
H_B = 8
D_K = 128
D_V = E_WIDTH // H_B
KW = H_B * D_K
VW = H_B * D_V
CONV_K = 5
CHUNK_B = 64
B_IN_WIDTH = 2 * KW + 2 * VW + 4 * H_B
DEEPNORM_ALPHA = (2.0 * DEPTH) ** 0.25
DEEPNORM_BETA = (8.0 * DEPTH) ** -0.25
LN_EPS = 1e-6
POS_BASE = 10000.0

kernel_name = "hybrid_gmlp_deltanet_diffusion_step"


def layer_norm(x, g, b):
    xf = x.astype(jnp.float32)
    mu = jnp.mean(xf, axis=-1, keepdims=True)
    var = jnp.mean(jnp.square(xf - mu), axis=-1, keepdims=True)
    return ((xf - mu) * lax.rsqrt(var + LN_EPS) * g.astype(jnp.float32) + b.astype(jnp.float32)).astype(x.dtype)


def l2norm(x):
    return x * lax.rsqrt(jnp.sum(jnp.square(x), axis=-1, keepdims=True) + LN_EPS)


def sincos_1d(pos, dim):
    omega = 1.0 / (POS_BASE ** (jnp.arange(dim // 2, dtype=jnp.float32) / (dim // 2)))
    ang = pos.astype(jnp.float32)[:, None] * omega[None, :]
    return jnp.concatenate([jnp.sin(ang), jnp.cos(ang)], axis=-1)


def grid_pos_embed(n_tokens):
    rows = n_tokens // GRID_W
    e_row = sincos_1d(jnp.arange(rows), D_MODEL // 2)
    e_col = sincos_1d(jnp.arange(GRID_W), D_MODEL // 2)
    emb = jnp.concatenate([
        jnp.broadcast_to(e_row[:, None, :], (rows, GRID_W, D_MODEL // 2)),
        jnp.broadcast_to(e_col[None, :, :], (rows, GRID_W, D_MODEL // 2))], axis=-1)
    return emb.reshape(rows * GRID_W, D_MODEL)


def adaln(cond, w, b):
    mod = jax.nn.silu(cond) @ w + b
    return jnp.split(mod[:, None, :], 3, axis=-1)


def chunk_mlp_mixer(h, w_in, v_ln_g, v_ln_b, w_s, b_s, w_out):
    B, N, _ = h.shape
    u, v, z = jnp.split(h @ w_in, 3, axis=-1)
    u = jax.nn.gelu(u)
    v = layer_norm(jax.nn.gelu(v), v_ln_g, v_ln_b)
    nc = N // CHUNK_A
    v = v.reshape(B, nc, CHUNK_A, G_A, E_WIDTH // G_A)
    mixed = jnp.einsum('gpq,bcqgd->bcpgd', w_s, v) + b_s.T[None, None, :, :, None]
    mixed = mixed.reshape(B, N, E_WIDTH)
    return (u * mixed * jax.nn.silu(z)) @ w_out


def short_conv(x, w):
    C = x.shape[-1]
    pad = CONV_K // 2
    return lax.conv_general_dilated(
        x, w[:, None, :].astype(x.dtype), window_strides=(1,), padding=[(pad, pad)],
        dimension_numbers=('NWC', 'WIO', 'NWC'), feature_group_count=C)


def gated_delta_chunked(q, k, v, g, beta, s0):
    B, N, H, _ = q.shape
    nc = N // CHUNK_B

    def to_chunks(t):
        t = t.reshape((B, nc, CHUNK_B) + t.shape[2:])
        return jnp.moveaxis(t, 3, 2)

    qc, kc, vc, gc, bc = (to_chunks(t) for t in (q, k, v, g, beta))
    gcum = jnp.cumsum(gc, axis=-1)
    idx = jnp.arange(CHUNK_B)
    tril_strict = idx[:, None] > idx[None, :]
    tril_incl = idx[:, None] >= idx[None, :]
    diff = gcum[..., :, None] - gcum[..., None, :]
    decay = jnp.exp(jnp.where(tril_incl, diff, -jnp.inf))
    kb = kc * bc[..., None]
    A = jnp.where(tril_strict, jnp.einsum('bnhid,bnhjd->bnhij', kb, kc) * decay, 0.0)
    lhs = A + jnp.eye(CHUNK_B, dtype=A.dtype)
    rhs = jnp.concatenate([vc * bc[..., None], kb * jnp.exp(gcum)[..., None]], axis=-1)
    sol = lax.linalg.triangular_solve(lhs, rhs, left_side=True, lower=True)
    u_c, w_c = sol[..., :D_V], sol[..., D_V:]
    qk = jnp.where(tril_incl, jnp.einsum('bnhid,bnhjd->bnhij', qc, kc) * decay, 0.0)
    q_dec = qc * jnp.exp(gcum)[..., None]
    k_dec = kc * jnp.exp(gcum[..., -1:] - gcum)[..., None]
    g_last = jnp.exp(gcum[..., -1])

    def step(S, xs):
        qk_i, qd_i, kd_i, u_i, w_i, gl_i = xs
        v_new = u_i - jnp.einsum('bhcd,bhde->bhce', w_i, S)
        o = jnp.einsum('bhcd,bhde->bhce', qd_i, S) + jnp.einsum('bhij,bhje->bhie', qk_i, v_new)
        S = S * gl_i[..., None, None] + jnp.einsum('bhcd,bhce->bhde', kd_i, v_new)
        return S, o

    xs = tuple(jnp.moveaxis(t, 1, 0) for t in (qk, q_dec, k_dec, u_c, w_c, g_last))
    s_fin, o = lax.scan(step, s0, xs)
    o = jnp.moveaxis(jnp.moveaxis(o, 0, 1), 3, 2).reshape(B, N, H, D_V)
    return o, s_fin


def delta_mixer(h, s0_f, s0_b, w_in, conv_w, A_log, dt_bias, norm_g, w_out):
    B, N, _ = h.shape
    proj = h @ w_in
    qkv, z, ab = jnp.split(proj, [2 * KW + VW, 2 * KW + 2 * VW], axis=-1)
    qkv = jax.nn.silu(short_conv(qkv, conv_w)).astype(jnp.float32)
    q, k, v = jnp.split(qkv, [KW, 2 * KW], axis=-1)
    q = l2norm(q.reshape(B, N, H_B, D_K)) * (D_K ** -0.5)
    k = l2norm(k.reshape(B, N, H_B, D_K))
    v = v.reshape(B, N, H_B, D_V)
    ab = ab.astype(jnp.float32).reshape(B, N, 2, 2, H_B)
    g = -jnp.exp(A_log.astype(jnp.float32)) * jax.nn.softplus(ab[:, :, 0] + dt_bias.astype(jnp.float32))
    beta = jax.nn.sigmoid(ab[:, :, 1])
    o_f, s_f = gated_delta_chunked(q, k, v, g[:, :, 0], beta[:, :, 0], s0_f.astype(jnp.float32))
    flip = lambda t: jnp.flip(t, axis=1)
    o_b, s_b = gated_delta_chunked(flip(q), flip(k), flip(v), flip(g[:, :, 1]), flip(beta[:, :, 1]),
                                   s0_b.astype(jnp.float32))
    o = o_f + flip(o_b)
    o = o * lax.rsqrt(jnp.mean(jnp.square(o), axis=-1, keepdims=True) + LN_EPS) * norm_g.astype(jnp.float32)
    y = o.reshape(B, N, VW).astype(h.dtype) * jax.nn.silu(z)
    return y @ w_out, s_f.astype(h.dtype), s_b.astype(h.dtype)


def run_trunk(x, cond, init_states, w_ada, b_ada, ln_g, ln_b,
              a_w_in, a_ln_g, a_ln_b, a_w_s, a_b_s, a_w_out,
              b_w_in, b_conv_w, b_A_log, b_dt_bias, b_norm_g, b_w_out):
    finals = []
    for i in range(DEPTH):
        shift, scale, gate = adaln(cond, w_ada[i], b_ada[i])
        h = x * (1.0 + scale) + shift
        j = i // N_MIXERS
        if i % N_MIXERS == 0:
            y = chunk_mlp_mixer(h, a_w_in[j], a_ln_g[j], a_ln_b[j], a_w_s[j], a_b_s[j], a_w_out[j])
        else:
            y, s_f, s_b = delta_mixer(h, init_states[:, j, 0], init_states[:, j, 1], b_w_in[j], b_conv_w[j],
                                      b_A_log[j], b_dt_bias[j], b_norm_g[j], b_w_out[j])
            finals.append(jnp.stack([s_f, s_b], axis=1))
        x = layer_norm(DEEPNORM_ALPHA * x + gate * y, ln_g[i], ln_b[i])
    return x, jnp.stack(finals, axis=1)


def setup_inputs(seed: int = 0) -> dict:
    key = jax.random.key(seed)
    ks = jax.random.split(key, 24)
    nrm = lambda k, shape, s: jax.random.normal(k, shape, jnp.float32) * s
    dt = jnp.exp(jax.random.uniform(ks[20], (N_B_LAYERS, 2, H_B), jnp.float32,
                                    math.log(1e-3), math.log(1e-1)))
    return {
        "x_prompt": nrm(ks[0], (BATCH, SEQ, D_MODEL), 1.0),
        "x_sample": nrm(ks[1], (DEC_BATCH, DEC_SEQ, D_MODEL), 1.0),
        "state_delta": nrm(ks[2], (DEC_BATCH, N_B_LAYERS, 2, H_B, D_K, D_V), 0.1),
        "c": nrm(ks[3], (DEC_BATCH, D_MODEL), 1.0),
        "c_ctx": nrm(ks[4], (D_MODEL,), 1.0),
        "w_ada": nrm(ks[5], (DEPTH, D_MODEL, 3 * D_MODEL), D_MODEL ** -0.5),
        "b_ada": nrm(ks[6], (DEPTH, 3 * D_MODEL), 0.02),
        "ln_g": 1.0 + nrm(ks[7], (DEPTH, D_MODEL), 0.02),
        "ln_b": nrm(ks[8], (DEPTH, D_MODEL), 0.02),
        "a_w_in": nrm(ks[9], (N_A_LAYERS, D_MODEL, 3 * E_WIDTH), D_MODEL ** -0.5),
        "a_ln_g": 1.0 + nrm(ks[10], (N_A_LAYERS, E_WIDTH), 0.02),
        "a_ln_b": nrm(ks[11], (N_A_LAYERS, E_WIDTH), 0.02),
        "a_w_s": nrm(ks[12], (N_A_LAYERS, G_A, CHUNK_A, CHUNK_A), CHUNK_A ** -0.5),
        "a_b_s": 1.0 + nrm(ks[13], (N_A_LAYERS, G_A, CHUNK_A), 0.02),
        "a_w_out": nrm(ks[14], (N_A_LAYERS, E_WIDTH, D_MODEL), DEEPNORM_BETA * E_WIDTH ** -0.5),
        "b_w_in": nrm(ks[15], (N_B_LAYERS, D_MODEL, B_IN_WIDTH), D_MODEL ** -0.5),
        "b_conv_w": nrm(ks[16], (N_B_LAYERS, CONV_K, 2 * KW + VW), CONV_K ** -0.5),
        "b_A_log": jnp.log(jax.random.uniform(ks[17], (N_B_LAYERS, 2, H_B), jnp.float32, 1.0, 16.0)),
        "b_dt_bias": dt + jnp.log(-jnp.expm1(-dt)),
        "b_norm_g": 1.0 + nrm(ks[18], (N_B_LAYERS, D_V), 0.02),
        "b_w_out": nrm(ks[19], (N_B_LAYERS, VW, D_MODEL), DEEPNORM_BETA * VW ** -0.5),
    }


def reference(x_prompt, x_sample, state_delta, c, c_ctx, w_ada, b_ada, ln_g, ln_b,
              a_w_in, a_ln_g, a_ln_b, a_w_s, a_b_s, a_w_out,
              b_w_in, b_conv_w, b_A_log, b_dt_bias, b_norm_g, b_w_out):
    trunk = functools.partial(run_trunk, w_ada=w_ada, b_ada=b_ada, ln_g=ln_g, ln_b=ln_b,
                              a_w_in=a_w_in, a_ln_g=a_ln_g, a_ln_b=a_ln_b, a_w_s=a_w_s, a_b_s=a_b_s,
                              a_w_out=a_w_out, b_w_in=b_w_in, b_conv_w=b_conv_w, b_A_log=b_A_log,
                              b_dt_bias=b_dt_bias, b_norm_g=b_norm_g, b_w_out=b_w_out)
    zero_states = jnp.zeros((x_prompt.shape[0], N_B_LAYERS, 2, H_B, D_K, D_V), x_prompt.dtype)
    y_prompt, new_state_delta = trunk(x_prompt, c_ctx[None, :], zero_states)
    x_lat = x_sample + grid_pos_embed(x_sample.shape[1]).astype(x_sample.dtype)[None]
    y_sample, _ = trunk(x_lat, c, state_delta)
    return (y_prompt, y_sample, new_state_delta)
```

```python
import numpy as np
from contextlib import ExitStack
import concourse.bass as bass
import concourse.mybir as mybir
from concourse.bass_utils import run_bass_kernel_spmd

F32 = mybir.dt.float32
BF16 = mybir.dt.bfloat16
AF = mybir.ActivationFunctionType
ALU = mybir.AluOpType

D = 1024
E = 2048
KW = 1024
DK = 128
DV = 256
NH = 8
BW = 6176
ALPHA = (2.0 * 4) ** 0.25
EPS = 1e-6
NEG = -1.0e30


class Buf:
    __slots__ = ("name", "w", "r", "x")

    def __init__(self, name="", x=False):
        self.name = name
        self.w = None
        self.r = {}
        self.x = x


class Prog:
    ENGS = ("pe", "dve", "act", "pool", "sp")
    NSLOT = 8
    SAME_ENG_SKIP = 12

    def __init__(self):
        self.ops = {e: [] for e in self.ENGS}
        self.cnt = {e: 0 for e in self.ENGS}
        self.waited = {e: {} for e in self.ENGS}
        self.dcnt = {e: 0 for e in self.ENGS}

    def _deps(self, eng, reads, writes):
        deps = {}

        def add(p):
            if p is None:
                return
            k = p[0]
            if k not in deps or deps[k][1] < p[1]:
                deps[k] = p
        for b in reads:
            add(b.w)
        for b in writes:
            add(b.w)
            for p in b.r.values():
                add(p)
        waits = []
        for k, p in deps.items():
            val = p[1]
            if k == eng and self.cnt[eng] - p[3] > self.SAME_ENG_SKIP:
                continue
            if self.waited[eng].get(k, 0) >= val:
                continue
            self.waited[eng][k] = val
            waits.append((k, val))
        return waits

    def _record(self, reads, writes, prod):
        k = prod[0]
        for b in reads:
            if k not in b.r or b.r[k][1] < prod[1]:
                b.r[k] = prod
        for b in writes:
            b.w = prod
            b.r = {}

    def op(self, eng, fn, reads=(), writes=()):
        xs = [b for b in reads if b.x]
        if xs:
            writes = list(writes) + xs
        waits = self._deps(eng, reads, writes)
        idx = self.cnt[eng]
        self.cnt[eng] = idx + 1
        self.ops[eng].append((waits, fn, (eng, 1)))
        self._record(reads, writes, (eng, idx + 1, eng, idx))

    def dma(self, q, fn, reads=(), writes=()):
        waits = self._deps(q, reads, writes)
        i = self.dcnt[q]
        self.dcnt[q] = i + 1
        k = ("dma", q, i % self.NSLOT)
        prev = 16 * (i // self.NSLOT)
        if prev > 0 and self.waited[q].get(k, 0) < prev:
            self.waited[q][k] = prev
            waits.append((k, prev))
        self.ops[q].append((waits, fn, (k, 16)))
        self._record(reads, writes, (k, prev + 16, None, None))

    def barrier(self):
        tgt = [(e, self.cnt[e]) for e in self.ENGS if self.cnt[e] > 0]
        for q in self.ENGS:
            n = self.dcnt[q]
            for slot in range(min(n, self.NSLOT)):
                tgt.append((("dma", q, slot), 16 * ((n - 1 - slot) // self.NSLOT + 1)))
        for e in self.ENGS:
            waits = []
            for k, v in tgt:
                if k == e:
                    continue
                if self.waited[e].get(k, 0) >= v:
                    continue
                self.waited[e][k] = v
                waits.append((k, v))
            if waits:
                self.ops[e].append((waits, None, None))

    def replay(self, nc):
        semkeys = list(self.ENGS)
        for q in self.ENGS:
            for s in range(min(self.dcnt[q], self.NSLOT)):
                semkeys.append(("dma", q, s))
        with ExitStack() as st:
            sems = {}
            for k in semkeys:
                nm = k if isinstance(k, str) else f"d_{k[1]}_{k[2]}"
                sems[k] = st.enter_context(nc.semaphore("s_" + nm))
            block = st.enter_context(nc.Block())
            engmap = {"pe": "tensor", "dve": "vector", "act": "scalar", "pool": "gpsimd", "sp": "sync"}

            def mk(ename):
                oplist = self.ops[ename]

                def body(e):
                    for waits, fn, inc in oplist:
                        for k, v in waits:
                            e.wait_ge(sems[k], v)
                        if fn is not None:
                            fn(e).then_inc(sems[inc[0]], inc[1])
                return body

            for ename in self.ENGS:
                if self.ops[ename]:
                    getattr(block, engmap[ename])(mk(ename))


class Builder:
    def __init__(self, NP, LP, LS, depth=4, dbg=False):
        self.NP, self.LP, self.LS, self.depth = NP, LP, LS, depth
        self.nc = nc = bass.Bass("TRN2", target_bir_lowering=False)
        self.P = Prog()
        self.st = ExitStack()
        self.groups = [dict(n_seq=NP, L=LP, cond=0, ntok=NP * LP), dict(n_seq=1, L=LS, cond=1, ntok=LS)]
        nA = (depth + 1) // 2
        nB = depth // 2
        self.nA, self.nB = nA, nB

        def din(name, shape, dt=F32):
            return nc.dram_tensor(name, list(shape), dt, kind="ExternalInput").ap()

        def dout(name, shape, dt=F32):
            return nc.dram_tensor(name, list(shape), dt, kind="ExternalOutput").ap()

        def dint(name, shape, dt=F32):
            return nc.dram_tensor(name, list(shape), dt, kind="Internal").ap()

        self.xin = [din("xp", [NP * LP, D]), din("xs", [LS, D])]
        self.pos = din("pos", [LS, D])
        self.condT = din("condT", [128, 16])
        self.s0 = din("s0", [max(nB, 1), 2, NH, DK, DV])
        self.w_ada = din("w_ada", [depth, D, 3 * D])
        self.b_adaT = din("b_adaT", [128, depth * 24])
        self.ln_g = din("ln_g", [depth, D])
        self.ln_b = din("ln_b", [depth, D])
        self.a_w_in = din("a_w_in", [nA, D, 3 * E])
        self.a_ln_gT = din("a_ln_gT", [128, nA * 16])
        self.a_ln_bT = din("a_ln_bT", [128, nA * 16])
        self.a_w_sT = din("a_w_sT", [nA, 128, 8 * 128])
        self.a_b_s = din("a_b_s", [nA, 8 * 128])
        self.a_w_out = din("a_w_out", [nA, E, D])
        self.b_w_in = din("b_w_in", [max(nB, 1), D, BW])
        self.b_convT = din("b_convT", [128, max(nB, 1) * 32 * 5])
        self.b_alogP = din("b_alogP", [max(nB, 1), 128, 1])
        self.b_dtbP = din("b_dtbP", [max(nB, 1), 128, 1])
        self.b_norm_g = din("b_norm_g", [max(nB, 1), DV])
        self.pmask_d = din("pmask", [128, 4])
        self.lvs_d = din("lvs", [128, 128])
        self.b_w_out = din("b_w_out", [max(nB, 1), E, D])
        self.yout = [dout("yp", [NP * LP, D]), dout("ys", [LS, D])]
        self.ns = dout("ns", [NP, max(nB, 1), 2, NH, DK, DV])
        self.X = [dint("X0", [NP * LP, D]), dint("X1", [LS, D])]
        self.wa_in = dint("wa_in", [nA, D, 3 * E], BF16)
        self.wa_out = dint("wa_out", [nA, E, D], BF16)
        self.wb_in = dint("wb_in", [max(nB, 1), D, BW], BF16)
        self.wb_out = dint("wb_out", [max(nB, 1), E, D], BF16)
        self.HTd = [dint("HTd0", [8, 128, NP * LP], BF16), dint("HTd1", [8, 128, LS], BF16)]
        self.YTd = [dint("YTd0", [16, 128, NP * LP], BF16), dint("YTd1", [16, 128, LS], BF16)]
        self.dbg = None
        if dbg:
            self.dbg = dout("dbg", [128, 4096])
        self.Bw = {}
        self.BX = [[Buf(f"X{g}_{t}") for t in range(self.groups[g]["ntok"] // 128)] for g in range(2)]
        self.BHT = [[Buf() for _ in range(self.groups[g]["ntok"] // 128)] for g in range(2)]
        self.BYT = [[[Buf() for _ in range(self.groups[g]["ntok"] // 128)] for _h in range(NH)] for g in range(2)]
        self.BNS = Buf("ns")
        self.BYO = Buf("yout")
        self._ps_i = 0

    def sb(self, name, shape, dt):
        return self.st.enter_context(self.nc.sbuf_tensor(name, list(shape), dt))

    def MM(self, out, lhsT, rhs, start, stop, R, W):
        self.P.op("pe", lambda e: e.matmul(out, lhsT=lhsT, rhs=rhs, start=start, stop=stop), R, W)

    def TR(self, out, in_, ident, R, W):
        self.P.op("pe", lambda e: e.transpose(out, in_, ident), R, W)

    def ACT(self, out, in_, func, R, W, bias=None, scale=None, accum=None):
        kw = {}
        if bias is not None:
            kw["bias"] = bias
        if scale is not None:
            kw["scale"] = scale
        if accum is not None:
            kw["accum_out"] = accum
        self.P.op("act", lambda e: e.activation(out=out, in_=in_, func=func, **kw), R, W)

    def TS(self, eng, out, in0, s1, s2, op0, op1, R, W):
        if op1 is None:
            self.P.op(eng, lambda e: e.tensor_scalar(out=out, in0=in0, scalar1=s1, scalar2=None, op0=op0), R, W)
        else:
            self.P.op(eng, lambda e: e.tensor_scalar(out=out, in0=in0, scalar1=s1, scalar2=s2, op0=op0, op1=op1), R, W)

    def STT(self, out, in0, scalar, in1, op0, op1, R, W):
        self.P.op("dve", lambda e: e.scalar_tensor_tensor(out=out, in0=in0, scalar=scalar, in1=in1, op0=op0, op1=op1), R, W)

    def TT(self, eng, out, in0, in1, op, R, W):
        self.P.op(eng, lambda e: e.tensor_tensor(out=out, in0=in0, in1=in1, op=op), R, W)

    def CP(self, eng, out, in_, R, W):
        if eng == "act":
            self.ACT(out, in_, AF.Copy, R, W)
        else:
            self.P.op(eng, lambda e: e.tensor_copy(out=out, in_=in_), R, W)

    def MSET(self, eng, ap, val, W):
        self.P.op(eng, lambda e: e.memset(ap, val), (), W)

    def DMA(self, q, out, in_, R, W):
        self.P.dma(q, lambda e: e.dma_start(out=out, in_=in_), R, W)

    def dump(self, src, col0, n, R):
        if self.dbg is None:
            return
        if not hasattr(self, "dbgst"):
            self.dbgst = self.sb("dbgst", [128, 512], F32)
            self.Bdbg = Buf("dbg")
        self.CP("act", self.dbgst[:, 0:n], src, R, [self.Bdbg])
        self.DMA("sp", self.dbg[:, col0:col0 + n], self.dbgst[:, 0:n], [self.Bdbg], [])

    def ps(self):
        i = self._ps_i
        self._ps_i = (i + 1) % len(self.psb)
        return self.psb[i], self.Bps[i]

    def build(self):
        nc = self.nc
        sb = self.sb
        self.psb = [self.st.enter_context(nc.psum_tensor(f"ps{i}", [128, 512], F32)) for i in range(8)]
        self.Bps = [Buf(f"ps{i}", x=True) for i in range(8)]
        self.ident = sb("ident", [128, 128], F32)
        self.identb = sb("identb", [128, 128], BF16)
        self.onesf = sb("onesf", [128, 128], F32)
        self.onesb = sb("onesb", [128, 128], BF16)
        self.Bc = Buf("consts")
        self.MSET("pool", self.onesf[:], 1.0, [self.Bc])
        self.P.op("pool", lambda e: e.affine_select(out=self.ident[:], in_=self.onesf[:], pattern=[[-1, 128]],
                                                     compare_op=ALU.is_equal, fill=0.0, base=0, channel_multiplier=1),
                  [self.Bc], [self.Bc])
        self.CP("dve", self.identb[:], self.ident[:], [self.Bc], [self.Bc])
        self.CP("dve", self.onesb[:], self.onesf[:], [self.Bc], [self.Bc])
        self.slabf = sb("slabf", [128, 12 * 1024], F32)
        self.slabb = sb("slabb", [128, 49 * 1024], BF16)
        self.sb_wo = sb("WO", [128, 16, 1024], BF16)
        self.MOD = sb("MOD", [128, self.depth * 48], F32)
        self.SC1 = sb("SC1", [128, self.depth * 16], F32)
        self.GATE = sb("GATE", [128, 2 * D], F32)
        self.LNG = sb("LNG", [128, D], F32)
        self.LNB = sb("LNB", [128, D], F32)
        self.BGATE = Buf("gate")
        self.BLN = Buf("ln")
        self.BMOD = Buf("mod")

        import os
        stage = int(os.environ.get("KSTAGE", "99"))
        if stage >= 1:
            self.weight_casts()
        self.small_consts()
        if stage >= 2:
            self.prologue()
            self.P.barrier()
        if stage >= 3:
            self.adaln()
        for i in range(self.depth):
            if stage < 4:
                break
            self.P.barrier()
            self.layer_consts(i)
            self.P.barrier()
            if stage < 5:
                break
            if i % 2 == 0:
                self.layer_a(i)
            else:
                self.layer_b(i)
        self.P.barrier()
        self.P.replay(nc)
        self.st.close()
        return nc

    def weight_casts(self):
        for nm, src, dst, n, rows in (("a_in", self.a_w_in, self.wa_in, self.nA, D), ("a_out", self.a_w_out, self.wa_out, self.nA, E),
                                      ("b_in", self.b_w_in, self.wb_in, self.nB, D), ("b_out", self.b_w_out, self.wb_out, self.nB, E)):
            for l in range(n):
                b = Buf(f"w_{nm}{l}")
                self.Bw[(nm, l)] = b
                for r0 in range(0, rows, 256):
                    self.DMA("pool", dst[l, r0:r0 + 256, :], src[l, r0:r0 + 256, :], [], [b])

    def adaln(self):
        P = self.P
        sf = self.slabf
        cT = sf[:, 0:16]
        sT = sf[:, 16:32]
        bT = sf[:, 32:32 + self.depth * 24]
        W = sf[:, 1024:1024 + 8192].rearrange("p (k n) -> p k n", n=1024)
        Bs = Buf("ada_s")
        Bb = Buf("ada_b")
        BW_ = Buf("adaw")
        self.DMA("sp", cT, self.condT[:, :], [], [Bs])
        self.DMA("sp", bT, self.b_adaT[:, :], [], [Bb])
        self.ACT(sT, cT, AF.Silu, [Bs], [Bs])
        for i in range(self.depth):
            pt, bp = self.ps()
            for part in range(3):
                self.DMA("sp", W, self.w_ada[i, :, part * 1024:(part + 1) * 1024].rearrange("(k p) n -> p k n", p=128), [], [BW_])
                for c8 in range(8):
                    ch = part * 8 + c8
                    for kc in range(8):
                        self.MM(pt[:, ch * 2:ch * 2 + 2], W[:, kc, c8 * 128:(c8 + 1) * 128], sT[:, kc * 2:kc * 2 + 2],
                                kc == 0, kc == 7, [BW_, Bs], [bp])
            mod = self.MOD[:, i * 48:(i + 1) * 48].rearrange("p (ch c) -> p ch c", c=2)
            self.TT("dve", mod, pt[:, 0:48].rearrange("p (ch c) -> p ch c", c=2),
                    bT[:, i * 24:(i + 1) * 24].unsqueeze(2).broadcast_to([128, 24, 2]), ALU.add, [bp, Bb], [self.BMOD])
            self.TS("dve", self.SC1[:, i * 16:(i + 1) * 16], self.MOD[:, i * 48 + 16:i * 48 + 32], 1.0, None, ALU.add, None,
                    [self.BMOD], [self.BMOD])
        if self.dbg is not None:
            self.DMA("sp", self.dbg[:, 0:48 * self.depth], self.MOD[:, :], [self.BMOD], [])
            self.DMA("sp", self.dbg[:, 512:512 + 16 * self.depth], self.SC1[:, :], [self.BMOD], [])

    def shift_ap(self, i, kc, c):
        o = i * 48 + kc * 2 + c
        return self.MOD[:, o:o + 1]

    def scale1_ap(self, i, kc, c):
        o = i * 16 + kc * 2 + c
        return self.SC1[:, o:o + 1]

    def layer_consts(self, i):
        gb = self.slabf[:, 0:128]
        Bg = Buf("gb")
        for c in range(2):
            for half in range(2):
                pt, bp = self.ps()
                for q in range(4):
                    kc = half * 4 + q
                    o = i * 48 + (16 + kc) * 2 + c
                    self.CP("dve", gb, self.MOD[:, o:o + 1].broadcast_to([128, 128]), [self.BMOD], [Bg])
                    self.MM(pt[:, q * 128:(q + 1) * 128], gb, self.ident[:], True, True, [Bg, self.Bc], [bp])
                self.CP("act", self.GATE[:, c * D + half * 512:c * D + (half + 1) * 512], pt[:, :], [bp], [self.BGATE])
        self.DMA("sp", self.LNG[:], self.ln_g[i:i + 1, :].partition_broadcast(128), [], [self.BLN])
        self.DMA("sp", self.LNB[:], self.ln_b[i:i + 1, :].partition_broadcast(128), [], [self.BLN])

    def prologue(self):
        sf = self.slabf
        A = sf[:, 0:4096].rearrange("p (s d) -> p s d", d=1024)
        Bq = sf[:, 4096:8192].rearrange("p (s d) -> p s d", d=1024)
        Ba, Bb = Buf("pa"), Buf("pb")
        for t4 in range(self.LS // 512):
            rows = slice(t4 * 512, (t4 + 1) * 512)
            self.DMA("sp", A, self.xin[1][rows, :].rearrange("(s p) d -> p s d", p=128), [], [Ba])
            self.DMA("sp", Bq, self.pos[rows, :].rearrange("(s p) d -> p s d", p=128), [], [Bb])
            self.TT("dve", A, A, Bq, ALU.add, [Ba, Bb], [Ba])
            self.DMA("pool", self.X[1][rows, :].rearrange("(s p) d -> p s d", p=128), A, [Ba],
                     [self.BX[1][t4 * 4 + s] for s in range(4)])

    def load_x_tile(self, i, g, t, XT, Bx, nsub):
        first = (i == 0 and g == 0)
        src = self.xin[g] if first else self.X[g]
        rows = slice(t * 128, (t + nsub) * 128)
        R = [] if first else [self.BX[g][t + s] for s in range(nsub)]
        self.DMA("sp", XT, src[rows, :].rearrange("(s p) d -> p s d", p=128), R, [Bx])

    def make_ht(self, i, c, XT, Bx, nsub, HT, Bh):
        for kc in range(8):
            pt, bp = self.ps()
            for s in range(nsub):
                self.TR(pt[:, s * 128:(s + 1) * 128], XT[:, s, kc * 128:(kc + 1) * 128], self.ident[:], [Bx, self.Bc], [bp])
            self.TS("dve", HT[:, kc, 0:nsub * 128], pt[:, 0:nsub * 128], self.scale1_ap(i, kc, c), self.shift_ap(i, kc, c),
                    ALU.mult, ALU.add, [bp, self.BMOD], [Bh])

    def epilogue(self, i, g, c, t, pts, xrow, Bx, R1, XN, Br, last):
        import os
        kep = int(os.environ.get("KEP", "99"))
        if kep < 1:
            return
        for half in range(2):
            pt, bp = pts[half]
            hs = slice(half * 512, (half + 1) * 512)
            self.TT("dve", R1[:, hs], pt[:, :], self.GATE[:, c * D + half * 512:c * D + (half + 1) * 512], ALU.mult,
                    [bp, self.BGATE], [Br])
        self.STT(R1, xrow, ALPHA, R1, ALU.mult, ALU.add, [Bx, Br], [Br])
        if kep < 2:
            return
        st6 = self.stat[:, 0:12]
        mv = self.stat[:, 12:14]
        rs = self.stat[:, 14:15]
        for half in range(2):
            self.P.op("dve", (lambda o, a: lambda e: e.bn_stats(out=o, in_=a))(st6[:, half * 6:(half + 1) * 6], R1[:, half * 512:(half + 1) * 512]),
                      [Br], [self.Bstat])
        self.P.op("dve", lambda e: e.bn_aggr(out=mv, in_=st6), [self.Bstat], [self.Bstat])
        self.ACT(rs, mv[:, 1:2], AF.Sqrt, [self.Bstat], [self.Bstat], bias=self.epsc[:, 0:1])
        self.P.op("dve", lambda e: e.reciprocal(out=rs, in_=rs), [self.Bstat], [self.Bstat])
        if kep < 3:
            return
        self.TS("dve", XN, R1, mv[:, 0:1], rs, ALU.subtract, ALU.mult, [Br, self.Bstat], [Br])
        if kep < 4:
            return
        self.TT("pool", XN, XN, self.LNG[:], ALU.mult, [Br, self.BLN], [Br])
        self.TT("pool", XN, XN, self.LNB[:], ALU.add, [Br, self.BLN], [Br])
        if kep < 5:
            return
        if last:
            self.DMA("sp", self.yout[g][t * 128:(t + 1) * 128, :], XN, [Br], [self.BYO])
        else:
            self.DMA("sp", self.X[g][t * 128:(t + 1) * 128, :], XN, [Br], [self.BX[g][t]])

    def small_consts(self):
        if hasattr(self, "stat"):
            return
        self.stat = self.sb("stat", [128, 16], F32)
        self.Bstat = Buf("stat")
        self.epsc = self.sb("epsc", [128, 2], F32)
        self.MSET("dve", self.epsc[:, 0:1], EPS, [self.Bc])
        self.MSET("dve", self.epsc[:, 1:2], 1.0, [self.Bc])

    def layer_a(self, i):
        self.small_consts()
        l = i // 2
        last = (i == self.depth - 1)
        sf, sbb = self.slabf, self.slabb
        XT = sf[:, 0:4096].rearrange("p (s d) -> p s d", d=1024)
        BIAS = sf[:, 4096:6144].rearrange("p (c q) -> p c q", q=128)
        R1 = sf[:, 6144:8192].rearrange("p (b n) -> p b n", n=1024)
        T1 = sf[:, 8192:9216].rearrange("p (b n) -> p b n", n=512)
        WSF = sf[:, 9216:10240]
        BSB = sf[:, 10240:11264].rearrange("p (g q) -> p g q", q=128)
        RS = sf[:, 11264:11392]
        GLN = sf[:, 11392:11408]
        BLNv = sf[:, 11408:11424]
        vst = sf[:, 11424:11424 + 64]
        HT = sbb[:, 0:4096].rearrange("p (k n) -> p k n", n=512)
        U = sbb[:, 4096:12288].rearrange("p (c n) -> p c n", n=512)
        Z = sbb[:, 12288:20480].rearrange("p (c n) -> p c n", n=512)
        VGa = sbb[:, 20480:28672].rearrange("p (b n) -> p b n", n=2048)
        VHa = sbb[:, 28672:36864].rearrange("p (b n) -> p b n", n=2048)
        WB = sbb[:, 36864:49152].rearrange("p (b k n) -> p b k n", k=8, n=512)
        WST = self.wst[:].rearrange("p (g q) -> p g q", q=128)
        WO = self.sb_wo[:]
        self.VGa, self.VHa = VGa, VHa
        Bxt, Bht, Bwst, Bbias, Bwo = Buf("XT"), Buf("HT"), Buf("WST"), Buf("BIAS"), Buf("WO")
        BU = [Buf() for _ in range(16)]
        BZ = [Buf() for _ in range(16)]
        BWB = [Buf(), Buf(), Buf()]
        BR = [Buf(), Buf()]
        BT1 = [Buf(), Buf()]
        Bsm = Buf("small")
        self.DMA("sp", WSF, self.a_w_sT[l, :, :], [], [Bwst])
        self.CP("dve", WST.rearrange("p g q -> p (g q)"), WSF, [Bwst], [Bwst])
        self.DMA("sp", GLN, self.a_ln_gT[:, l * 16:(l + 1) * 16], [], [Bsm])
        self.DMA("sp", BLNv, self.a_ln_bT[:, l * 16:(l + 1) * 16], [], [Bsm])
        self.DMA("sp", BSB.rearrange("p g q -> p (g q)"), self.a_b_s[l:l + 1, :].partition_broadcast(128), [], [Bsm])
        for gq in range(8):
            pt, bp = self.ps()
            self.MM(pt[:, 0:128], self.onesb[:], WST[:, gq, :], True, True, [self.Bc, Bwst], [bp])
            self.CP("act", RS, pt[:, 0:128], [bp], [Bsm])
            for cc in range(2):
                ch = gq * 2 + cc
                self.STT(BIAS[:, ch, :], RS, BLNv[:, ch:ch + 1], BSB[:, gq, :], ALU.mult, ALU.add, [Bsm], [Bbias])
        import os
        sub = int(os.environ.get("KSUB", "99"))
        if sub < 1:
            return
        self.DMA("sp", WO, self.wa_out[l].rearrange("(k p) n -> p k n", p=128), [self.Bw[("a_out", l)]], [Bwo])
        wsrc = self.wa_in[l]
        wcnt = [0]

        def load_w(col0):
            b = wcnt[0] % 3
            wcnt[0] += 1
            self.DMA("sp", WB[:, b, :, :], wsrc[:, col0:col0 + 512].rearrange("(k p) n -> p k n", p=128),
                     [self.Bw[("a_in", l)]], [BWB[b]])
            return b

        for g, G in enumerate(self.groups):
            c = G["cond"]
            for t4 in range(G["ntok"] // 512):
                t = t4 * 4
                self.load_x_tile(i, g, t, XT, Bxt, 4)
                self.make_ht(i, c, XT, Bxt, 4, HT, Bht)
                if sub < 2:
                    continue
                blocks = [("v", 2048 + b * 512, b) for b in range(4)] + [("u", b * 512, b) for b in range(4)] + \
                         [("z", 4096 + b * 512, b) for b in range(4)]
                nxt = load_w(blocks[0][1])
                for bi, (kind, col0, b4) in enumerate(blocks):
                    wb = nxt
                    if bi + 1 < len(blocks):
                        nxt = load_w(blocks[bi + 1][1])
                    if kind == "v":
                        for s in range(4):
                            pt, bp = self.ps()
                            for kc in range(8):
                                self.MM(pt[:, :], HT[:, kc, s * 128:(s + 1) * 128], WB[:, wb, kc, :], kc == 0, kc == 7,
                                        [Bht, BWB[wb]], [bp])
                            self.ACT(self._vg(sf, s)[:, b4 * 512:(b4 + 1) * 512],
                                     pt[:, :], AF.Gelu_apprx_tanh, [bp], [self.BVG4[s]])
                    else:
                        dst, Bd, fn = (U, BU, AF.Gelu_apprx_tanh) if kind == "u" else (Z, BZ, AF.Silu)
                        for cc in range(4):
                            ch = b4 * 4 + cc
                            pt, bp = self.ps()
                            for kc in range(8):
                                self.MM(pt[:, :], WB[:, wb, kc, cc * 128:(cc + 1) * 128], HT[:, kc, :], kc == 0, kc == 7,
                                        [Bht, BWB[wb]], [bp])
                            self.ACT(dst[:, ch, :], pt[:, :], fn, [bp], [Bd[ch]])
                    if kind == "v" and b4 == 3:
                        for s in range(4):
                            vg = self._vg(sf, s)
                            st = vst[:, 0:24]
                            mv = vst[:, 24:26]
                            rs = vst[:, 26:27]
                            for q in range(4):
                                self.P.op("dve", (lambda o, a: lambda e: e.bn_stats(out=o, in_=a))(st[:, q * 6:(q + 1) * 6], vg[:, q * 512:(q + 1) * 512]),
                                          [self.BVG4[s]], [Bsm])
                            self.P.op("dve", lambda e: e.bn_aggr(out=mv, in_=st), [Bsm], [Bsm])
                            self.ACT(rs, mv[:, 1:2], AF.Sqrt, [Bsm], [Bsm], bias=self.epsc[:, 0:1])
                            self.P.op("dve", lambda e: e.reciprocal(out=rs, in_=rs), [Bsm], [Bsm])
                            self.TS("dve", self._vh(sbb, s), vg, mv[:, 0:1], rs, ALU.subtract, ALU.mult, [self.BVG4[s], Bsm], [self.BVH4[s]])
                if sub < 3:
                    continue
                for ch in range(16):
                    self.TT("pool", U[:, ch, :], U[:, ch, :], Z[:, ch, :], ALU.mult, [BU[ch], BZ[ch]], [BU[ch]])
                for ch in range(16):
                    pt, bp = self.ps()
                    for s in range(4):
                        self.MM(pt[:, s * 128:(s + 1) * 128], self._vh(sbb, s)[:, ch * 128:(ch + 1) * 128], WST[:, ch // 2, :], True, True,
                                [self.BVH4[s], Bwst], [bp])
                    tb = ch % 2
                    self.STT(T1[:, tb, :].rearrange("p (s q) -> p s q", q=128), pt[:, :].rearrange("p (s q) -> p s q", q=128),
                             GLN[:, ch:ch + 1], BIAS[:, ch, :].unsqueeze(1).broadcast_to([128, 4, 128]), ALU.mult, ALU.add,
                             [bp, Bsm, Bbias], [BT1[tb]])
                    self.TT("pool", Z[:, ch, :], T1[:, tb, :], U[:, ch, :], ALU.mult, [BT1[tb], BU[ch]], [BZ[ch]])
                if sub < 4:
                    continue
                for s in range(4):
                    pts = []
                    for half in range(2):
                        pt, bp = self.ps()
                        for ec in range(16):
                            self.MM(pt[:, :], Z[:, ec, s * 128:(s + 1) * 128], WO[:, ec, half * 512:(half + 1) * 512], ec == 0, ec == 15,
                                    [BZ[ec], Bwo], [bp])
                        pts.append((pt, bp))
                    rb = s % 2
                    self.epilogue(i, g, c, t + s, pts, XT[:, s, :], Bxt, R1[:, rb, :], R1[:, rb, :], BR[rb], last)

    def _vg(self, sf, s):
        return self.VGa[:, s, :]

    def _vh(self, sbb, s):
        return self.VHa[:, s, :]

    def layer_b_consts(self, j):
        if not hasattr(self, "negcat"):
            self.negcat = self.sb("negcat", [128, 2, 256], F32)
            self.cw = self.sb("cw", [128, 160], F32)
            self.ngt = self.sb("ngt", [128, 256], F32)
            self.gsc = self.sb("gsc", [128, 4], F32)
            self.pmask = self.sb("pmaskt", [128, 4], F32)
            self.lvs = self.sb("lvst", [128, 128], F32)
            self.Bbc = Buf("bconst")
            zer = self.slabf[:, 0:128]
            Bz = Buf("zer")
            self.MSET("pool", zer, 0.0, [Bz])
            for d_, (pat, cm) in enumerate((([[1, 128]], -1), ([[-1, 128]], 1))):
                for kk, cmp in enumerate((ALU.is_ge, ALU.is_gt)):
                    self.P.op("pool", (lambda o, pat, cm, cmp: lambda e: e.affine_select(
                        out=o, in_=zer, pattern=pat, compare_op=cmp, fill=NEG, base=0, channel_multiplier=cm))(
                        self.negcat[:, d_, kk * 128:(kk + 1) * 128], pat, cm, cmp), [Bz], [self.Bbc])
        self.DMA("sp", self.pmask[:], self.pmask_d[:, :], [], [self.Bbc])
        self.DMA("sp", self.lvs[:], self.lvs_d[:, :], [], [self.Bbc])
        self.DMA("sp", self.cw[:], self.b_convT[:, j * 160:(j + 1) * 160], [], [self.Bbc])
        self.DMA("sp", self.ngt[:], self.b_norm_g[j:j + 1, :].partition_broadcast(128), [], [self.Bbc])
        self.DMA("sp", self.gsc[:, 0:1], self.b_dtbP[j], [], [self.Bbc])
        self.DMA("sp", self.gsc[:, 2:3], self.b_alogP[j], [], [self.Bbc])
        self.ACT(self.gsc[:, 3:4], self.gsc[:, 2:3], AF.Exp, [self.Bbc], [self.Bbc])
        self.TS("dve", self.gsc[:, 1:2], self.gsc[:, 3:4], -1.0, None, ALU.mult, None, [self.Bbc], [self.Bbc])

    def layer_b(self, i):
        j = i // 2
        last = (i == self.depth - 1)
        sf, sbb = self.slabf, self.slabb
        wor = self.sb_wo[:].rearrange("p a b -> p (a b)")
        self.layer_b_consts(j)
        ident, identb = self.ident, self.identb
        FEAT = sf[:, 0:4096]
        Bfeat = Buf("feat")
        WABF = sf[:, 9728:10752].rearrange("p (k n) -> p k n", n=128)
        WAB = sbb[:, 47616:48640].rearrange("p (k n) -> p k n", n=128)
        Bwab = Buf("wab")
        self.MSET("dve", WABF, 0.0, [Bwab])
        for q4, c0 in enumerate((0, 32, 64, 96)):
            self.DMA("sp", WABF[:, :, c0:c0 + 8],
                     self.b_w_in[j, :, 6144 + q4 * 8:6144 + (q4 + 1) * 8].rearrange("(k p) n -> p k n", p=128), [], [Bwab])
        self.CP("dve", WAB, WABF, [Bwab], [Bwab])
        import os
        kb = int(os.environ.get("KB", "99"))
        self.kb = kb
        for g, G in enumerate(self.groups):
            self.P.barrier()
            if kb >= 1:
                self.b_phase0(i, j, g, G, FEAT, Bfeat, WAB, Bwab)
            self.P.barrier()
            if kb >= 2:
                self.b_heads(i, j, g, G, FEAT, Bfeat, wor)
            self.P.barrier()
            if kb >= 5:
                self.b_outproj(i, j, g, G, last)

    def b_phase0(self, i, j, g, G, FEAT, Bfeat, WAB, Bwab):
        sf, sbb = self.slabf, self.slabb
        c = G["cond"]
        XT = sf[:, 4096:8192].rearrange("p (s d) -> p s d", d=1024)
        T1 = sf[:, 8192:8704]
        T2 = sf[:, 8704:9216]
        T3 = sf[:, 9216:9728]
        MASK = sf[:, 10752:11264]
        HTb = sbb[:, 0:8192].rearrange("p (b k n) -> p b k n", k=8, n=512)
        Bxt, Bt = Buf("bxt"), Buf("bt")
        Bh = [Buf(), Buf()]
        Bm = Buf("mask")
        self.MSET("dve", MASK, 1.0, [Bm])
        self.MSET("dve", MASK.rearrange("p (a b) -> p a b", b=128)[:, :, 0:1], 0.0, [Bm])
        import os
        kp = int(os.environ.get("KP", "99"))
        for b4 in range(G["ntok"] // 512 if kp >= 1 else 0):
            hb = b4 % 2
            cols = slice(b4 * 512, (b4 + 1) * 512)
            self.load_x_tile(i, g, b4 * 4, XT, Bxt, 4)
            self.make_ht(i, c, XT, Bxt, 4, HTb[:, hb], Bh[hb])
            self.DMA("sp", self.HTd[g][:, :, cols].rearrange("k p n -> p k n"), HTb[:, hb], [Bh[hb]],
                     [self.BHT[g][b4 * 4 + s_] for s_ in range(4)])
            if kp < 2:
                continue
            pt, bp = self.ps()
            for kc in range(8):
                self.MM(pt[:, :], WAB[:, kc, :], HTb[:, hb, kc, :], kc == 0, kc == 7, [Bwab, Bh[hb]], [bp])
            if kp < 3:
                continue
            self.ACT(T1[0:64, :], pt[0:64, :], AF.Exp, [bp, self.Bbc], [Bt], bias=self.gsc[0:64, 0:1])
            self.ACT(T1[0:64, :], T1[0:64, :], AF.Ln, [Bt, self.Bc], [Bt], bias=self.epsc[0:64, 1:2])
            self.TS("dve", T1[0:64, :], T1[0:64, :], self.gsc[0:64, 1:2], None, ALU.mult, None, [Bt, self.Bbc], [Bt])
            self.ACT(T1[64:128, :], pt[64:128, :], AF.Exp, [bp], [Bt], scale=-1.0)
            self.ACT(T1[64:128, :], T1[64:128, :], AF.Ln, [Bt, self.Bc], [Bt], bias=self.epsc[64:128, 1:2])
            self.TS("dve", T1[64:128, :], T1[64:128, :], -1.0, None, ALU.mult, None, [Bt], [Bt])
            if kp < 4:
                continue
            self.P.op("dve", (lambda o, m, d: lambda e: e.tensor_tensor_scan(out=o, data0=m, data1=d, initial=0.0, op0=ALU.mult, op1=ALU.add))(
                T2, MASK, T1), [Bt, Bm], [Bt])
            self.TT("dve", T3, T1, T2, ALU.subtract, [Bt], [Bt])
            self.TT("dve", T3.rearrange("p (a b) -> p a b", b=128), T3.rearrange("p (a b) -> p a b", b=128),
                    T2.rearrange("p (a b) -> p a b", b=128)[:, :, 127:128].broadcast_to([128, 4, 128]), ALU.add, [Bt], [Bt])
            pm = self.pmask
            self.TS("dve", FEAT[:, cols], T2, pm[:, 0:1], None, ALU.mult, None, [Bt, self.Bbc], [Bfeat])
            self.STT(FEAT[:, cols], T3, pm[:, 1:2], FEAT[:, cols], ALU.mult, ALU.add, [Bt, self.Bbc, Bfeat], [Bfeat])
            self.STT(FEAT[:, cols], T1, pm[:, 2:3], FEAT[:, cols], ALU.mult, ALU.add, [Bt, self.Bbc, Bfeat], [Bfeat])

    def b_heads(self, i, j, g, G, FEAT, Bfeat, wor):
        sf, sbb = self.slabf, self.slabb
        ident, identb = self.ident, self.identb
        n_seq, L, ntok = G["n_seq"], G["L"], G["ntok"]
        nt = ntok // 128
        tps = L // 128
        ACC = sf[:, 4096:4608]
        A_ = sf[:, 4608:5120]
        RN = sf[:, 5120:5632]
        U4 = sf[:, 5632:6656].rearrange("p (b n) -> p b n", n=256)
        EX4 = sf[:, 6656:7680].rearrange("p (b n) -> p b n", n=256)
        OS2 = sf[:, 7680:8192].rearrange("p (b n) -> p b n", n=256)
        S8 = sf[:, 8192:10240].rearrange("p (b n) -> p b n", n=256)
        Y12 = sf[:, 10240:10752].rearrange("p (b n) -> p b n", n=256)
        SCAL4 = sf[:, 10752:10784].rearrange("p (b n) -> p b n", n=8)
        SEL4 = sf[:, 10784:11808].rearrange("p (b k n) -> p b k n", k=2, n=128)
        EG4 = sf[:, 11808:12288].rearrange("p (b n) -> p b n", n=120) if False else None
        fst = self.stat
        HTb = sbb[:, 0:4096].rearrange("p (b k n) -> p b k n", k=8, n=256)
        WH = sbb[:, 4096:10240].rearrange("p (k n) -> p k n", n=768)
        PRE = sbb[:, 10240:18432].rearrange("p (b n) -> p b n", n=4096)
        SQ = sbb[:, 18432:18944]
        QT = sbb[:, 18944:23040]
        KT = sbb[:, 23040:27136]
        V = sbb[:, 27136:35328].rearrange("p (t n) -> p t n", n=256)
        KTOK = sbb[:, 35328:39424].rearrange("p (t n) -> p t n", n=128)
        ZS = sbb[:, 39424:47616].rearrange("p (t n) -> p t n", n=256)
        VT = sbb[:, 48640:49152]
        YB2 = sbb[:, 49152:49664].rearrange("p (b n) -> p b n", n=256)
        YT2 = sbb[:, 49664:50176].rearrange("p (b n) -> p b n", n=256)
        O = wor[:, 0:nt * 256].rearrange("p (t n) -> p t n", n=256)
        jb = nt * 256
        NSET = 4 if n_seq > 1 else 2
        SETW = 2304
        sets = []
        for q in range(NSET):
            b0 = jb + q * SETW
            sets.append(dict(
                QKNT=wor[:, b0:b0 + 256], NTt=wor[:, b0 + 256:b0 + 384],
                DC=wor[:, b0 + 384:b0 + 896].rearrange("p (b n) -> p b n", n=256),
                MC=wor[:, b0 + 896:b0 + 1152],
                LC=wor[:, b0 + 1152:b0 + 1664].rearrange("p (b n) -> p b n", n=256),
                KBG=wor[:, b0 + 1664:b0 + 1792], WT=wor[:, b0 + 1792:b0 + 1920],
                QDT=wor[:, b0 + 1920:b0 + 2048], KD=wor[:, b0 + 2048:b0 + 2176], EG=wor[:, b0 + 2176:b0 + 2304],
                qi=q, B=Buf(f"set{q}"), U=U4[:, q, :], EX=EX4[:, q, :], SCAL=SCAL4[:, q, :], SEL=SEL4[:, q], BU=Buf(), BQ=Buf(), BL=Buf()))
        b1 = jb + NSET * SETW
        VN2 = wor[:, b1:b1 + 512].rearrange("p (b n) -> p b n", n=256)
        VB2 = wor[:, b1 + 512:b1 + 1024].rearrange("p (b n) -> p b n", n=256)
        nch = n_seq * 2
        Sb8 = wor[:, b1 + 1024:b1 + 1024 + nch * 256].rearrange("p (b n) -> p b n", n=256)
        assert b1 + 1024 + nch * 256 <= 16384, (b1, nch)
        BVN = [Buf(), Buf()]
        BVB = [Buf(), Buf()]
        BS = [Buf() for _ in range(nch)]
        BO = [Buf() for _ in range(nt)]
        BOS = [Buf(), Buf()]
        BY = [Buf(), Buf()]
        Bh = [Buf(), Buf()]
        Bwh, Bacc, Bsq = Buf("wh"), Buf("acc"), Buf("sq")
        BPRE = [Buf(), Buf()]
        Bq, Bk, Bv, Bkt, Bz, Bvt = Buf("QT"), Buf("KT"), Buf("V"), Buf("KTOK"), Buf("ZS"), Buf("VT")
        Bpsh = [Buf() for _ in range(8)]
        pshc = [0]

        def psh_next():
            pt_, bp_ = self.ps()
            return pt_[:, 0:64].bitcast(BF16), bp_

        wsrc = self.wb_in[j]
        Bwsrc = self.Bw[("b_in", j)]
        cnt = [0]
        for hd in range(NH):
            for (c0, n, o) in ((hd * 128, 128, 0), (1024 + hd * 128, 128, 128), (2048 + hd * 256, 256, 256), (4096 + hd * 256, 256, 512)):
                self.DMA("sp", WH[:, :, o:o + n], wsrc[:, c0:c0 + n].rearrange("(k p) n -> p k n", p=128), [Bwsrc], [Bwh])
            for pas in range(2):
                for b2 in range(ntok // 256):
                    hb = b2 % 2
                    cols = slice(b2 * 256, (b2 + 1) * 256)
                    self.DMA("sp", HTb[:, hb], self.HTd[g][:, :, cols].rearrange("k p n -> p k n"),
                             [self.BHT[g][b2 * 2], self.BHT[g][b2 * 2 + 1]], [Bh[hb]])
                    for cc in range(2):
                        wo = (pas * 2 + cc) * 128
                        pt, bp = self.ps()
                        for kc in range(8):
                            self.MM(pt[:, 0:256], WH[:, kc, wo:wo + 128], HTb[:, hb, kc, :], kc == 0, kc == 7, [Bwh, Bh[hb]], [bp])
                        self.CP("act", PRE[:, cc, cols], pt[:, 0:256], [bp], [BPRE[cc]])
                    if pas == 0:
                        for s2 in range(2):
                            t = b2 * 2 + s2
                            pt, bp = self.ps()
                            for kc in range(8):
                                self.MM(pt[:, 0:256], HTb[:, hb, kc, s2 * 128:(s2 + 1) * 128], WH[:, kc, 512:768], kc == 0, kc == 7,
                                        [Bwh, Bh[hb]], [bp])
                            self.ACT(ZS[:, t, :], pt[:, 0:256], AF.Silu, [bp], [Bz])
                import os
                kh = int(os.environ.get("KH", "99"))
                for cc in range(2 if kh >= 2 else 0):
                    kind = ("q", "k")[cc] if pas == 0 else "v"
                    cwi = (hd if kind == "q" else 8 + hd) if pas == 0 else 16 + hd * 2 + cc
                    wv = self.cw[:, cwi * 5:(cwi + 1) * 5]
                    for s_ in range(n_seq):
                        for a in range(0, L, 512):
                            bnd = min(a + 512, L)
                            n = bnd - a
                            base = s_ * L
                            self.TS("pool", ACC[:, 0:n], PRE[:, cc, base + a:base + bnd], wv[:, 2:3], None, ALU.mult, None,
                                    [BPRE[cc], self.Bbc], [Bacc])
                            for off in (-2, -1, 1, 2):
                                lo = max(a, -off) if off < 0 else a
                                hi = min(bnd, L - off) if off > 0 else bnd
                                if hi <= lo:
                                    continue
                                self.STT(ACC[:, lo - a:hi - a], PRE[:, cc, base + lo + off:base + hi + off], wv[:, off + 2:off + 3],
                                         ACC[:, lo - a:hi - a], ALU.mult, ALU.add, [BPRE[cc], self.Bbc, Bacc], [Bacc])
                            gcols = slice(base + a, base + bnd)
                            if kh < 3:
                                continue
                            if kind == "v":
                                self.ACT(VT[:, 0:n], ACC[:, 0:n], AF.Silu, [Bacc], [Bvt])
                                for q in range(n // 128):
                                    t = (base + a) // 128 + q
                                    ph, bph = psh_next()
                                    self.TR(ph, VT[:, q * 128:(q + 1) * 128], identb[:], [Bvt, self.Bc], [bph])
                                    self.CP("dve" if q % 2 else "pool" if False else "dve", V[:, t, cc * 128:(cc + 1) * 128], ph, [bph], [Bv])
                            elif kh >= 4:
                                self.ACT(A_[:, 0:n], ACC[:, 0:n], AF.Silu, [Bacc], [Bacc])
                                self.TT("pool", SQ[:, 0:n], A_[:, 0:n], A_[:, 0:n], ALU.mult, [Bacc], [Bsq])
                                pt, bp = self.ps()
                                self.MM(pt[:, 0:n], self.onesb[:], SQ[:, 0:n], True, True, [self.Bc, Bsq], [bp])
                                self.ACT(RN[:, 0:n], pt[:, 0:n], AF.Sqrt, [bp, self.Bc], [Bsq], bias=self.epsc[:, 0:1])
                                self.P.op("dve", (lambda o: lambda e: e.reciprocal(out=o, in_=o))(RN[:, 0:n]), [Bsq], [Bsq])
                                dst, Bd = (QT, Bq) if kind == "q" else (KT, Bk)
                                self.STT(dst[:, gcols], A_[:, 0:n], (DK ** -0.5) if kind == "q" else 1.0, RN[:, 0:n], ALU.mult, ALU.mult,
                                         [Bacc, Bsq], [Bd])
                                if kind == "k" and kh >= 5:
                                    for q in range(n // 128):
                                        t = (base + a) // 128 + q
                                        ph, bph = psh_next()
                                        self.TR(ph, KT[:, t * 128:(t + 1) * 128], identb[:], [Bk, self.Bc], [bph])
                                        self.CP("dve", KTOK[:, t, :], ph, [bph], [Bkt])
            if g == 0 and hd == 0:
                self.dump(FEAT[:, 0:512], 0, 512, [Bfeat])
                self.dump(QT[:, 0:512], 512, 512, [Bq])
                self.dump(KT[:, 0:512], 1024, 512, [Bk])
                self.dump(V[:, 0, :], 1536, 256, [Bv])
                self.dump(V[:, 1, :], 1792, 256, [Bv])
                self.dump(KTOK[:, 0, :], 2048, 128, [Bkt])
                self.dump(ZS[:, 0, :], 2176, 256, [Bz])
            if self.kb < 3:
                continue
            for s_ in range(n_seq):
                for d_ in range(2):
                    ch = s_ * 2 + d_
                    if g == 1:
                        self.DMA("sp", S8[:, ch, :], self.s0[j, d_, hd], [], [BS[ch]])
                    else:
                        self.MSET("dve", S8[:, ch, :], 0.0, [BS[ch]])
                    self.CP("act", Sb8[:, ch, :], S8[:, ch, :], [BS[ch]], [BS[ch]])
            visited = set()

            def visit(t, d_, ch, st_):
                r = d_ * 32 + hd
                tc = slice(t * 128, (t + 1) * 128)
                B_ = st_["B"]
                SC = st_["SCAL"]
                SEL = st_["SEL"]
                qi = st_["qi"]
                bkA = (self.psb[2 * qi], self.Bps[2 * qi])
                bkB = (self.psb[2 * qi + 1], self.Bps[2 * qi + 1])
                self.CP("dve", SEL[:, 0, :], ident[:, r:r + 1].broadcast_to([128, 128]), [self.Bc], [B_])
                self.TT("dve", SEL[:, 1, :], SEL[:, 0, :], ident[:, 64 + r:64 + r + 1].broadcast_to([128, 128]), ALU.add, [self.Bc, B_], [B_])
                pg, bpg = bkA
                self.MM(pg[:, 0:128], SEL[:, 0, :], FEAT[:, tc], True, True, [B_, Bfeat], [bpg])
                self.MM(pg[:, 128:256], SEL[:, 1, :], FEAT[:, tc], True, True, [B_, Bfeat], [bpg])
                colT = t * 128 + (127 if d_ == 0 else 0)
                self.MM(pg[:, 256:257], SEL[:, 0, :], FEAT[:, colT:colT + 1], True, True, [B_, Bfeat], [bpg])
                pf, bpf = bkB
                self.TR(pf[:, 0:128], FEAT[:, tc], ident[:], [Bfeat, self.Bc], [bpf])
                yield
                self.CP("dve", SC[:, 0:1], pf[:, r:r + 1], [bpf], [B_])
                self.CP("dve", SC[:, 1:2], pf[:, 64 + r:64 + r + 1], [bpf], [B_])
                self.CP("dve", SC[:, 2:3], pg[:, 256:257], [bpg], [B_])
                yield
                self.ACT(SC[:, 3:4], SC[:, 1:2], AF.Exp, [B_], [B_])
                self.ACT(SC[:, 4:5], SC[:, 0:1], AF.Exp, [B_], [B_], bias=SC[:, 1:2])
                self.ACT(SC[:, 5:6], SC[:, 0:1], AF.Exp, [B_], [B_], bias=SC[:, 2:3], scale=-1.0)
                self.ACT(SC[:, 6:7], SC[:, 2:3], AF.Exp, [B_], [B_])
                self.STT(st_["EX"], pg[:, 0:256], SC[:, 0:1], self.negcat[:, d_, :], ALU.subtract, ALU.add, [bpg, B_, self.Bbc], [B_])
                self.ACT(st_["EG"], pg[:, 0:128], AF.Exp, [bpg], [B_])
                pk, bpk = bkB
                self.MM(pk[:, 0:128], KT[:, tc], QT[:, tc], True, True, [Bk, Bq], [bpk])
                self.MM(pk[:, 128:256], KT[:, tc], KT[:, tc], True, True, [Bk], [bpk])
                yield
                self.ACT(st_["EX"], st_["EX"], AF.Exp, [B_], [B_])
                yield
                self.TT("dve", st_["QKNT"], pk[:, 0:256], st_["EX"], ALU.mult, [bpk, B_], [B_])
                self.TT("pool", st_["QDT"], QT[:, tc], st_["EG"], ALU.mult, [Bq, B_], [st_["BQ"]])
                self.TS("pool", st_["KD"], KTOK[:, t, :], SC[:, 5:6], None, ALU.mult, None, [Bkt, B_], [st_["BQ"]])
                self.TS("pool", st_["KBG"], KTOK[:, t, :], SC[:, 4:5], None, ALU.mult, None, [Bkt, B_], [st_["BQ"]])
                yield
                N = st_["QKNT"][:, 128:256]
                ph, bph = bkA[0][:, 0:64].bitcast(BF16), bkA[1]
                self.TR(ph, N, identb[:], [B_, self.Bc], [bph])
                yield
                self.CP("act", st_["NTt"], ph, [bph], [B_])
                yield
                Am = st_["NTt"]
                LVS = self.lvs[:]
                BL = st_["BL"]

                def mk_l(k, lb):
                    self.STT(st_["LC"][:, lb, 0:128], LVS, float(k), N, ALU.is_equal, ALU.mult, [self.Bbc, B_], [BL])
                    self.STT(st_["LC"][:, lb, 128:256], LVS, float(k), Am, ALU.is_equal, ALU.mult, [self.Bbc, B_], [BL])
                mk_l(0, 0)
                self.TT("dve", st_["DC"][:, 0, 0:128], identb[:], st_["LC"][:, 0, 0:128], ALU.subtract, [self.Bc, BL], [B_])
                self.TT("dve", st_["DC"][:, 0, 128:256], identb[:], st_["LC"][:, 0, 128:256], ALU.subtract, [self.Bc, BL], [B_])
                dcur = 0
                for k in range(1, 7):
                    lb = k % 2
                    mk_l(k, lb)
                    yield
                    Dt_, D_ = st_["DC"][:, dcur, 0:128], st_["DC"][:, dcur, 128:256]
                    px, bpx = bkB
                    self.MM(px[:, 0:128], st_["LC"][:, lb, 128:256], Dt_, True, True, [BL, B_], [bpx])
                    self.MM(px[:, 128:256], st_["LC"][:, lb, 0:128], D_, True, True, [BL, B_], [bpx])
                    yield
                    self.CP("act", st_["MC"], px[:, 0:256], [bpx], [B_])
                    yield
                    pp, bpp = bkA
                    self.MM(pp[:, 0:128], D_, st_["MC"][:, 0:128], True, True, [B_], [bpp])
                    self.MM(pp[:, 128:256], Dt_, st_["MC"][:, 128:256], True, True, [B_], [bpp])
                    yield
                    self.TT("dve", st_["DC"][:, 1 - dcur, :], st_["DC"][:, dcur, :], pp[:, 0:256], ALU.subtract, [B_, bpp], [B_])
                    dcur = 1 - dcur
                    yield
                TTm = st_["DC"][:, dcur, 0:128]
                vb = cnt[0] % 2
                cnt[0] += 1
                self.TS("pool", VB2[:, vb, :], V[:, t, :], SC[:, 3:4], None, ALU.mult, None, [Bv, B_], [BVB[vb]])
                pu, bpu = bkB
                self.MM(pu[:, 0:256], TTm, VB2[:, vb, :], True, True, [B_, BVB[vb]], [bpu])
                self.MM(pu[:, 256:384], st_["KBG"], TTm, True, True, [B_, st_["BQ"]], [bpu])
                yield
                self.CP("act", st_["U"], pu[:, 0:256], [bpu], [st_["BU"]])
                self.CP("dve", st_["WT"], pu[:, 256:384], [bpu], [st_["BU"]])
                yield
                pa, bpa = bkA
                self.MM(pa[:, 0:256], st_["WT"], Sb8[:, ch, :], True, True, [st_["BU"], BS[ch]], [bpa])
                po, bpo = bkB
                self.MM(po[:, 0:256], st_["QDT"], Sb8[:, ch, :], True, False, [st_["BQ"], BS[ch]], [bpo])
                yield
                vn = cnt[0] % 2
                cnt[0] += 1
                self.TT("dve", VN2[:, vn, :], st_["U"], pa[:, 0:256], ALU.subtract, [st_["BU"], bpa], [BVN[vn]])
                self.MM(po[:, 0:256], st_["QKNT"][:, 0:128], VN2[:, vn, :], False, True, [B_, BVN[vn]], [bpo])
                pss, bps_ = bkA
                self.MM(pss[:, 0:256], st_["KD"], VN2[:, vn, :], True, True, [st_["BQ"], BVN[vn]], [bps_])
                yield
                self.STT(S8[:, ch, :], S8[:, ch, :], SC[:, 6:7], pss[:, 0:256], ALU.mult, ALU.add, [BS[ch], B_, bps_], [BS[ch]])
                if t not in visited:
                    visited.add(t)
                    self.CP("act", O[:, t, :], po[:, 0:256], [bpo], [BO[t]])
                    yield
                    self.CP("act", Sb8[:, ch, :], S8[:, ch, :], [BS[ch]], [BS[ch]])
                else:
                    self.CP("act", Sb8[:, ch, :], S8[:, ch, :], [BS[ch]], [BS[ch]])
                    yield
                    ob = cnt[0] % 2
                    cnt[0] += 1
                    OSb = OS2[:, ob, :]
                    self.TT("dve", OSb, po[:, 0:256], O[:, t, :], ALU.add, [bpo, BO[t]], [BOS[ob]])
                    if g == 0 and hd == 0 and t < 2:
                        self.dump(OSb, 2432 + t * 256, 256, [BOS[ob]])
                    st6 = fst[:, 0:6]
                    mv = fst[:, 6:8]
                    ms = fst[:, 8:9]
                    self.P.op("dve", (lambda a: lambda e: e.bn_stats(out=st6, in_=a))(OSb), [BOS[ob]], [self.Bstat])
                    self.P.op("dve", lambda e: e.bn_aggr(out=mv, in_=st6), [self.Bstat], [self.Bstat])
                    self.STT(ms, mv[:, 0:1], mv[:, 0:1], mv[:, 1:2], ALU.mult, ALU.add, [self.Bstat], [self.Bstat])
                    self.ACT(ms, ms, AF.Sqrt, [self.Bstat, self.Bc], [self.Bstat], bias=self.epsc[:, 0:1])
                    self.P.op("dve", lambda e: e.reciprocal(out=ms, in_=ms), [self.Bstat], [self.Bstat])
                    self.STT(Y12[:, ob, :], OSb, ms, self.ngt[:], ALU.mult, ALU.mult, [BOS[ob], self.Bstat, self.Bbc], [BY[ob]])
                    self.TT("pool", YB2[:, ob, :], Y12[:, ob, :], ZS[:, t, :], ALU.mult, [BY[ob], Bz], [BY[ob]])
                    if g == 0 and hd == 0 and t < 2:
                        self.dump(YB2[:, ob, :], 2944 + t * 256, 256, [BY[ob]])
                    for e2 in range(2):
                        ph, bph = bkA[0][:, e2 * 64:(e2 + 1) * 64].bitcast(BF16), bkA[1]
                        self.TR(ph, YB2[:, ob, e2 * 128:(e2 + 1) * 128], identb[:], [BY[ob], self.Bc], [bph])
                        self.CP("dve", YT2[:, ob, e2 * 128:(e2 + 1) * 128], ph, [bph], [BY[ob]])
                    self.DMA("sp", self.YTd[g][hd * 2:hd * 2 + 2, :, tc].rearrange("e p n -> p e n"),
                             YT2[:, ob, :].rearrange("p (e n) -> p e n", n=128), [BY[ob]], [self.BYT[g][hd][t]])

            setc = 0
            for step in range(tps if self.kb >= 4 else 0):
                gens = []
                for s_ in range(n_seq):
                    for d_ in range(2):
                        t = s_ * tps + (step if d_ == 0 else tps - 1 - step)
                        gens.append((t, d_, s_ * 2 + d_))
                for b0 in range(0, len(gens), NSET):
                    batch = []
                    for q, (t, d_, ch) in enumerate(gens[b0:b0 + NSET]):
                        batch.append(visit(t, d_, ch, sets[(setc + q) % NSET]))
                    setc += len(batch)
                    alive = list(batch)
                    while alive:
                        nxt = []
                        for gen in alive:
                            try:
                                next(gen)
                                nxt.append(gen)
                            except StopIteration:
                                pass
                        alive = nxt
            if g == 0:
                for s_ in range(n_seq):
                    for d_ in range(2):
                        ch = s_ * 2 + d_
                        self.DMA("sp", self.ns[s_, j, d_, hd], S8[:, ch, :], [BS[ch]], [self.BNS])

    def b_outproj(self, i, j, g, G, last):
        sf, sbb = self.slabf, self.slabb
        c = G["cond"]
        WO = self.sb_wo[:]
        Bwo = Buf("wo")
        self.DMA("sp", WO, self.wb_out[j].rearrange("(k p) n -> p k n", p=128), [self.Bw[("b_out", j)]], [Bwo])
        XT2 = sf[:, 4096:6144].rearrange("p (b n) -> p b n", n=1024)
        R12 = sf[:, 6144:8192].rearrange("p (b n) -> p b n", n=1024)
        YT16 = sbb[:, 0:4096].rearrange("p (b e n) -> p b e n", e=16, n=128)
        Bx = [Buf(), Buf()]
        Br = [Buf(), Buf()]
        Byt = [Buf(), Buf()]
        for t in range(G["ntok"] // 128):
            b = t % 2
            tc = slice(t * 128, (t + 1) * 128)
            self.DMA("sp", XT2[:, b, :], self.X[g][tc, :], [self.BX[g][t]], [Bx[b]])
            self.DMA("sp", YT16[:, b], self.YTd[g][:, :, tc].rearrange("e p n -> p e n"), [self.BYT[g][h][t] for h in range(NH)], [Byt[b]])
            pts = []
            for half in range(2):
                pt, bp = self.ps()
                for ec in range(16):
                    self.MM(pt[:, :], YT16[:, b, ec, :], WO[:, ec, half * 512:(half + 1) * 512], ec == 0, ec == 15, [Byt[b], Bwo], [bp])
                pts.append((pt, bp))
            self.epilogue(i, g, c, t, pts, XT2[:, b, :], Bx[b], R12[:, b, :], R12[:, b, :], Br[b], last)


def build_program(NP, LP, LS, depth=4, dbg=False):
    import os
    dbg = dbg or bool(os.environ.get('KDBG'))
    b = Builder(NP, LP, LS, depth, dbg)
    b.wst = b.sb("wst", [128, 1024], BF16)
    b.BVG4 = [Buf() for _ in range(4)]
    b.BVH4 = [Buf() for _ in range(4)]
    return b.build()


def _pos_table(n_tokens, grid_w=64):
    def sincos(pos, dim):
        omega = (1.0 / (np.float32(10000.0) ** (np.arange(dim // 2, dtype=np.float32) / np.float32(dim // 2)))).astype(np.float32)
        ang = pos.astype(np.float32)[:, None] * omega[None, :]
        return np.concatenate([np.sin(ang), np.cos(ang)], axis=-1).astype(np.float32)
    rows = n_tokens // grid_w
    er = sincos(np.arange(rows), D // 2)
    ec = sincos(np.arange(grid_w), D // 2)
    emb = np.concatenate([np.broadcast_to(er[:, None, :], (rows, grid_w, D // 2)),
                          np.broadcast_to(ec[None, :, :], (rows, grid_w, D // 2))], axis=-1)
    return np.ascontiguousarray(emb.reshape(rows * grid_w, D), dtype=np.float32)


def prep_shared(inp, depth):
    nA, nB = (depth + 1) // 2, depth // 2
    f = lambda a: np.ascontiguousarray(a, dtype=np.float32)
    out = {}
    out["w_ada"] = f(inp["w_ada"][:depth])
    out["b_adaT"] = f(inp["b_ada"][:depth].reshape(depth, 24, 128).transpose(2, 0, 1).reshape(128, depth * 24))
    out["ln_g"] = f(inp["ln_g"][:depth])
    out["ln_b"] = f(inp["ln_b"][:depth])
    out["a_w_in"] = f(inp["a_w_in"][:nA])
    out["a_ln_gT"] = f(inp["a_ln_g"][:nA].reshape(nA, 16, 128).transpose(2, 0, 1).reshape(128, nA * 16))
    out["a_ln_bT"] = f(inp["a_ln_b"][:nA].reshape(nA, 16, 128).transpose(2, 0, 1).reshape(128, nA * 16))
    out["a_w_sT"] = f(inp["a_w_s"][:nA].transpose(0, 3, 1, 2).reshape(nA, 128, 8 * 128))
    out["a_b_s"] = f(inp["a_b_s"][:nA].reshape(nA, 8 * 128))
    out["a_w_out"] = f(inp["a_w_out"][:nA])
    nb = max(nB, 1)
    out["b_w_in"] = f(inp["b_w_in"][:nb])
    out["b_convT"] = f(inp["b_conv_w"][:nb].reshape(nb, 5, 32, 128).transpose(3, 0, 2, 1).reshape(128, nb * 32 * 5))
    al = np.zeros((nb, 128, 1), np.float32)
    db = np.zeros((nb, 128, 1), np.float32)
    for j in range(nb):
        for d_ in range(2):
            al[j, d_ * 32:d_ * 32 + 8, 0] = inp["b_A_log"][j, d_]
            db[j, d_ * 32:d_ * 32 + 8, 0] = inp["b_dt_bias"][j, d_]
    out["b_alogP"] = al
    out["b_dtbP"] = db
    out["b_norm_g"] = f(inp["b_norm_g"][:nb])
    pm = np.zeros((128, 4), np.float32)
    pm[0:32, 0] = 1.0
    pm[32:64, 1] = 1.0
    pm[64:128, 2] = 1.0
    out["pmask"] = pm
    ii = np.arange(128)
    xr = ii[:, None] ^ ii[None, :]
    lv = np.full((128, 128), -1.0, np.float32)
    nz = xr > 0
    lv[nz] = np.floor(np.log2(xr[nz])).astype(np.float32)
    out["lvs"] = lv
    out["b_w_out"] = f(inp["b_w_out"][:nb])
    return out


def prep_core(inp, shared, core, NP, LP, LS, depth, n_per_group):
    nb = max(depth // 2, 1)
    b = core // n_per_group
    f = lambda a: np.ascontiguousarray(a, dtype=np.float32)
    m = dict(shared)
    m["xp"] = f(inp["x_prompt"][core * NP:(core + 1) * NP].reshape(NP * LP, D))
    m["xs"] = f(inp["x_sample"][b])
    m["pos"] = _pos_table(LS)
    cond2 = np.stack([inp["c_ctx"], inp["c"][b]], axis=0)
    m["condT"] = f(cond2.reshape(2, 8, 128).transpose(2, 1, 0).reshape(128, 16))
    m["s0"] = f(inp["state_delta"][b][:nb])
    return m


_CACHE = {}


def kernel(**inputs):
    inp = {k: np.asarray(v) for k, v in inputs.items()}
    NP, LP, LS, depth = 4, 256, 4096, 4
    key = (NP, LP, LS, depth)
    if key not in _CACHE:
        _CACHE[key] = build_program(NP, LP, LS, depth)
    nc = _CACHE[key]
    shared = prep_shared(inp, depth)
    in_maps = [prep_core(inp, shared, c, NP, LP, LS, depth, 4) for c in range(8)]
    res = run_bass_kernel_spmd(nc, in_maps, core_ids=list(range(8)))
    r = res.results
    y_prompt = np.concatenate([r[c]["yp"].reshape(NP, LP, D) for c in range(8)], axis=0)
    y_sample = np.stack([r[0]["ys"], r[4]["ys"]], axis=0)
    ns = np.concatenate([r[c]["ns"] for c in range(8)], axis=0)
    return (y_prompt.astype(np.float32), y_sample.astype(np.float32), ns.astype(np.float32))
```

```python
import numpy as np
from contextlib import ExitStack
import concourse.bass as bass
import concourse.mybir as mybir
from concourse.bass_utils import run_bass_kernel_spmd

F32 = mybir.dt.float32
BF16 = mybir.dt.bfloat16
AF = mybir.ActivationFunctionType
ALU = mybir.AluOpType

D = 1024
E = 2048
KW = 1024
DK = 128
DV = 256
NH = 8
BW = 6176
ALPHA = (2.0 * 4) ** 0.25
EPS = 1e-6
NEG = -1.0e30


class Buf:
    __slots__ = ("name", "w", "r", "x")

    def __init__(self, name="", x=False):
        self.name = name
        self.w = None
        self.r = {}
        self.x = x


class Prog:
    ENGS = ("pe", "dve", "act", "pool", "sp")
    NSLOT = 8
    SAME_ENG_SKIP = 12

    def __init__(self):
        self.ops = {e: [] for e in self.ENGS}
        self.cnt = {e: 0 for e in self.ENGS}
        self.waited = {e: {} for e in self.ENGS}
        self.dcnt = {e: 0 for e in self.ENGS}

    def _deps(self, eng, reads, writes):
        deps = {}

        def add(p):
            if p is None:
                return
            k = p[0]
            if k not in deps or deps[k][1] < p[1]:
                deps[k] = p
        for b in reads:
            add(b.w)
        for b in writes:
            add(b.w)
            for p in b.r.values():
                add(p)
        waits = []
        for k, p in deps.items():
            val = p[1]
            if k == eng and self.cnt[eng] - p[3] > self.SAME_ENG_SKIP:
                continue
            if self.waited[eng].get(k, 0) >= val:
                continue
            self.waited[eng][k] = val
            waits.append((k, val))
        return waits

    def _record(self, reads, writes, prod):
        k = prod[0]
        for b in reads:
            if k not in b.r or b.r[k][1] < prod[1]:
                b.r[k] = prod
        for b in writes:
            b.w = prod
            b.r = {}

    def op(self, eng, fn, reads=(), writes=()):
        xs = [b for b in reads if b.x]
        if xs:
            writes = list(writes) + xs
        waits = self._deps(eng, reads, writes)
        idx = self.cnt[eng]
        self.cnt[eng] = idx + 1
        self.ops[eng].append((waits, fn, (eng, 1)))
        self._record(reads, writes, (eng, idx + 1, eng, idx))

    def dma(self, q, fn, reads=(), writes=()):
        waits = self._deps(q, reads, writes)
        i = self.dcnt[q]
        self.dcnt[q] = i + 1
        k = ("dma", q, i % self.NSLOT)
        prev = 16 * (i // self.NSLOT)
        if prev > 0 and self.waited[q].get(k, 0) < prev:
            self.waited[q][k] = prev
            waits.append((k, prev))
        self.ops[q].append((waits, fn, (k, 16)))
        self._record(reads, writes, (k, prev + 16, None, None))

    def barrier(self):
        tgt = [(e, self.cnt[e]) for e in self.ENGS if self.cnt[e] > 0]
        for q in self.ENGS:
            n = self.dcnt[q]
            for slot in range(min(n, self.NSLOT)):
                tgt.append((("dma", q, slot), 16 * ((n - 1 - slot) // self.NSLOT + 1)))
        for e in self.ENGS:
            waits = []
            for k, v in tgt:
                if k == e:
                    continue
                if self.waited[e].get(k, 0) >= v:
                    continue
                self.waited[e][k] = v
                waits.append((k, v))
            if waits:
                self.ops[e].append((waits, None, None))

    def replay(self, nc):
        semkeys = list(self.ENGS)
        for q in self.ENGS:
            for s in range(min(self.dcnt[q], self.NSLOT)):
                semkeys.append(("dma", q, s))
        with ExitStack() as st:
            sems = {}
            for k in semkeys:
                nm = k if isinstance(k, str) else f"d_{k[1]}_{k[2]}"
                sems[k] = st.enter_context(nc.semaphore("s_" + nm))
            block = st.enter_context(nc.Block())
            engmap = {"pe": "tensor", "dve": "vector", "act": "scalar", "pool": "gpsimd", "sp": "sync"}

            def mk(ename):
                oplist = self.ops[ename]

                def body(e):
                    for waits, fn, inc in oplist:
                        for k, v in waits:
                            e.wait_ge(sems[k], v)
                        if fn is not None:
                            fn(e).then_inc(sems[inc[0]], inc[1])
                return body

            for ename in self.ENGS:
                if self.ops[ename]:
                    getattr(block, engmap[ename])(mk(ename))


class Builder:
    def __init__(self, NP, LP, LS, depth=4, dbg=False):
        self.NP, self.LP, self.LS, self.depth = NP, LP, LS, depth
        self.nc = nc = bass.Bass("TRN2", target_bir_lowering=False)
        self.P = Prog()
        self.st = ExitStack()
        self.groups = [dict(n_seq=NP, L=LP, cond=0, ntok=NP * LP), dict(n_seq=1, L=LS, cond=1, ntok=LS)]
        nA = (depth + 1) // 2
        nB = depth // 2
        self.nA, self.nB = nA, nB

        def din(name, shape, dt=F32):
            return nc.dram_tensor(name, list(shape), dt, kind="ExternalInput").ap()

        def dout(name, shape, dt=F32):
            return nc.dram_tensor(name, list(shape), dt, kind="ExternalOutput").ap()

        def dint(name, shape, dt=F32):
            return nc.dram_tensor(name, list(shape), dt, kind="Internal").ap()

        self.xin = [din("xp", [NP * LP, D]), din("xs", [LS, D])]
        self.pos = din("pos", [LS, D])
        self.condT = din("condT", [128, 16])
        self.s0 = din("s0", [max(nB, 1), 2, NH, DK, DV])
        self.w_ada = din("w_ada", [depth, D, 3 * D])
        self.b_adaT = din("b_adaT", [128, depth * 24])
        self.ln_g = din("ln_g", [depth, D])
        self.ln_b = din("ln_b", [depth, D])
        self.a_w_in = din("a_w_in", [nA, D, 3 * E])
        self.a_ln_gT = din("a_ln_gT", [128, nA * 16])
        self.a_ln_bT = din("a_ln_bT", [128, nA * 16])
        self.a_w_sT = din("a_w_sT", [nA, 128, 8 * 128])
        self.a_b_s = din("a_b_s", [nA, 8 * 128])
        self.a_w_out = din("a_w_out", [nA, E, D])
        self.b_w_in = din("b_w_in", [max(nB, 1), D, BW])
        self.b_convT = din("b_convT", [128, max(nB, 1) * 32 * 5])
        self.b_alogP = din("b_alogP", [max(nB, 1), 128, 1])
        self.b_dtbP = din("b_dtbP", [max(nB, 1), 128, 1])
        self.b_norm_g = din("b_norm_g", [max(nB, 1), DV])
        self.pmask_d = din("pmask", [128, 4])
        self.lvs_d = din("lvs", [128, 128])
        self.b_w_out = din("b_w_out", [max(nB, 1), E, D])
        self.yout = [dout("yp", [NP * LP, D]), dout("ys", [LS, D])]
        self.ns = dout("ns", [NP, max(nB, 1), 2, NH, DK, DV])
        self.X = [dint("X0", [NP * LP, D]), dint("X1", [LS, D])]
        self.wa_in = dint("wa_in", [nA, D, 3 * E], BF16)
        self.wa_out = dint("wa_out", [nA, E, D], BF16)
        self.wb_in = dint("wb_in", [max(nB, 1), D, BW], BF16)
        self.wb_out = dint("wb_out", [max(nB, 1), E, D], BF16)
        self.HTd = [dint("HTd0", [8, 128, NP * LP], BF16), dint("HTd1", [8, 128, LS], BF16)]
        self.YTd = [dint("YTd0", [16, 128, NP * LP], BF16), dint("YTd1", [16, 128, LS], BF16)]
        self.dbg = None
        if dbg:
            self.dbg = dout("dbg", [128, 4096])
        self.Bw = {}
        self.BX = [[Buf(f"X{g}_{t}") for t in range(self.groups[g]["ntok"] // 128)] for g in range(2)]
        self.BHT = [[Buf() for _ in range(self.groups[g]["ntok"] // 128)] for g in range(2)]
        self.BYT = [[[Buf() for _ in range(self.groups[g]["ntok"] // 128)] for _h in range(NH)] for g in range(2)]
        self.BNS = Buf("ns")
        self.BYO = Buf("yout")
        self._ps_i = 0

    def sb(self, name, shape, dt):
        return self.st.enter_context(self.nc.sbuf_tensor(name, list(shape), dt))

    def MM(self, out, lhsT, rhs, start, stop, R, W):
        self.P.op("pe", lambda e: e.matmul(out, lhsT=lhsT, rhs=rhs, start=start, stop=stop), R, W)

    def TR(self, out, in_, ident, R, W):
        self.P.op("pe", lambda e: e.transpose(out, in_, ident), R, W)

    def ACT(self, out, in_, func, R, W, bias=None, scale=None, accum=None):
        kw = {}
        if bias is not None:
            kw["bias"] = bias
        if scale is not None:
            kw["scale"] = scale
        if accum is not None:
            kw["accum_out"] = accum
        self.P.op("act", lambda e: e.activation(out=out, in_=in_, func=func, **kw), R, W)

    def TS(self, eng, out, in0, s1, s2, op0, op1, R, W):
        if op1 is None:
            self.P.op(eng, lambda e: e.tensor_scalar(out=out, in0=in0, scalar1=s1, scalar2=None, op0=op0), R, W)
        else:
            self.P.op(eng, lambda e: e.tensor_scalar(out=out, in0=in0, scalar1=s1, scalar2=s2, op0=op0, op1=op1), R, W)

    def STT(self, out, in0, scalar, in1, op0, op1, R, W):
        self.P.op("dve", lambda e: e.scalar_tensor_tensor(out=out, in0=in0, scalar=scalar, in1=in1, op0=op0, op1=op1), R, W)

    def TT(self, eng, out, in0, in1, op, R, W):
        self.P.op(eng, lambda e: e.tensor_tensor(out=out, in0=in0, in1=in1, op=op), R, W)

    def CP(self, eng, out, in_, R, W):
        if eng == "act":
            self.ACT(out, in_, AF.Copy, R, W)
        else:
            self.P.op(eng, lambda e: e.tensor_copy(out=out, in_=in_), R, W)

    def MSET(self, eng, ap, val, W):
        self.P.op(eng, lambda e: e.memset(ap, val), (), W)

    def DMA(self, q, out, in_, R, W):
        self.P.dma(q, lambda e: e.dma_start(out=out, in_=in_), R, W)

    def dump(self, src, col0, n, R):
        if self.dbg is None:
            return
        if not hasattr(self, "dbgst"):
            self.dbgst = self.sb("dbgst", [128, 512], F32)
            self.Bdbg = Buf("dbg")
        self.CP("act", self.dbgst[:, 0:n], src, R, [self.Bdbg])
        self.DMA("sp", self.dbg[:, col0:col0 + n], self.dbgst[:, 0:n], [self.Bdbg], [])

    def ps(self):
        i = self._ps_i
        self._ps_i = (i + 1) % len(self.psb)
        return self.psb[i], self.Bps[i]

    def build(self):
        nc = self.nc
        sb = self.sb
        self.psb = [self.st.enter_context(nc.psum_tensor(f"ps{i}", [128, 512], F32)) for i in range(8)]
        self.Bps = [Buf(f"ps{i}", x=True) for i in range(8)]
        self.ident = sb("ident", [128, 128], F32)
        self.identb = sb("identb", [128, 128], BF16)
        self.onesf = sb("onesf", [128, 128], F32)
        self.onesb = sb("onesb", [128, 128], BF16)
        self.Bc = Buf("consts")
        self.MSET("pool", self.onesf[:], 1.0, [self.Bc])
        self.P.op("pool", lambda e: e.affine_select(out=self.ident[:], in_=self.onesf[:], pattern=[[-1, 128]],
                                                     compare_op=ALU.is_equal, fill=0.0, base=0, channel_multiplier=1),
                  [self.Bc], [self.Bc])
        self.CP("dve", self.identb[:], self.ident[:], [self.Bc], [self.Bc])
        self.CP("dve", self.onesb[:], self.onesf[:], [self.Bc], [self.Bc])
        self.slabf = sb("slabf", [128, 12 * 1024], F32)
        self.slabb = sb("slabb", [128, 49 * 1024], BF16)
        self.sb_wo = sb("WO", [128, 16, 1024], BF16)
        self.MOD = sb("MOD", [128, self.depth * 48], F32)
        self.SC1 = sb("SC1", [128, self.depth * 16], F32)
        self.GATE = sb("GATE", [128, 2 * D], F32)
        self.LNG = sb("LNG", [128, D], F32)
        self.LNB = sb("LNB", [128, D], F32)
        self.BGATE = Buf("gate")
        self.BLN = Buf("ln")
        self.BMOD = Buf("mod")

        import os
        stage = int(os.environ.get("KSTAGE", "99"))
        if stage >= 1:
            self.weight_casts()
        self.small_consts()
        if stage >= 2:
            self.prologue()
            self.P.barrier()
        if stage >= 3:
            self.adaln()
        for i in range(self.depth):
            if stage < 4:
                break
            self.P.barrier()
            self.layer_consts(i)
            self.P.barrier()
            if stage < 5:
                break
            if i % 2 == 0:
                self.layer_a(i)
            else:
                self.layer_b(i)
        self.P.barrier()
        self.P.replay(nc)
        self.st.close()
        return nc

    def weight_casts(self):
        for nm, src, dst, n, rows in (("a_in", self.a_w_in, self.wa_in, self.nA, D), ("a_out", self.a_w_out, self.wa_out, self.nA, E),
                                      ("b_in", self.b_w_in, self.wb_in, self.nB, D), ("b_out", self.b_w_out, self.wb_out, self.nB, E)):
            for l in range(n):
                b = Buf(f"w_{nm}{l}")
                self.Bw[(nm, l)] = b
                for r0 in range(0, rows, 256):
                    self.DMA("pool", dst[l, r0:r0 + 256, :], src[l, r0:r0 + 256, :], [], [b])

    def adaln(self):
        P = self.P
        sf = self.slabf
        cT = sf[:, 0:16]
        sT = sf[:, 16:32]
        bT = sf[:, 32:32 + self.depth * 24]
        W = sf[:, 1024:1024 + 8192].rearrange("p (k n) -> p k n", n=1024)
        Bs = Buf("ada_s")
        Bb = Buf("ada_b")
        BW_ = Buf("adaw")
        self.DMA("sp", cT, self.condT[:, :], [], [Bs])
        self.DMA("sp", bT, self.b_adaT[:, :], [], [Bb])
        self.ACT(sT, cT, AF.Silu, [Bs], [Bs])
        for i in range(self.depth):
            pt, bp = self.ps()
            for part in range(3):
                self.DMA("sp", W, self.w_ada[i, :, part * 1024:(part + 1) * 1024].rearrange("(k p) n -> p k n", p=128), [], [BW_])
                for c8 in range(8):
                    ch = part * 8 + c8
                    for kc in range(8):
                        self.MM(pt[:, ch * 2:ch * 2 + 2], W[:, kc, c8 * 128:(c8 + 1) * 128], sT[:, kc * 2:kc * 2 + 2],
                                kc == 0, kc == 7, [BW_, Bs], [bp])
            mod = self.MOD[:, i * 48:(i + 1) * 48].rearrange("p (ch c) -> p ch c", c=2)
            self.TT("dve", mod, pt[:, 0:48].rearrange("p (ch c) -> p ch c", c=2),
                    bT[:, i * 24:(i + 1) * 24].unsqueeze(2).broadcast_to([128, 24, 2]), ALU.add, [bp, Bb], [self.BMOD])
            self.TS("dve", self.SC1[:, i * 16:(i + 1) * 16], self.MOD[:, i * 48 + 16:i * 48 + 32], 1.0, None, ALU.add, None,
                    [self.BMOD], [self.BMOD])
        if self.dbg is not None:
            self.DMA("sp", self.dbg[:, 0:48 * self.depth], self.MOD[:, :], [self.BMOD], [])
            self.DMA("sp", self.dbg[:, 512:512 + 16 * self.depth], self.SC1[:, :], [self.BMOD], [])

    def shift_ap(self, i, kc, c):
        o = i * 48 + kc * 2 + c
        return self.MOD[:, o:o + 1]

    def scale1_ap(self, i, kc, c):
        o = i * 16 + kc * 2 + c
        return self.SC1[:, o:o + 1]

    def layer_consts(self, i):
        gb = self.slabf[:, 0:128]
        Bg = Buf("gb")
        for c in range(2):
            for half in range(2):
                pt, bp = self.ps()
                for q in range(4):
                    kc = half * 4 + q
                    o = i * 48 + (16 + kc) * 2 + c
                    self.CP("dve", gb, self.MOD[:, o:o + 1].broadcast_to([128, 128]), [self.BMOD], [Bg])
                    self.MM(pt[:, q * 128:(q + 1) * 128], gb, self.ident[:], True, True, [Bg, self.Bc], [bp])
                self.CP("act", self.GATE[:, c * D + half * 512:c * D + (half + 1) * 512], pt[:, :], [bp], [self.BGATE])
        self.DMA("sp", self.LNG[:], self.ln_g[i:i + 1, :].partition_broadcast(128), [], [self.BLN])
        self.DMA("sp", self.LNB[:], self.ln_b[i:i + 1, :].partition_broadcast(128), [], [self.BLN])

    def prologue(self):
        sf = self.slabf
        A = sf[:, 0:4096].rearrange("p (s d) -> p s d", d=1024)
        Bq = sf[:, 4096:8192].rearrange("p (s d) -> p s d", d=1024)
        Ba, Bb = Buf("pa"), Buf("pb")
        for t4 in range(self.LS // 512):
            rows = slice(t4 * 512, (t4 + 1) * 512)
            self.DMA("sp", A, self.xin[1][rows, :].rearrange("(s p) d -> p s d", p=128), [], [Ba])
            self.DMA("sp", Bq, self.pos[rows, :].rearrange("(s p) d -> p s d", p=128), [], [Bb])
            self.TT("dve", A, A, Bq, ALU.add, [Ba, Bb], [Ba])
            self.DMA("pool", self.X[1][rows, :].rearrange("(s p) d -> p s d", p=128), A, [Ba],
                     [self.BX[1][t4 * 4 + s] for s in range(4)])

    def load_x_tile(self, i, g, t, XT, Bx, nsub):
        first = (i == 0 and g == 0)
        src = self.xin[g] if first else self.X[g]
        rows = slice(t * 128, (t + nsub) * 128)
        R = [] if first else [self.BX[g][t + s] for s in range(nsub)]
        self.DMA("sp", XT, src[rows, :].rearrange("(s p) d -> p s d", p=128), R, [Bx])

    def make_ht(self, i, c, XT, Bx, nsub, HT, Bh):
        for kc in range(8):
            pt, bp = self.ps()
            for s in range(nsub):
                self.TR(pt[:, s * 128:(s + 1) * 128], XT[:, s, kc * 128:(kc + 1) * 128], self.ident[:], [Bx, self.Bc], [bp])
            self.TS("dve", HT[:, kc, 0:nsub * 128], pt[:, 0:nsub * 128], self.scale1_ap(i, kc, c), self.shift_ap(i, kc, c),
                    ALU.mult, ALU.add, [bp, self.BMOD], [Bh])

    def epilogue(self, i, g, c, t, pts, xrow, Bx, R1, XN, Br, last):
        import os
        kep = int(os.environ.get("KEP", "99"))
        if kep < 1:
            return
        for half in range(2):
            pt, bp = pts[half]
            hs = slice(half * 512, (half + 1) * 512)
            self.TT("dve", R1[:, hs], pt[:, :], self.GATE[:, c * D + half * 512:c * D + (half + 1) * 512], ALU.mult,
                    [bp, self.BGATE], [Br])
        self.STT(R1, xrow, ALPHA, R1, ALU.mult, ALU.add, [Bx, Br], [Br])
        if kep < 2:
            return
        st6 = self.stat[:, 0:12]
        mv = self.stat[:, 12:14]
        rs = self.stat[:, 14:15]
        for half in range(2):
            self.P.op("dve", (lambda o, a: lambda e: e.bn_stats(out=o, in_=a))(st6[:, half * 6:(half + 1) * 6], R1[:, half * 512:(half + 1) * 512]),
                      [Br], [self.Bstat])
        self.P.op("dve", lambda e: e.bn_aggr(out=mv, in_=st6), [self.Bstat], [self.Bstat])
        self.ACT(rs, mv[:, 1:2], AF.Sqrt, [self.Bstat], [self.Bstat], bias=self.epsc[:, 0:1])
        self.P.op("dve", lambda e: e.reciprocal(out=rs, in_=rs), [self.Bstat], [self.Bstat])
        if kep < 3:
            return
        self.TS("dve", XN, R1, mv[:, 0:1], rs, ALU.subtract, ALU.mult, [Br, self.Bstat], [Br])
        if kep < 4:
            return
        self.TT("pool", XN, XN, self.LNG[:], ALU.mult, [Br, self.BLN], [Br])
        self.TT("pool", XN, XN, self.LNB[:], ALU.add, [Br, self.BLN], [Br])
        if kep < 5:
            return
        if last:
            self.DMA("sp", self.yout[g][t * 128:(t + 1) * 128, :], XN, [Br], [self.BYO])
        else:
            self.DMA("sp", self.X[g][t * 128:(t + 1) * 128, :], XN, [Br], [self.BX[g][t]])

    def small_consts(self):
        if hasattr(self, "stat"):
            return
        self.stat = self.sb("stat", [128, 16], F32)
        self.Bstat = Buf("stat")
        self.epsc = self.sb("epsc", [128, 2], F32)
        self.MSET("dve", self.epsc[:, 0:1], EPS, [self.Bc])
        self.MSET("dve", self.epsc[:, 1:2], 1.0, [self.Bc])

    def layer_a(self, i):
        self.small_consts()
        l = i // 2
        last = (i == self.depth - 1)
        sf, sbb = self.slabf, self.slabb
        XT = sf[:, 0:4096].rearrange("p (s d) -> p s d", d=1024)
        BIAS = sf[:, 4096:6144].rearrange("p (c q) -> p c q", q=128)
        R1 = sf[:, 6144:8192].rearrange("p (b n) -> p b n", n=1024)
        T1 = sf[:, 8192:9216].rearrange("p (b n) -> p b n", n=512)
        WSF = sf[:, 9216:10240]
        BSB = sf[:, 10240:11264].rearrange("p (g q) -> p g q", q=128)
        RS = sf[:, 11264:11392]
        GLN = sf[:, 11392:11408]
        BLNv = sf[:, 11408:11424]
        vst = sf[:, 11424:11424 + 64]
        HT = sbb[:, 0:4096].rearrange("p (k n) -> p k n", n=512)
        U = sbb[:, 4096:12288].rearrange("p (c n) -> p c n", n=512)
        Z = sbb[:, 12288:20480].rearrange("p (c n) -> p c n", n=512)
        VGa = sbb[:, 20480:28672].rearrange("p (b n) -> p b n", n=2048)
        VHa = sbb[:, 28672:36864].rearrange("p (b n) -> p b n", n=2048)
        WB = sbb[:, 36864:49152].rearrange("p (b k n) -> p b k n", k=8, n=512)
        WST = self.wst[:].rearrange("p (g q) -> p g q", q=128)
        WO = self.sb_wo[:]
        self.VGa, self.VHa = VGa, VHa
        Bxt, Bht, Bwst, Bbias, Bwo = Buf("XT"), Buf("HT"), Buf("WST"), Buf("BIAS"), Buf("WO")
        BU = [Buf() for _ in range(16)]
        BZ = [Buf() for _ in range(16)]
        BWB = [Buf(), Buf(), Buf()]
        BR = [Buf(), Buf()]
        BT1 = [Buf(), Buf()]
        Bsm = Buf("small")
        self.DMA("sp", WSF, self.a_w_sT[l, :, :], [], [Bwst])
        self.CP("dve", WST.rearrange("p g q -> p (g q)"), WSF, [Bwst], [Bwst])
        self.DMA("sp", GLN, self.a_ln_gT[:, l * 16:(l + 1) * 16], [], [Bsm])
        self.DMA("sp", BLNv, self.a_ln_bT[:, l * 16:(l + 1) * 16], [], [Bsm])
        self.DMA("sp", BSB.rearrange("p g q -> p (g q)"), self.a_b_s[l:l + 1, :].partition_broadcast(128), [], [Bsm])
        for gq in range(8):
            pt, bp = self.ps()
            self.MM(pt[:, 0:128], self.onesb[:], WST[:, gq, :], True, True, [self.Bc, Bwst], [bp])
            self.CP("act", RS, pt[:, 0:128], [bp], [Bsm])
            for cc in range(2):
                ch = gq * 2 + cc
                self.STT(BIAS[:, ch, :], RS, BLNv[:, ch:ch + 1], BSB[:, gq, :], ALU.mult, ALU.add, [Bsm], [Bbias])
        import os
        sub = int(os.environ.get("KSUB", "99"))
        if sub < 1:
            return
        self.DMA("sp", WO, self.wa_out[l].rearrange("(k p) n -> p k n", p=128), [self.Bw[("a_out", l)]], [Bwo])
        wsrc = self.wa_in[l]
        wcnt = [0]

        def load_w(col0):
            b = wcnt[0] % 3
            wcnt[0] += 1
            self.DMA("sp", WB[:, b, :, :], wsrc[:, col0:col0 + 512].rearrange("(k p) n -> p k n", p=128),
                     [self.Bw[("a_in", l)]], [BWB[b]])
            return b

        for g, G in enumerate(self.groups):
            c = G["cond"]
            for t4 in range(G["ntok"] // 512):
                t = t4 * 4
                self.load_x_tile(i, g, t, XT, Bxt, 4)
                self.make_ht(i, c, XT, Bxt, 4, HT, Bht)
                if sub < 2:
                    continue
                blocks = [("v", 2048 + b * 512, b) for b in range(4)] + [("u", b * 512, b) for b in range(4)] + \
                         [("z", 4096 + b * 512, b) for b in range(4)]
                nxt = load_w(blocks[0][1])
                for bi, (kind, col0, b4) in enumerate(blocks):
                    wb = nxt
                    if bi + 1 < len(blocks):
                        nxt = load_w(blocks[bi + 1][1])
                    if kind == "v":
                        for s in range(4):
                            pt, bp = self.ps()
                            for kc in range(8):
                                self.MM(pt[:, :], HT[:, kc, s * 128:(s + 1) * 128], WB[:, wb, kc, :], kc == 0, kc == 7,
                                        [Bht, BWB[wb]], [bp])
                            self.ACT(self._vg(sf, s)[:, b4 * 512:(b4 + 1) * 512],
                                     pt[:, :], AF.Gelu_apprx_tanh, [bp], [self.BVG4[s]])
                    else:
                        dst, Bd, fn = (U, BU, AF.Gelu_apprx_tanh) if kind == "u" else (Z, BZ, AF.Silu)
                        for cc in range(4):
                            ch = b4 * 4 + cc
                            pt, bp = self.ps()
                            for kc in range(8):
                                self.MM(pt[:, :], WB[:, wb, kc, cc * 128:(cc + 1) * 128], HT[:, kc, :], kc == 0, kc == 7,
                                        [Bht, BWB[wb]], [bp])
                            self.ACT(dst[:, ch, :], pt[:, :], fn, [bp], [Bd[ch]])
                    if kind == "v" and b4 == 3:
                        for s in range(4):
                            vg = self._vg(sf, s)
                            st = vst[:, 0:24]
                            mv = vst[:, 24:26]
                            rs = vst[:, 26:27]
                            for q in range(4):
                                self.P.op("dve", (lambda o, a: lambda e: e.bn_stats(out=o, in_=a))(st[:, q * 6:(q + 1) * 6], vg[:, q * 512:(q + 1) * 512]),
                                          [self.BVG4[s]], [Bsm])
                            self.P.op("dve", lambda e: e.bn_aggr(out=mv, in_=st), [Bsm], [Bsm])
                            self.ACT(rs, mv[:, 1:2], AF.Sqrt, [Bsm], [Bsm], bias=self.epsc[:, 0:1])
                            self.P.op("dve", lambda e: e.reciprocal(out=rs, in_=rs), [Bsm], [Bsm])
                            self.TS("dve", self._vh(sbb, s), vg, mv[:, 0:1], rs, ALU.subtract, ALU.mult, [self.BVG4[s], Bsm], [self.BVH4[s]])
                if sub < 3:
                    continue
                for ch in range(16):
                    self.TT("pool", U[:, ch, :], U[:, ch, :], Z[:, ch, :], ALU.mult, [BU[ch], BZ[ch]], [BU[ch]])
                for ch in range(16):
                    pt, bp = self.ps()
                    for s in range(4):
                        self.MM(pt[:, s * 128:(s + 1) * 128], self._vh(sbb, s)[:, ch * 128:(ch + 1) * 128], WST[:, ch // 2, :], True, True,
                                [self.BVH4[s], Bwst], [bp])
                    tb = ch % 2
                    self.STT(T1[:, tb, :].rearrange("p (s q) -> p s q", q=128), pt[:, :].rearrange("p (s q) -> p s q", q=128),
                             GLN[:, ch:ch + 1], BIAS[:, ch, :].unsqueeze(1).broadcast_to([128, 4, 128]), ALU.mult, ALU.add,
                             [bp, Bsm, Bbias], [BT1[tb]])
                    self.TT("pool", Z[:, ch, :], T1[:, tb, :], U[:, ch, :], ALU.mult, [BT1[tb], BU[ch]], [BZ[ch]])
                if sub < 4:
                    continue
                for s in range(4):
                    pts = []
                    for half in range(2):
                        pt, bp = self.ps()
                        for ec in range(16):
                            self.MM(pt[:, :], Z[:, ec, s * 128:(s + 1) * 128], WO[:, ec, half * 512:(half + 1) * 512], ec == 0, ec == 15,
                                    [BZ[ec], Bwo], [bp])
                        pts.append((pt, bp))
                    rb = s % 2
                    self.epilogue(i, g, c, t + s, pts, XT[:, s, :], Bxt, R1[:, rb, :], R1[:, rb, :], BR[rb], last)

    def _vg(self, sf, s):
        return self.VGa[:, s, :]

    def _vh(self, sbb, s):
        return self.VHa[:, s, :]

    def layer_b_consts(self, j):
        if not hasattr(self, "negcat"):
            self.negcat = self.sb("negcat", [128, 2, 256], F32)
            self.cw = self.sb("cw", [128, 160], F32)
            self.ngt = self.sb("ngt", [128, 256], F32)
            self.gsc = self.sb("gsc", [128, 4], F32)
            self.pmask = self.sb("pmaskt", [128, 4], F32)
            self.lvs = self.sb("lvst", [128, 128], F32)
            self.Bbc = Buf("bconst")
            zer = self.slabf[:, 0:128]
            Bz = Buf("zer")
            self.MSET("pool", zer, 0.0, [Bz])
            for d_, (pat, cm) in enumerate((([[1, 128]], -1), ([[-1, 128]], 1))):
                for kk, cmp in enumerate((ALU.is_ge, ALU.is_gt)):
                    self.P.op("pool", (lambda o, pat, cm, cmp: lambda e: e.affine_select(
                        out=o, in_=zer, pattern=pat, compare_op=cmp, fill=NEG, base=0, channel_multiplier=cm))(
                        self.negcat[:, d_, kk * 128:(kk + 1) * 128], pat, cm, cmp), [Bz], [self.Bbc])
        self.DMA("sp", self.pmask[:], self.pmask_d[:, :], [], [self.Bbc])
        self.DMA("sp", self.lvs[:], self.lvs_d[:, :], [], [self.Bbc])
        self.DMA("sp", self.cw[:], self.b_convT[:, j * 160:(j + 1) * 160], [], [self.Bbc])
        self.DMA("sp", self.ngt[:], self.b_norm_g[j:j + 1, :].partition_broadcast(128), [], [self.Bbc])
        self.DMA("sp", self.gsc[:, 0:1], self.b_dtbP[j], [], [self.Bbc])
        self.DMA("sp", self.gsc[:, 2:3], self.b_alogP[j], [], [self.Bbc])
        self.ACT(self.gsc[:, 3:4], self.gsc[:, 2:3], AF.Exp, [self.Bbc], [self.Bbc])
        self.TS("dve", self.gsc[:, 1:2], self.gsc[:, 3:4], -1.0, None, ALU.mult, None, [self.Bbc], [self.Bbc])

    def layer_b(self, i):
        j = i // 2
        last = (i == self.depth - 1)
        sf, sbb = self.slabf, self.slabb
        wor = self.sb_wo[:].rearrange("p a b -> p (a b)")
        self.layer_b_consts(j)
        ident, identb = self.ident, self.identb
        FEAT = sf[:, 0:4096]
        Bfeat = Buf("feat")
        WABF = sf[:, 9728:10752].rearrange("p (k n) -> p k n", n=128)
        WAB = sbb[:, 47616:48640].rearrange("p (k n) -> p k n", n=128)
        Bwab = Buf("wab")
        self.MSET("dve", WABF, 0.0, [Bwab])
        for q4, c0 in enumerate((0, 32, 64, 96)):
            self.DMA("sp", WABF[:, :, c0:c0 + 8],
                     self.b_w_in[j, :, 6144 + q4 * 8:6144 + (q4 + 1) * 8].rearrange("(k p) n -> p k n", p=128), [], [Bwab])
        self.CP("dve", WAB, WABF, [Bwab], [Bwab])
        import os
        kb = int(os.environ.get("KB", "99"))
        self.kb = kb
        for g, G in enumerate(self.groups):
            self.P.barrier()
            if kb >= 1:
                self.b_phase0(i, j, g, G, FEAT, Bfeat, WAB, Bwab)
            self.P.barrier()
            if kb >= 2:
                self.b_heads(i, j, g, G, FEAT, Bfeat, wor)
            self.P.barrier()
            if kb >= 5:
                self.b_outproj(i, j, g, G, last)

    def b_phase0(self, i, j, g, G, FEAT, Bfeat, WAB, Bwab):
        sf, sbb = self.slabf, self.slabb
        c = G["cond"]
        XT = sf[:, 4096:8192].rearrange("p (s d) -> p s d", d=1024)
        T1 = sf[:, 8192:8704]
        T2 = sf[:, 8704:9216]
        T3 = sf[:, 9216:9728]
        MASK = sf[:, 10752:11264]
        HTb = sbb[:, 0:8192].rearrange("p (b k n) -> p b k n", k=8, n=512)
        Bxt, Bt = Buf("bxt"), Buf("bt")
        Bh = [Buf(), Buf()]
        Bm = Buf("mask")
        self.MSET("dve", MASK, 1.0, [Bm])
        self.MSET("dve", MASK.rearrange("p (a b) -> p a b", b=128)[:, :, 0:1], 0.0, [Bm])
        import os
        kp = int(os.environ.get("KP", "99"))
        for b4 in range(G["ntok"] // 512 if kp >= 1 else 0):
            hb = b4 % 2
            cols = slice(b4 * 512, (b4 + 1) * 512)
            self.load_x_tile(i, g, b4 * 4, XT, Bxt, 4)
            self.make_ht(i, c, XT, Bxt, 4, HTb[:, hb], Bh[hb])
            self.DMA("sp", self.HTd[g][:, :, cols].rearrange("k p n -> p k n"), HTb[:, hb], [Bh[hb]],
                     [self.BHT[g][b4 * 4 + s_] for s_ in range(4)])
            if kp < 2:
                continue
            pt, bp = self.ps()
            for kc in range(8):
                self.MM(pt[:, :], WAB[:, kc, :], HTb[:, hb, kc, :], kc == 0, kc == 7, [Bwab, Bh[hb]], [bp])
            if kp < 3:
                continue
            self.ACT(T1[0:64, :], pt[0:64, :], AF.Exp, [bp, self.Bbc], [Bt], bias=self.gsc[0:64, 0:1])
            self.ACT(T1[0:64, :], T1[0:64, :], AF.Ln, [Bt, self.Bc], [Bt], bias=self.epsc[0:64, 1:2])
            self.TS("dve", T1[0:64, :], T1[0:64, :], self.gsc[0:64, 1:2], None, ALU.mult, None, [Bt, self.Bbc], [Bt])
            self.ACT(T1[64:128, :], pt[64:128, :], AF.Exp, [bp], [Bt], scale=-1.0)
            self.ACT(T1[64:128, :], T1[64:128, :], AF.Ln, [Bt, self.Bc], [Bt], bias=self.epsc[64:128, 1:2])
            self.TS("dve", T1[64:128, :], T1[64:128, :], -1.0, None, ALU.mult, None, [Bt], [Bt])
            if kp < 4:
                continue
            self.P.op("dve", (lambda o, m, d: lambda e: e.tensor_tensor_scan(out=o, data0=m, data1=d, initial=0.0, op0=ALU.mult, op1=ALU.add))(
                T2, MASK, T1), [Bt, Bm], [Bt])
            self.TT("dve", T3, T1, T2, ALU.subtract, [Bt], [Bt])
            self.TT("dve", T3.rearrange("p (a b) -> p a b", b=128), T3.rearrange("p (a b) -> p a b", b=128),
                    T2.rearrange("p (a b) -> p a b", b=128)[:, :, 127:128].broadcast_to([128, 4, 128]), ALU.add, [Bt], [Bt])
            pm = self.pmask
            self.TS("dve", FEAT[:, cols], T2, pm[:, 0:1], None, ALU.mult, None, [Bt, self.Bbc], [Bfeat])
            self.STT(FEAT[:, cols], T3, pm[:, 1:2], FEAT[:, cols], ALU.mult, ALU.add, [Bt, self.Bbc, Bfeat], [Bfeat])
            self.STT(FEAT[:, cols], T1, pm[:, 2:3], FEAT[:, cols], ALU.mult, ALU.add, [Bt, self.Bbc, Bfeat], [Bfeat])

    def b_heads(self, i, j, g, G, FEAT, Bfeat, wor):
        sf, sbb = self.slabf, self.slabb
        ident, identb = self.ident, self.identb
        n_seq, L, ntok = G["n_seq"], G["L"], G["ntok"]
        nt = ntok // 128
        tps = L // 128
        ACC = sf[:, 4096:4608]
        A_ = sf[:, 4608:5120]
        RN = sf[:, 5120:5632]
        EX8 = sf[:, 5632:7680].rearrange("p (b n) -> p b n", n=256)
        OS2 = sf[:, 7680:8192].rearrange("p (b n) -> p b n", n=256)
        S8 = sf[:, 8192:10240].rearrange("p (b n) -> p b n", n=256)
        Y12 = sf[:, 10240:10752].rearrange("p (b n) -> p b n", n=256)
        SCAL8 = sf[:, 10752:10816].rearrange("p (b n) -> p b n", n=8)
        SELD = sf[:, 10816:11328].rearrange("p (d k n) -> p d k n", k=2, n=128)
        fst = self.stat
        HTb = sbb[:, 0:4096].rearrange("p (b k n) -> p b k n", k=8, n=256)
        WH = sbb[:, 4096:10240].rearrange("p (k n) -> p k n", n=768)
        PRE = sbb[:, 10240:18432].rearrange("p (b n) -> p b n", n=4096)
        SQ = sbb[:, 18432:18944]
        QT = sbb[:, 18944:23040]
        KT = sbb[:, 23040:27136]
        V = sbb[:, 27136:35328].rearrange("p (t n) -> p t n", n=256)
        KTOK = sbb[:, 35328:39424].rearrange("p (t n) -> p t n", n=128)
        ZS = sbb[:, 39424:47616].rearrange("p (t n) -> p t n", n=256)
        VT = sbb[:, 48640:49152]
        YB2 = sbb[:, 49152:49664].rearrange("p (b n) -> p b n", n=256)
        YT2 = sbb[:, 49664:50176].rearrange("p (b n) -> p b n", n=256)
        O = wor[:, 0:nt * 256].rearrange("p (t n) -> p t n", n=256)
        jb = nt * 256
        NSET = 8
        SETW = 2304
        nch = n_seq * 2
        U8 = wor[:, jb:jb + 2048].rearrange("p (b n) -> p b n", n=256)
        b1 = jb + 2048
        VN2 = wor[:, b1:b1 + 512].rearrange("p (b n) -> p b n", n=256)
        VB2 = wor[:, b1 + 512:b1 + 1024].rearrange("p (b n) -> p b n", n=256)
        Sb8 = wor[:, b1 + 1024:b1 + 1024 + nch * 256].rearrange("p (b n) -> p b n", n=256)
        assert b1 + 1024 + nch * 256 <= 16384, (b1, nch)
        sets = []
        for q in range(NSET):
            b0 = q * SETW
            sets.append(dict(
                QKNT=sbb[:, b0:b0 + 256], NTt=sbb[:, b0 + 256:b0 + 384],
                DC=sbb[:, b0 + 384:b0 + 896].rearrange("p (b n) -> p b n", n=256),
                MC=sbb[:, b0 + 896:b0 + 1152],
                LC=sbb[:, b0 + 1152:b0 + 1664].rearrange("p (b n) -> p b n", n=256),
                KBG=sbb[:, b0 + 1664:b0 + 1792], WT=sbb[:, b0 + 1792:b0 + 1920],
                QDT=sbb[:, b0 + 1920:b0 + 2048], KD=sbb[:, b0 + 2048:b0 + 2176], EG=sbb[:, b0 + 2176:b0 + 2304],
                qi=q, B=Buf(f"set{q}"), U=U8[:, q, :], EX=EX8[:, q, :], SCAL=SCAL8[:, q, :], BU=Buf(), BQ=Buf(), BL=Buf()))
        Bsel = Buf("sel")
        BVN = [Buf(), Buf()]
        BVB = [Buf(), Buf()]
        BS = [Buf() for _ in range(nch)]
        BO = [Buf() for _ in range(nt)]
        BOS = [Buf(), Buf()]
        BY = [Buf(), Buf()]
        Bh = [Buf(), Buf()]
        Bwh, Bacc, Bsq = Buf("wh"), Buf("acc"), Buf("sq")
        BPRE = [Buf(), Buf()]
        Bq, Bk, Bv, Bkt, Bz, Bvt = Buf("QT"), Buf("KT"), Buf("V"), Buf("KTOK"), Buf("ZS"), Buf("VT")
        Bpsh = [Buf() for _ in range(8)]
        pshc = [0]

        def psh_next():
            pt_, bp_ = self.ps()
            return pt_[:, 0:64].bitcast(BF16), bp_

        wsrc = self.wb_in[j]
        Bwsrc = self.Bw[("b_in", j)]
        cnt = [0]
        for hd in range(NH):
            for (c0, n, o) in ((hd * 128, 128, 0), (1024 + hd * 128, 128, 128), (2048 + hd * 256, 256, 256), (4096 + hd * 256, 256, 512)):
                self.DMA("sp", WH[:, :, o:o + n], wsrc[:, c0:c0 + n].rearrange("(k p) n -> p k n", p=128), [Bwsrc], [Bwh])
            for pas in range(2):
                for b2 in range(ntok // 256):
                    hb = b2 % 2
                    cols = slice(b2 * 256, (b2 + 1) * 256)
                    self.DMA("sp", HTb[:, hb], self.HTd[g][:, :, cols].rearrange("k p n -> p k n"),
                             [self.BHT[g][b2 * 2], self.BHT[g][b2 * 2 + 1]], [Bh[hb]])
                    for cc in range(2):
                        wo = (pas * 2 + cc) * 128
                        pt, bp = self.ps()
                        for kc in range(8):
                            self.MM(pt[:, 0:256], WH[:, kc, wo:wo + 128], HTb[:, hb, kc, :], kc == 0, kc == 7, [Bwh, Bh[hb]], [bp])
                        self.CP("act", PRE[:, cc, cols], pt[:, 0:256], [bp], [BPRE[cc]])
                    if pas == 0:
                        for s2 in range(2):
                            t = b2 * 2 + s2
                            pt, bp = self.ps()
                            for kc in range(8):
                                self.MM(pt[:, 0:256], HTb[:, hb, kc, s2 * 128:(s2 + 1) * 128], WH[:, kc, 512:768], kc == 0, kc == 7,
                                        [Bwh, Bh[hb]], [bp])
                            self.ACT(ZS[:, t, :], pt[:, 0:256], AF.Silu, [bp], [Bz])
                import os
                kh = int(os.environ.get("KH", "99"))
                for cc in range(2 if kh >= 2 else 0):
                    kind = ("q", "k")[cc] if pas == 0 else "v"
                    cwi = (hd if kind == "q" else 8 + hd) if pas == 0 else 16 + hd * 2 + cc
                    wv = self.cw[:, cwi * 5:(cwi + 1) * 5]
                    for s_ in range(n_seq):
                        for a in range(0, L, 512):
                            bnd = min(a + 512, L)
                            n = bnd - a
                            base = s_ * L
                            self.TS("pool", ACC[:, 0:n], PRE[:, cc, base + a:base + bnd], wv[:, 2:3], None, ALU.mult, None,
                                    [BPRE[cc], self.Bbc], [Bacc])
                            for off in (-2, -1, 1, 2):
                                lo = max(a, -off) if off < 0 else a
                                hi = min(bnd, L - off) if off > 0 else bnd
                                if hi <= lo:
                                    continue
                                self.STT(ACC[:, lo - a:hi - a], PRE[:, cc, base + lo + off:base + hi + off], wv[:, off + 2:off + 3],
                                         ACC[:, lo - a:hi - a], ALU.mult, ALU.add, [BPRE[cc], self.Bbc, Bacc], [Bacc])
                            gcols = slice(base + a, base + bnd)
                            if kh < 3:
                                continue
                            if kind == "v":
                                self.ACT(VT[:, 0:n], ACC[:, 0:n], AF.Silu, [Bacc], [Bvt])
                                for q in range(n // 128):
                                    t = (base + a) // 128 + q
                                    ph, bph = psh_next()
                                    self.TR(ph, VT[:, q * 128:(q + 1) * 128], identb[:], [Bvt, self.Bc], [bph])
                                    self.CP("dve" if q % 2 else "pool" if False else "dve", V[:, t, cc * 128:(cc + 1) * 128], ph, [bph], [Bv])
                            elif kh >= 4:
                                self.ACT(A_[:, 0:n], ACC[:, 0:n], AF.Silu, [Bacc], [Bacc])
                                self.TT("pool", SQ[:, 0:n], A_[:, 0:n], A_[:, 0:n], ALU.mult, [Bacc], [Bsq])
                                pt, bp = self.ps()
                                self.MM(pt[:, 0:n], self.onesb[:], SQ[:, 0:n], True, True, [self.Bc, Bsq], [bp])
                                self.ACT(RN[:, 0:n], pt[:, 0:n], AF.Sqrt, [bp, self.Bc], [Bsq], bias=self.epsc[:, 0:1])
                                self.P.op("dve", (lambda o: lambda e: e.reciprocal(out=o, in_=o))(RN[:, 0:n]), [Bsq], [Bsq])
                                dst, Bd = (QT, Bq) if kind == "q" else (KT, Bk)
                                self.STT(dst[:, gcols], A_[:, 0:n], (DK ** -0.5) if kind == "q" else 1.0, RN[:, 0:n], ALU.mult, ALU.mult,
                                         [Bacc, Bsq], [Bd])
                                if kind == "k" and kh >= 5:
                                    for q in range(n // 128):
                                        t = (base + a) // 128 + q
                                        ph, bph = psh_next()
                                        self.TR(ph, KT[:, t * 128:(t + 1) * 128], identb[:], [Bk, self.Bc], [bph])
                                        self.CP("dve", KTOK[:, t, :], ph, [bph], [Bkt])
            if g == 0 and hd == 0:
                self.dump(FEAT[:, 0:512], 0, 512, [Bfeat])
                self.dump(QT[:, 0:512], 512, 512, [Bq])
                self.dump(KT[:, 0:512], 1024, 512, [Bk])
                self.dump(V[:, 0, :], 1536, 256, [Bv])
                self.dump(V[:, 1, :], 1792, 256, [Bv])
                self.dump(KTOK[:, 0, :], 2048, 128, [Bkt])
                self.dump(ZS[:, 0, :], 2176, 256, [Bz])
            if self.kb < 3:
                continue
            for s_ in range(n_seq):
                for d_ in range(2):
                    ch = s_ * 2 + d_
                    if g == 1:
                        self.DMA("sp", S8[:, ch, :], self.s0[j, d_, hd], [], [BS[ch]])
                    else:
                        self.MSET("dve", S8[:, ch, :], 0.0, [BS[ch]])
                    self.CP("act", Sb8[:, ch, :], S8[:, ch, :], [BS[ch]], [BS[ch]])
            self.P.barrier()
            for d_ in range(2):
                r = d_ * 32 + hd
                self.CP("dve", SELD[:, d_, 0, :], ident[:, r:r + 1].broadcast_to([128, 128]), [self.Bc], [Bsel])
                self.TT("dve", SELD[:, d_, 1, :], SELD[:, d_, 0, :], ident[:, 64 + r:64 + r + 1].broadcast_to([128, 128]), ALU.add,
                        [self.Bc, Bsel], [Bsel])
            visited = set()
            done = set()

            def visit(t, d_, ch, step, st_):
                r = d_ * 32 + hd
                tc = slice(t * 128, (t + 1) * 128)
                B_ = st_["B"]
                SC = st_["SCAL"]
                bank, bbk = self.psb[st_["qi"]], self.Bps[st_["qi"]]
                SEL1, SEL2 = SELD[:, d_, 0, :], SELD[:, d_, 1, :]
                self.TR(bank[:, 0:128], FEAT[:, tc], ident[:], [Bfeat, self.Bc], [bbk])
                yield
                self.CP("dve", SC[:, 0:1], bank[:, r:r + 1], [bbk], [B_])
                self.CP("dve", SC[:, 1:2], bank[:, 64 + r:64 + r + 1], [bbk], [B_])
                self.MM(bank[:, 0:128], SEL1, FEAT[:, tc], True, True, [Bsel, Bfeat], [bbk])
                self.MM(bank[:, 128:256], SEL2, FEAT[:, tc], True, True, [Bsel, Bfeat], [bbk])
                colT = t * 128 + (127 if d_ == 0 else 0)
                self.MM(bank[:, 256:257], SEL1, FEAT[:, colT:colT + 1], True, True, [Bsel, Bfeat], [bbk])
                yield
                self.CP("dve", SC[:, 2:3], bank[:, 256:257], [bbk], [B_])
                self.STT(st_["EX"], bank[:, 0:256], SC[:, 0:1], self.negcat[:, d_, :], ALU.subtract, ALU.add, [bbk, B_, self.Bbc], [B_])
                self.ACT(st_["EG"], bank[:, 0:128], AF.Exp, [bbk], [B_])
                yield
                self.ACT(SC[:, 3:4], SC[:, 1:2], AF.Exp, [B_], [B_])
                self.ACT(SC[:, 4:5], SC[:, 0:1], AF.Exp, [B_], [B_], bias=SC[:, 1:2])
                self.ACT(SC[:, 5:6], SC[:, 0:1], AF.Exp, [B_], [B_], bias=SC[:, 2:3], scale=-1.0)
                self.ACT(SC[:, 6:7], SC[:, 2:3], AF.Exp, [B_], [B_])
                self.ACT(st_["EX"], st_["EX"], AF.Exp, [B_], [B_])
                self.MM(bank[:, 0:128], KT[:, tc], QT[:, tc], True, True, [Bk, Bq], [bbk])
                self.MM(bank[:, 128:256], KT[:, tc], KT[:, tc], True, True, [Bk], [bbk])
                yield
                self.TT("dve", st_["QKNT"], bank[:, 0:256], st_["EX"], ALU.mult, [bbk, B_], [B_])
                self.TT("pool", st_["QDT"], QT[:, tc], st_["EG"], ALU.mult, [Bq, B_], [st_["BQ"]])
                self.TS("pool", st_["KD"], KTOK[:, t, :], SC[:, 5:6], None, ALU.mult, None, [Bkt, B_], [st_["BQ"]])
                self.TS("pool", st_["KBG"], KTOK[:, t, :], SC[:, 4:5], None, ALU.mult, None, [Bkt, B_], [st_["BQ"]])
                yield
                N = st_["QKNT"][:, 128:256]
                ph = bank[:, 0:64].bitcast(BF16)
                self.TR(ph, N, identb[:], [B_, self.Bc], [bbk])
                yield
                self.CP("act", st_["NTt"], ph, [bbk], [B_])
                yield
                Am = st_["NTt"]
                LVS = self.lvs[:]
                BL = st_["BL"]

                def mk_l(k, lb):
                    self.STT(st_["LC"][:, lb, 0:128], LVS, float(k), N, ALU.is_equal, ALU.mult, [self.Bbc, B_], [BL])
                    self.STT(st_["LC"][:, lb, 128:256], LVS, float(k), Am, ALU.is_equal, ALU.mult, [self.Bbc, B_], [BL])
                mk_l(0, 0)
                self.TT("dve", st_["DC"][:, 0, 0:128], identb[:], st_["LC"][:, 0, 0:128], ALU.subtract, [self.Bc, BL], [B_])
                self.TT("dve", st_["DC"][:, 0, 128:256], identb[:], st_["LC"][:, 0, 128:256], ALU.subtract, [self.Bc, BL], [B_])
                dcur = 0
                for k in range(1, 7):
                    lb = k % 2
                    mk_l(k, lb)
                    yield
                    Dt_, D_ = st_["DC"][:, dcur, 0:128], st_["DC"][:, dcur, 128:256]
                    self.MM(bank[:, 0:128], st_["LC"][:, lb, 128:256], Dt_, True, True, [BL, B_], [bbk])
                    self.MM(bank[:, 128:256], st_["LC"][:, lb, 0:128], D_, True, True, [BL, B_], [bbk])
                    yield
                    self.CP("act", st_["MC"], bank[:, 0:256], [bbk], [B_])
                    yield
                    self.MM(bank[:, 0:128], D_, st_["MC"][:, 0:128], True, True, [B_], [bbk])
                    self.MM(bank[:, 128:256], Dt_, st_["MC"][:, 128:256], True, True, [B_], [bbk])
                    yield
                    self.TT("dve", st_["DC"][:, 1 - dcur, :], st_["DC"][:, dcur, :], bank[:, 0:256], ALU.subtract, [B_, bbk], [B_])
                    dcur = 1 - dcur
                    yield
                TTm = st_["DC"][:, dcur, 0:128]
                vb = cnt[0] % 2
                cnt[0] += 1
                self.TS("pool", VB2[:, vb, :], V[:, t, :], SC[:, 3:4], None, ALU.mult, None, [Bv, B_], [BVB[vb]])
                self.MM(bank[:, 0:256], TTm, VB2[:, vb, :], True, True, [B_, BVB[vb]], [bbk])
                self.MM(bank[:, 256:384], st_["KBG"], TTm, True, True, [B_, st_["BQ"]], [bbk])
                yield
                self.CP("act", st_["U"], bank[:, 0:256], [bbk], [st_["BU"]])
                self.CP("dve", st_["WT"], bank[:, 256:384], [bbk], [st_["BU"]])
                yield
                while step > 0 and (ch, step - 1) not in done:
                    yield
                self.MM(bank[:, 0:256], st_["WT"], Sb8[:, ch, :], True, True, [st_["BU"], BS[ch]], [bbk])
                yield
                vn = cnt[0] % 2
                cnt[0] += 1
                self.TT("dve", VN2[:, vn, :], st_["U"], bank[:, 0:256], ALU.subtract, [st_["BU"], bbk], [BVN[vn]])
                self.MM(bank[:, 0:256], st_["QDT"], Sb8[:, ch, :], True, False, [st_["BQ"], BS[ch]], [bbk])
                self.MM(bank[:, 0:256], st_["QKNT"][:, 0:128], VN2[:, vn, :], False, True, [B_, BVN[vn]], [bbk])
                second = t in visited
                visited.add(t)
                if not second:
                    self.CP("act", O[:, t, :], bank[:, 0:256], [bbk], [BO[t]])
                else:
                    ob = cnt[0] % 2
                    cnt[0] += 1
                    OSb = OS2[:, ob, :]
                    self.TT("dve", OSb, bank[:, 0:256], O[:, t, :], ALU.add, [bbk, BO[t]], [BOS[ob]])
                self.MM(bank[:, 0:256], st_["KD"], VN2[:, vn, :], True, True, [st_["BQ"], BVN[vn]], [bbk])
                self.STT(S8[:, ch, :], S8[:, ch, :], SC[:, 6:7], bank[:, 0:256], ALU.mult, ALU.add, [BS[ch], B_, bbk], [BS[ch]])
                self.CP("act", Sb8[:, ch, :], S8[:, ch, :], [BS[ch]], [BS[ch]])
                done.add((ch, step))
                if second:
                    st6 = fst[:, 0:6]
                    mv = fst[:, 6:8]
                    ms = fst[:, 8:9]
                    self.P.op("dve", (lambda a_: lambda e: e.bn_stats(out=st6, in_=a_))(OSb), [BOS[ob]], [self.Bstat])
                    self.P.op("dve", lambda e: e.bn_aggr(out=mv, in_=st6), [self.Bstat], [self.Bstat])
                    self.STT(ms, mv[:, 0:1], mv[:, 0:1], mv[:, 1:2], ALU.mult, ALU.add, [self.Bstat], [self.Bstat])
                    self.ACT(ms, ms, AF.Sqrt, [self.Bstat, self.Bc], [self.Bstat], bias=self.epsc[:, 0:1])
                    self.P.op("dve", lambda e: e.reciprocal(out=ms, in_=ms), [self.Bstat], [self.Bstat])
                    self.STT(Y12[:, ob, :], OSb, ms, self.ngt[:], ALU.mult, ALU.mult, [BOS[ob], self.Bstat, self.Bbc], [BY[ob]])
                    self.TT("pool", YB2[:, ob, :], Y12[:, ob, :], ZS[:, t, :], ALU.mult, [BY[ob], Bz], [BY[ob]])
                    for e2 in range(2):
                        ph2 = bank[:, e2 * 64:(e2 + 1) * 64].bitcast(BF16)
                        self.TR(ph2, YB2[:, ob, e2 * 128:(e2 + 1) * 128], identb[:], [BY[ob], self.Bc], [bbk])
                        self.CP("dve", YT2[:, ob, e2 * 128:(e2 + 1) * 128], ph2, [bbk], [BY[ob]])
                    self.DMA("sp", self.YTd[g][hd * 2:hd * 2 + 2, :, tc].rearrange("e p n -> p e n"),
                             YT2[:, ob, :].rearrange("p (e n) -> p e n", n=128), [BY[ob]], [self.BYT[g][hd][t]])

            jobs = []
            for step in range(tps if self.kb >= 4 else 0):
                for s_ in range(n_seq):
                    for d_ in range(2):
                        t = s_ * tps + (step if d_ == 0 else tps - 1 - step)
                        jobs.append((t, d_, s_ * 2 + d_, step))
            free = list(range(NSET))
            active = []
            ji = 0
            while ji < len(jobs) or active:
                while ji < len(jobs) and free:
                    q = free.pop(0)
                    t, d_, ch, step = jobs[ji]
                    ji += 1
                    active.append((visit(t, d_, ch, step, sets[q]), q))
                nxt = []
                for gen, q in active:
                    try:
                        next(gen)
                        nxt.append((gen, q))
                    except StopIteration:
                        free.append(q)
                active = nxt
            self.P.barrier()
            if g == 0:
                for s_ in range(n_seq):
                    for d_ in range(2):
                        ch = s_ * 2 + d_
                        self.DMA("sp", self.ns[s_, j, d_, hd], S8[:, ch, :], [BS[ch]], [self.BNS])

    def b_outproj(self, i, j, g, G, last):
        sf, sbb = self.slabf, self.slabb
        c = G["cond"]
        WO = self.sb_wo[:]
        Bwo = Buf("wo")
        self.DMA("sp", WO, self.wb_out[j].rearrange("(k p) n -> p k n", p=128), [self.Bw[("b_out", j)]], [Bwo])
        XT2 = sf[:, 4096:6144].rearrange("p (b n) -> p b n", n=1024)
        R12 = sf[:, 6144:8192].rearrange("p (b n) -> p b n", n=1024)
        YT16 = sbb[:, 0:4096].rearrange("p (b e n) -> p b e n", e=16, n=128)
        Bx = [Buf(), Buf()]
        Br = [Buf(), Buf()]
        Byt = [Buf(), Buf()]
        for t in range(G["ntok"] // 128):
            b = t % 2
            tc = slice(t * 128, (t + 1) * 128)
            self.DMA("sp", XT2[:, b, :], self.X[g][tc, :], [self.BX[g][t]], [Bx[b]])
            self.DMA("sp", YT16[:, b], self.YTd[g][:, :, tc].rearrange("e p n -> p e n"), [self.BYT[g][h][t] for h in range(NH)], [Byt[b]])
            pts = []
            for half in range(2):
                pt, bp = self.ps()
                for ec in range(16):
                    self.MM(pt[:, :], YT16[:, b, ec, :], WO[:, ec, half * 512:(half + 1) * 512], ec == 0, ec == 15, [Byt[b], Bwo], [bp])
                pts.append((pt, bp))
            self.epilogue(i, g, c, t, pts, XT2[:, b, :], Bx[b], R12[:, b, :], R12[:, b, :], Br[b], last)


def build_program(NP, LP, LS, depth=4, dbg=False):
    import os
    dbg = dbg or bool(os.environ.get('KDBG'))
    b = Builder(NP, LP, LS, depth, dbg)
    b.wst = b.sb("wst", [128, 1024], BF16)
    b.BVG4 = [Buf() for _ in range(4)]
    b.BVH4 = [Buf() for _ in range(4)]
    return b.build()


def _pos_table(n_tokens, grid_w=64):
    def sincos(pos, dim):
        omega = (1.0 / (np.float32(10000.0) ** (np.arange(dim // 2, dtype=np.float32) / np.float32(dim // 2)))).astype(np.float32)
        ang = pos.astype(np.float32)[:, None] * omega[None, :]
        return np.concatenate([np.sin(ang), np.cos(ang)], axis=-1).astype(np.float32)
    rows = n_tokens // grid_w
    er = sincos(np.arange(rows), D // 2)
    ec = sincos(np.arange(grid_w), D // 2)
    emb = np.concatenate([np.broadcast_to(er[:, None, :], (rows, grid_w, D // 2)),
                          np.broadcast_to(ec[None, :, :], (rows, grid_w, D // 2))], axis=-1)
    return np.ascontiguousarray(emb.reshape(rows * grid_w, D), dtype=np.float32)


def prep_shared(inp, depth):
    nA, nB = (depth + 1) // 2, depth // 2
    f = lambda a: np.ascontiguousarray(a, dtype=np.float32)
    out = {}
    out["w_ada"] = f(inp["w_ada"][:depth])
    out["b_adaT"] = f(inp["b_ada"][:depth].reshape(depth, 24, 128).transpose(2, 0, 1).reshape(128, depth * 24))
    out["ln_g"] = f(inp["ln_g"][:depth])
    out["ln_b"] = f(inp["ln_b"][:depth])
    out["a_w_in"] = f(inp["a_w_in"][:nA])
    out["a_ln_gT"] = f(inp["a_ln_g"][:nA].reshape(nA, 16, 128).transpose(2, 0, 1).reshape(128, nA * 16))
    out["a_ln_bT"] = f(inp["a_ln_b"][:nA].reshape(nA, 16, 128).transpose(2, 0, 1).reshape(128, nA * 16))
    out["a_w_sT"] = f(inp["a_w_s"][:nA].transpose(0, 3, 1, 2).reshape(nA, 128, 8 * 128))
    out["a_b_s"] = f(inp["a_b_s"][:nA].reshape(nA, 8 * 128))
    out["a_w_out"] = f(inp["a_w_out"][:nA])
    nb = max(nB, 1)
    out["b_w_in"] = f(inp["b_w_in"][:nb])
    out["b_convT"] = f(inp["b_conv_w"][:nb].reshape(nb, 5, 32, 128).transpose(3, 0, 2, 1).reshape(128, nb * 32 * 5))
    al = np.zeros((nb, 128, 1), np.float32)
    db = np.zeros((nb, 128, 1), np.float32)
    for j in range(nb):
        for d_ in range(2):
            al[j, d_ * 32:d_ * 32 + 8, 0] = inp["b_A_log"][j, d_]
            db[j, d_ * 32:d_ * 32 + 8, 0] = inp["b_dt_bias"][j, d_]
    out["b_alogP"] = al
    out["b_dtbP"] = db
    out["b_norm_g"] = f(inp["b_norm_g"][:nb])
    pm = np.zeros((128, 4), np.float32)
    pm[0:32, 0] = 1.0
    pm[32:64, 1] = 1.0
    pm[64:128, 2] = 1.0
    out["pmask"] = pm
    ii = np.arange(128)
    xr = ii[:, None] ^ ii[None, :]
    lv = np.full((128, 128), -1.0, np.float32)
    nz = xr > 0
    lv[nz] = np.floor(np.log2(xr[nz])).astype(np.float32)
    out["lvs"] = lv
    out["b_w_out"] = f(inp["b_w_out"][:nb])
    return out


def prep_core(inp, shared, core, NP, LP, LS, depth, n_per_group):
    nb = max(depth // 2, 1)
    b = core // n_per_group
    f = lambda a: np.ascontiguousarray(a, dtype=np.float32)
    m = dict(shared)
    m["xp"] = f(inp["x_prompt"][core * NP:(core + 1) * NP].reshape(NP * LP, D))
    m["xs"] = f(inp["x_sample"][b])
    m["pos"] = _pos_table(LS)
    cond2 = np.stack([inp["c_ctx"], inp["c"][b]], axis=0)
    m["condT"] = f(cond2.reshape(2, 8, 128).transpose(2, 1, 0).reshape(128, 16))
    m["s0"] = f(inp["state_delta"][b][:nb])
    return m


_CACHE = {}


def kernel(**inputs):
    inp = {k: np.asarray(v) for k, v in inputs.items()}
    NP, LP, LS, depth = 4, 256, 4096, 4
    key = (NP, LP, LS, depth)
    if key not in _CACHE:
        _CACHE[key] = build_program(NP, LP, LS, depth)
    nc = _CACHE[key]
    shared = prep_shared(inp, depth)
    in_maps = [prep_core(inp, shared, c, NP, LP, LS, depth, 4) for c in range(8)]
    res = run_bass_kernel_spmd(nc, in_maps, core_ids=list(range(8)))
    r = res.results
    y_prompt = np.concatenate([r[c]["yp"].reshape(NP, LP, D) for c in range(8)], axis=0)
    y_sample = np.stack([r[0]["ys"], r[4]["ys"]], axis=0)
    ns = np.concatenate([r[c]["ns"] for c in range(8)], axis=0)
    return (y_prompt.astype(np.float32), y_sample.astype(np.float32), ns.astype(np.float32))
```

```python
import numpy as np
from contextlib import ExitStack
import concourse.bass as bass
import concourse.mybir as mybir
from concourse.bass_utils import run_bass_kernel_spmd

F32 = mybir.dt.float32
BF16 = mybir.dt.bfloat16
AF = mybir.ActivationFunctionType
ALU = mybir.AluOpType

D = 1024
E = 2048
KW = 1024
DK = 128
DV = 256
NH = 8
BW = 6176
ALPHA = (2.0 * 4) ** 0.25
EPS = 1e-6
NEG = -1.0e30


class Buf:
    __slots__ = ("name", "w", "r", "x")

    def __init__(self, name="", x=False):
        self.name = name
        self.w = None
        self.r = {}
        self.x = x


class Prog:
    ENGS = ("pe", "dve", "act", "pool", "sp")
    NSLOT = 8
    SAME_ENG_SKIP = 12

    def __init__(self):
        self.ops = {e: [] for e in self.ENGS}
        self.cnt = {e: 0 for e in self.ENGS}
        self.waited = {e: {} for e in self.ENGS}
        self.dcnt = {e: 0 for e in self.ENGS}

    def _deps(self, eng, reads, writes):
        deps = {}

        def add(p):
            if p is None:
                return
            k = p[0]
            if k not in deps or deps[k][1] < p[1]:
                deps[k] = p
        for b in reads:
            add(b.w)
        for b in writes:
            add(b.w)
            for p in b.r.values():
                add(p)
        waits = []
        for k, p in deps.items():
            val = p[1]
            if k == eng and self.cnt[eng] - p[3] > self.SAME_ENG_SKIP:
                continue
            if self.waited[eng].get(k, 0) >= val:
                continue
            self.waited[eng][k] = val
            waits.append((k, val))
        return waits

    def _record(self, reads, writes, prod):
        k = prod[0]
        for b in reads:
            if k not in b.r or b.r[k][1] < prod[1]:
                b.r[k] = prod
        for b in writes:
            b.w = prod
            b.r = {}

    def op(self, eng, fn, reads=(), writes=()):
        xs = [b for b in reads if b.x]
        if xs:
            writes = list(writes) + xs
        waits = self._deps(eng, reads, writes)
        idx = self.cnt[eng]
        self.cnt[eng] = idx + 1
        self.ops[eng].append((waits, fn, (eng, 1)))
        self._record(reads, writes, (eng, idx + 1, eng, idx))

    def dma(self, q, fn, reads=(), writes=()):
        waits = self._deps(q, reads, writes)
        i = self.dcnt[q]
        self.dcnt[q] = i + 1
        k = ("dma", q, i % self.NSLOT)
        prev = 16 * (i // self.NSLOT)
        if prev > 0 and self.waited[q].get(k, 0) < prev:
            self.waited[q][k] = prev
            waits.append((k, prev))
        self.ops[q].append((waits, fn, (k, 16)))
        self._record(reads, writes, (k, prev + 16, None, None))

    def barrier(self):
        tgt = [(e, self.cnt[e]) for e in self.ENGS if self.cnt[e] > 0]
        for q in self.ENGS:
            n = self.dcnt[q]
            for slot in range(min(n, self.NSLOT)):
                tgt.append((("dma", q, slot), 16 * ((n - 1 - slot) // self.NSLOT + 1)))
        for e in self.ENGS:
            waits = []
            for k, v in tgt:
                if k == e:
                    continue
                if self.waited[e].get(k, 0) >= v:
                    continue
                self.waited[e][k] = v
                waits.append((k, v))
            if waits:
                self.ops[e].append((waits, None, None))

    def replay(self, nc):
        semkeys = list(self.ENGS)
        for q in self.ENGS:
            for s in range(min(self.dcnt[q], self.NSLOT)):
                semkeys.append(("dma", q, s))
        with ExitStack() as st:
            sems = {}
            for k in semkeys:
                nm = k if isinstance(k, str) else f"d_{k[1]}_{k[2]}"
                sems[k] = st.enter_context(nc.semaphore("s_" + nm))
            block = st.enter_context(nc.Block())
            engmap = {"pe": "tensor", "dve": "vector", "act": "scalar", "pool": "gpsimd", "sp": "sync"}

            def mk(ename):
                oplist = self.ops[ename]

                def body(e):
                    for waits, fn, inc in oplist:
                        for k, v in waits:
                            e.wait_ge(sems[k], v)
                        if fn is not None:
                            fn(e).then_inc(sems[inc[0]], inc[1])
                return body

            for ename in self.ENGS:
                if self.ops[ename]:
                    getattr(block, engmap[ename])(mk(ename))


class Builder:
    def __init__(self, NP, LP, LS, depth=4, dbg=False):
        self.NP, self.LP, self.LS, self.depth = NP, LP, LS, depth
        self.nc = nc = bass.Bass("TRN2", target_bir_lowering=False)
        self.P = Prog()
        self.st = ExitStack()
        self.groups = [dict(n_seq=NP, L=LP, cond=0, ntok=NP * LP), dict(n_seq=1, L=LS, cond=1, ntok=LS)]
        nA = (depth + 1) // 2
        nB = depth // 2
        self.nA, self.nB = nA, nB

        def din(name, shape, dt=F32):
            return nc.dram_tensor(name, list(shape), dt, kind="ExternalInput").ap()

        def dout(name, shape, dt=F32):
            return nc.dram_tensor(name, list(shape), dt, kind="ExternalOutput").ap()

        def dint(name, shape, dt=F32):
            return nc.dram_tensor(name, list(shape), dt, kind="Internal").ap()

        self.xin = [din("xp", [NP * LP, D]), din("xs", [LS, D])]
        self.pos = din("pos", [LS, D])
        self.condT = din("condT", [128, 16])
        self.s0 = din("s0", [max(nB, 1), 2, NH, DK, DV])
        self.w_ada = din("w_ada", [depth, D, 3 * D])
        self.b_adaT = din("b_adaT", [128, depth * 24])
        self.ln_g = din("ln_g", [depth, D])
        self.ln_b = din("ln_b", [depth, D])
        self.a_w_in = din("a_w_in", [nA, D, 3 * E])
        self.a_ln_gT = din("a_ln_gT", [128, nA * 16])
        self.a_ln_bT = din("a_ln_bT", [128, nA * 16])
        self.a_w_sT = din("a_w_sT", [nA, 128, 8 * 128])
        self.a_b_s = din("a_b_s", [nA, 8 * 128])
        self.a_w_out = din("a_w_out", [nA, E, D])
        self.b_w_in = din("b_w_in", [max(nB, 1), D, BW])
        self.b_convT = din("b_convT", [128, max(nB, 1) * 32 * 5])
        self.b_alogP = din("b_alogP", [max(nB, 1), 128, 1])
        self.b_dtbP = din("b_dtbP", [max(nB, 1), 128, 1])
        self.b_norm_g = din("b_norm_g", [max(nB, 1), DV])
        self.pmask_d = din("pmask", [128, 4])
        self.lvs_d = din("lvs", [128, 128])
        self.b_w_out = din("b_w_out", [max(nB, 1), E, D])
        self.yout = [dout("yp", [NP * LP, D]), dout("ys", [LS, D])]
        self.ns = dout("ns", [NP, max(nB, 1), 2, NH, DK, DV])
        self.X = [dint("X0", [NP * LP, D]), dint("X1", [LS, D])]
        self.wa_in = dint("wa_in", [nA, D, 3 * E], BF16)
        self.wa_out = dint("wa_out", [nA, E, D], BF16)
        self.wb_in = dint("wb_in", [max(nB, 1), D, BW], BF16)
        self.wb_out = dint("wb_out", [max(nB, 1), E, D], BF16)
        self.HTd = [dint("HTd0", [NP * LP // 256, 128, 2048], BF16), dint("HTd1", [LS // 256, 128, 2048], BF16)]
        self.YTd = [dint("YTd0", [16, 128, NP * LP], BF16), dint("YTd1", [16, 128, LS], BF16)]
        self.dbg = None
        if dbg:
            self.dbg = dout("dbg", [128, 4096])
        self.Bw = {}
        self.BX = [[Buf(f"X{g}_{t}") for t in range(self.groups[g]["ntok"] // 128)] for g in range(2)]
        self.BHT = [[Buf() for _ in range(self.groups[g]["ntok"] // 128)] for g in range(2)]
        self.BYT = [[[Buf() for _ in range(self.groups[g]["ntok"] // 128)] for _h in range(NH)] for g in range(2)]
        self.BNS = Buf("ns")
        self.BYO = Buf("yout")
        self._ps_i = 0

    def sb(self, name, shape, dt):
        return self.st.enter_context(self.nc.sbuf_tensor(name, list(shape), dt))

    def MM(self, out, lhsT, rhs, start, stop, R, W):
        self.P.op("pe", lambda e: e.matmul(out, lhsT=lhsT, rhs=rhs, start=start, stop=stop), R, W)

    def TR(self, out, in_, ident, R, W):
        self.P.op("pe", lambda e: e.transpose(out, in_, ident), R, W)

    def ACT(self, out, in_, func, R, W, bias=None, scale=None, accum=None):
        kw = {}
        if bias is not None:
            kw["bias"] = bias
        if scale is not None:
            kw["scale"] = scale
        if accum is not None:
            kw["accum_out"] = accum
        self.P.op("act", lambda e: e.activation(out=out, in_=in_, func=func, **kw), R, W)

    def TS(self, eng, out, in0, s1, s2, op0, op1, R, W):
        if op1 is None:
            self.P.op(eng, lambda e: e.tensor_scalar(out=out, in0=in0, scalar1=s1, scalar2=None, op0=op0), R, W)
        else:
            self.P.op(eng, lambda e: e.tensor_scalar(out=out, in0=in0, scalar1=s1, scalar2=s2, op0=op0, op1=op1), R, W)

    def STT(self, out, in0, scalar, in1, op0, op1, R, W):
        self.P.op("dve", lambda e: e.scalar_tensor_tensor(out=out, in0=in0, scalar=scalar, in1=in1, op0=op0, op1=op1), R, W)

    def TT(self, eng, out, in0, in1, op, R, W):
        self.P.op(eng, lambda e: e.tensor_tensor(out=out, in0=in0, in1=in1, op=op), R, W)

    def CP(self, eng, out, in_, R, W):
        if eng == "act":
            self.ACT(out, in_, AF.Copy, R, W)
        else:
            self.P.op(eng, lambda e: e.tensor_copy(out=out, in_=in_), R, W)

    def MSET(self, eng, ap, val, W):
        self.P.op(eng, lambda e: e.memset(ap, val), (), W)

    def DMA(self, q, out, in_, R, W):
        self.P.dma(q, lambda e: e.dma_start(out=out, in_=in_), R, W)

    def dump(self, src, col0, n, R):
        if self.dbg is None:
            return
        if not hasattr(self, "dbgst"):
            self.dbgst = self.sb("dbgst", [128, 512], F32)
            self.Bdbg = Buf("dbg")
        self.CP("act", self.dbgst[:, 0:n], src, R, [self.Bdbg])
        self.DMA("sp", self.dbg[:, col0:col0 + n], self.dbgst[:, 0:n], [self.Bdbg], [])

    def ps(self):
        i = self._ps_i
        self._ps_i = (i + 1) % len(self.psb)
        return self.psb[i], self.Bps[i]

    def build(self):
        nc = self.nc
        sb = self.sb
        self.psb = [self.st.enter_context(nc.psum_tensor(f"ps{i}", [128, 512], F32)) for i in range(8)]
        self.Bps = [Buf(f"ps{i}", x=True) for i in range(8)]
        self.ident = sb("ident", [128, 128], F32)
        self.identb = sb("identb", [128, 128], BF16)
        self.onesf = sb("onesf", [128, 128], F32)
        self.onesb = sb("onesb", [128, 128], BF16)
        self.Bc = Buf("consts")
        self.MSET("pool", self.onesf[:], 1.0, [self.Bc])
        self.P.op("pool", lambda e: e.affine_select(out=self.ident[:], in_=self.onesf[:], pattern=[[-1, 128]],
                                                     compare_op=ALU.is_equal, fill=0.0, base=0, channel_multiplier=1),
                  [self.Bc], [self.Bc])
        self.CP("dve", self.identb[:], self.ident[:], [self.Bc], [self.Bc])
        self.CP("dve", self.onesb[:], self.onesf[:], [self.Bc], [self.Bc])
        self.slabf = sb("slabf", [128, 12 * 1024], F32)
        self.slabb = sb("slabb", [128, 49 * 1024], BF16)
        self.sb_wo = sb("WO", [128, 16, 1024], BF16)
        self.MOD = sb("MOD", [128, self.depth * 48], F32)
        self.SC1 = sb("SC1", [128, self.depth * 16], F32)
        self.GATE = sb("GATE", [128, 2 * D], F32)
        self.LNG = sb("LNG", [128, D], F32)
        self.LNB = sb("LNB", [128, D], F32)
        self.BGATE = Buf("gate")
        self.BLN = Buf("ln")
        self.BMOD = Buf("mod")

        import os
        stage = int(os.environ.get("KSTAGE", "99"))
        if stage >= 1:
            self.weight_casts()
        self.small_consts()
        if stage >= 2:
            self.prologue()
            self.P.barrier()
        if stage >= 3:
            self.adaln()
        for i in range(self.depth):
            if stage < 4:
                break
            self.P.barrier()
            self.layer_consts(i)
            self.P.barrier()
            if stage < 5:
                break
            if i % 2 == 0:
                self.layer_a(i)
            else:
                self.layer_b(i)
        self.P.barrier()
        self.P.replay(nc)
        self.st.close()
        return nc

    def weight_casts(self):
        for nm, src, dst, n, rows in (("a_in", self.a_w_in, self.wa_in, self.nA, D), ("a_out", self.a_w_out, self.wa_out, self.nA, E),
                                      ("b_in", self.b_w_in, self.wb_in, self.nB, D), ("b_out", self.b_w_out, self.wb_out, self.nB, E)):
            for l in range(n):
                b = Buf(f"w_{nm}{l}")
                self.Bw[(nm, l)] = b
                for r0 in range(0, rows, 256):
                    self.DMA("pool", dst[l, r0:r0 + 256, :], src[l, r0:r0 + 256, :], [], [b])

    def adaln(self):
        P = self.P
        sf = self.slabf
        cT = sf[:, 0:16]
        sT = sf[:, 16:32]
        bT = sf[:, 32:32 + self.depth * 24]
        W = sf[:, 1024:1024 + 8192].rearrange("p (k n) -> p k n", n=1024)
        Bs = Buf("ada_s")
        Bb = Buf("ada_b")
        BW_ = Buf("adaw")
        self.DMA("sp", cT, self.condT[:, :], [], [Bs])
        self.DMA("sp", bT, self.b_adaT[:, :], [], [Bb])
        self.ACT(sT, cT, AF.Silu, [Bs], [Bs])
        for i in range(self.depth):
            pt, bp = self.ps()
            for part in range(3):
                self.DMA("sp", W, self.w_ada[i, :, part * 1024:(part + 1) * 1024].rearrange("(k p) n -> p k n", p=128), [], [BW_])
                for c8 in range(8):
                    ch = part * 8 + c8
                    for kc in range(8):
                        self.MM(pt[:, ch * 2:ch * 2 + 2], W[:, kc, c8 * 128:(c8 + 1) * 128], sT[:, kc * 2:kc * 2 + 2],
                                kc == 0, kc == 7, [BW_, Bs], [bp])
            mod = self.MOD[:, i * 48:(i + 1) * 48].rearrange("p (ch c) -> p ch c", c=2)
            self.TT("dve", mod, pt[:, 0:48].rearrange("p (ch c) -> p ch c", c=2),
                    bT[:, i * 24:(i + 1) * 24].unsqueeze(2).broadcast_to([128, 24, 2]), ALU.add, [bp, Bb], [self.BMOD])
            self.TS("dve", self.SC1[:, i * 16:(i + 1) * 16], self.MOD[:, i * 48 + 16:i * 48 + 32], 1.0, None, ALU.add, None,
                    [self.BMOD], [self.BMOD])
        if self.dbg is not None:
            self.DMA("sp", self.dbg[:, 0:48 * self.depth], self.MOD[:, :], [self.BMOD], [])
            self.DMA("sp", self.dbg[:, 512:512 + 16 * self.depth], self.SC1[:, :], [self.BMOD], [])

    def shift_ap(self, i, kc, c):
        o = i * 48 + kc * 2 + c
        return self.MOD[:, o:o + 1]

    def scale1_ap(self, i, kc, c):
        o = i * 16 + kc * 2 + c
        return self.SC1[:, o:o + 1]

    def layer_consts(self, i):
        gb = self.slabf[:, 0:128]
        Bg = Buf("gb")
        for c in range(2):
            for half in range(2):
                pt, bp = self.ps()
                for q in range(4):
                    kc = half * 4 + q
                    o = i * 48 + (16 + kc) * 2 + c
                    self.CP("dve", gb, self.MOD[:, o:o + 1].broadcast_to([128, 128]), [self.BMOD], [Bg])
                    self.MM(pt[:, q * 128:(q + 1) * 128], gb, self.ident[:], True, True, [Bg, self.Bc], [bp])
                self.CP("act", self.GATE[:, c * D + half * 512:c * D + (half + 1) * 512], pt[:, :], [bp], [self.BGATE])
        self.DMA("sp", self.LNG[:], self.ln_g[i:i + 1, :].partition_broadcast(128), [], [self.BLN])
        self.DMA("sp", self.LNB[:], self.ln_b[i:i + 1, :].partition_broadcast(128), [], [self.BLN])

    def prologue(self):
        sf = self.slabf
        A = sf[:, 0:4096].rearrange("p (s d) -> p s d", d=1024)
        Bq = sf[:, 4096:8192].rearrange("p (s d) -> p s d", d=1024)
        Ba, Bb = Buf("pa"), Buf("pb")
        for t4 in range(self.LS // 512):
            rows = slice(t4 * 512, (t4 + 1) * 512)
            self.DMA("sp", A, self.xin[1][rows, :].rearrange("(s p) d -> p s d", p=128), [], [Ba])
            self.DMA("sp", Bq, self.pos[rows, :].rearrange("(s p) d -> p s d", p=128), [], [Bb])
            self.TT("dve", A, A, Bq, ALU.add, [Ba, Bb], [Ba])
            self.DMA("pool", self.X[1][rows, :].rearrange("(s p) d -> p s d", p=128), A, [Ba],
                     [self.BX[1][t4 * 4 + s] for s in range(4)])

    def load_x_tile(self, i, g, t, XT, Bx, nsub):
        first = (i == 0 and g == 0)
        src = self.xin[g] if first else self.X[g]
        rows = slice(t * 128, (t + nsub) * 128)
        R = [] if first else [self.BX[g][t + s] for s in range(nsub)]
        self.DMA("sp", XT, src[rows, :].rearrange("(s p) d -> p s d", p=128), R, [Bx])

    def make_ht(self, i, c, XT, Bx, nsub, HT, Bh):
        for kc in range(8):
            pt, bp = self.ps()
            for s in range(nsub):
                self.TR(pt[:, s * 128:(s + 1) * 128], XT[:, s, kc * 128:(kc + 1) * 128], self.ident[:], [Bx, self.Bc], [bp])
            self.TS("dve", HT[:, kc, 0:nsub * 128], pt[:, 0:nsub * 128], self.scale1_ap(i, kc, c), self.shift_ap(i, kc, c),
                    ALU.mult, ALU.add, [bp, self.BMOD], [Bh])

    def epilogue(self, i, g, c, t, pts, xrow, Bx, R1, XN, Br, last):
        import os
        kep = int(os.environ.get("KEP", "99"))
        if kep < 1:
            return
        for half in range(2):
            pt, bp = pts[half]
            hs = slice(half * 512, (half + 1) * 512)
            self.TT("dve", R1[:, hs], pt[:, :], self.GATE[:, c * D + half * 512:c * D + (half + 1) * 512], ALU.mult,
                    [bp, self.BGATE], [Br])
        self.STT(R1, xrow, ALPHA, R1, ALU.mult, ALU.add, [Bx, Br], [Br])
        if kep < 2:
            return
        st6 = self.stat[:, 0:12]
        mv = self.stat[:, 12:14]
        rs = self.stat[:, 14:15]
        for half in range(2):
            self.P.op("dve", (lambda o, a: lambda e: e.bn_stats(out=o, in_=a))(st6[:, half * 6:(half + 1) * 6], R1[:, half * 512:(half + 1) * 512]),
                      [Br], [self.Bstat])
        self.P.op("dve", lambda e: e.bn_aggr(out=mv, in_=st6), [self.Bstat], [self.Bstat])
        self.ACT(rs, mv[:, 1:2], AF.Sqrt, [self.Bstat], [self.Bstat], bias=self.epsc[:, 0:1])
        self.P.op("dve", lambda e: e.reciprocal(out=rs, in_=rs), [self.Bstat], [self.Bstat])
        if kep < 3:
            return
        self.TS("dve", XN, R1, mv[:, 0:1], rs, ALU.subtract, ALU.mult, [Br, self.Bstat], [Br])
        if kep < 4:
            return
        self.TT("pool", XN, XN, self.LNG[:], ALU.mult, [Br, self.BLN], [Br])
        self.TT("pool", XN, XN, self.LNB[:], ALU.add, [Br, self.BLN], [Br])
        if kep < 5:
            return
        if last:
            self.DMA("sp", self.yout[g][t * 128:(t + 1) * 128, :], XN, [Br], [self.BYO])
        else:
            self.DMA("sp", self.X[g][t * 128:(t + 1) * 128, :], XN, [Br], [self.BX[g][t]])

    def small_consts(self):
        if hasattr(self, "stat"):
            return
        self.stat = self.sb("stat", [128, 16], F32)
        self.Bstat = Buf("stat")
        self.epsc = self.sb("epsc", [128, 2], F32)
        self.MSET("dve", self.epsc[:, 0:1], EPS, [self.Bc])
        self.MSET("dve", self.epsc[:, 1:2], 1.0, [self.Bc])

    def layer_a(self, i):
        self.small_consts()
        l = i // 2
        last = (i == self.depth - 1)
        sf, sbb = self.slabf, self.slabb
        XT = sf[:, 0:4096].rearrange("p (s d) -> p s d", d=1024)
        BIAS = sf[:, 4096:6144].rearrange("p (c q) -> p c q", q=128)
        R1 = sf[:, 6144:8192].rearrange("p (b n) -> p b n", n=1024)
        T1 = sf[:, 8192:9216].rearrange("p (b n) -> p b n", n=512)
        WSF = sf[:, 9216:10240]
        BSB = sf[:, 10240:11264].rearrange("p (g q) -> p g q", q=128)
        RS = sf[:, 11264:11392]
        GLN = sf[:, 11392:11408]
        BLNv = sf[:, 11408:11424]
        vst = sf[:, 11424:11424 + 64]
        HT = sbb[:, 0:4096].rearrange("p (k n) -> p k n", n=512)
        U = sbb[:, 4096:12288].rearrange("p (c n) -> p c n", n=512)
        Z = sbb[:, 12288:20480].rearrange("p (c n) -> p c n", n=512)
        VGa = sbb[:, 20480:28672].rearrange("p (b n) -> p b n", n=2048)
        VHa = sbb[:, 28672:36864].rearrange("p (b n) -> p b n", n=2048)
        WB = sbb[:, 36864:49152].rearrange("p (b k n) -> p b k n", k=8, n=512)
        WST = self.wst[:].rearrange("p (g q) -> p g q", q=128)
        WO = self.sb_wo[:]
        self.VGa, self.VHa = VGa, VHa
        Bxt, Bht, Bwst, Bbias, Bwo = Buf("XT"), Buf("HT"), Buf("WST"), Buf("BIAS"), Buf("WO")
        BU = [Buf() for _ in range(16)]
        BZ = [Buf() for _ in range(16)]
        BWB = [Buf(), Buf(), Buf()]
        BR = [Buf(), Buf()]
        BT1 = [Buf(), Buf()]
        Bsm = Buf("small")
        self.DMA("sp", WSF, self.a_w_sT[l, :, :], [], [Bwst])
        self.CP("dve", WST.rearrange("p g q -> p (g q)"), WSF, [Bwst], [Bwst])
        self.DMA("sp", GLN, self.a_ln_gT[:, l * 16:(l + 1) * 16], [], [Bsm])
        self.DMA("sp", BLNv, self.a_ln_bT[:, l * 16:(l + 1) * 16], [], [Bsm])
        self.DMA("sp", BSB.rearrange("p g q -> p (g q)"), self.a_b_s[l:l + 1, :].partition_broadcast(128), [], [Bsm])
        for gq in range(8):
            pt, bp = self.ps()
            self.MM(pt[:, 0:128], self.onesb[:], WST[:, gq, :], True, True, [self.Bc, Bwst], [bp])
            self.CP("act", RS, pt[:, 0:128], [bp], [Bsm])
            for cc in range(2):
                ch = gq * 2 + cc
                self.STT(BIAS[:, ch, :], RS, BLNv[:, ch:ch + 1], BSB[:, gq, :], ALU.mult, ALU.add, [Bsm], [Bbias])
        import os
        sub = int(os.environ.get("KSUB", "99"))
        if sub < 1:
            return
        self.DMA("sp", WO, self.wa_out[l].rearrange("(k p) n -> p k n", p=128), [self.Bw[("a_out", l)]], [Bwo])
        wsrc = self.wa_in[l]
        wcnt = [0]

        def load_w(col0):
            b = wcnt[0] % 3
            wcnt[0] += 1
            self.DMA("sp", WB[:, b, :, :], wsrc[:, col0:col0 + 512].rearrange("(k p) n -> p k n", p=128),
                     [self.Bw[("a_in", l)]], [BWB[b]])
            return b

        for g, G in enumerate(self.groups):
            c = G["cond"]
            for t4 in range(G["ntok"] // 512):
                t = t4 * 4
                self.load_x_tile(i, g, t, XT, Bxt, 4)
                self.make_ht(i, c, XT, Bxt, 4, HT, Bht)
                if sub < 2:
                    continue
                blocks = [("v", 2048 + b * 512, b) for b in range(4)] + [("u", b * 512, b) for b in range(4)] + \
                         [("z", 4096 + b * 512, b) for b in range(4)]
                nxt = load_w(blocks[0][1])
                for bi, (kind, col0, b4) in enumerate(blocks):
                    wb = nxt
                    if bi + 1 < len(blocks):
                        nxt = load_w(blocks[bi + 1][1])
                    if kind == "v":
                        for s in range(4):
                            pt, bp = self.ps()
                            for kc in range(8):
                                self.MM(pt[:, :], HT[:, kc, s * 128:(s + 1) * 128], WB[:, wb, kc, :], kc == 0, kc == 7,
                                        [Bht, BWB[wb]], [bp])
                            self.ACT(self._vg(sf, s)[:, b4 * 512:(b4 + 1) * 512],
                                     pt[:, :], AF.Gelu_apprx_tanh, [bp], [self.BVG4[s]])
                    else:
                        dst, Bd, fn = (U, BU, AF.Gelu_apprx_tanh) if kind == "u" else (Z, BZ, AF.Silu)
                        for cc in range(4):
                            ch = b4 * 4 + cc
                            pt, bp = self.ps()
                            for kc in range(8):
                                self.MM(pt[:, :], WB[:, wb, kc, cc * 128:(cc + 1) * 128], HT[:, kc, :], kc == 0, kc == 7,
                                        [Bht, BWB[wb]], [bp])
                            self.ACT(dst[:, ch, :], pt[:, :], fn, [bp], [Bd[ch]])
                    if kind == "v" and b4 == 3:
                        for s in range(4):
                            vg = self._vg(sf, s)
                            st = vst[:, 0:24]
                            mv = vst[:, 24:26]
                            rs = vst[:, 26:27]
                            for q in range(4):
                                self.P.op("dve", (lambda o, a: lambda e: e.bn_stats(out=o, in_=a))(st[:, q * 6:(q + 1) * 6], vg[:, q * 512:(q + 1) * 512]),
                                          [self.BVG4[s]], [Bsm])
                            self.P.op("dve", lambda e: e.bn_aggr(out=mv, in_=st), [Bsm], [Bsm])
                            self.ACT(rs, mv[:, 1:2], AF.Sqrt, [Bsm], [Bsm], bias=self.epsc[:, 0:1])
                            self.P.op("dve", lambda e: e.reciprocal(out=rs, in_=rs), [Bsm], [Bsm])
                            self.TS("dve", self._vh(sbb, s), vg, mv[:, 0:1], rs, ALU.subtract, ALU.mult, [self.BVG4[s], Bsm], [self.BVH4[s]])
                if sub < 3:
                    continue
                for ch in range(16):
                    self.TT("pool", U[:, ch, :], U[:, ch, :], Z[:, ch, :], ALU.mult, [BU[ch], BZ[ch]], [BU[ch]])
                for ch in range(16):
                    pt, bp = self.ps()
                    for s in range(4):
                        self.MM(pt[:, s * 128:(s + 1) * 128], self._vh(sbb, s)[:, ch * 128:(ch + 1) * 128], WST[:, ch // 2, :], True, True,
                                [self.BVH4[s], Bwst], [bp])
                    tb = ch % 2
                    self.STT(T1[:, tb, :].rearrange("p (s q) -> p s q", q=128), pt[:, :].rearrange("p (s q) -> p s q", q=128),
                             GLN[:, ch:ch + 1], BIAS[:, ch, :].unsqueeze(1).broadcast_to([128, 4, 128]), ALU.mult, ALU.add,
                             [bp, Bsm, Bbias], [BT1[tb]])
                    self.TT("pool", Z[:, ch, :], T1[:, tb, :], U[:, ch, :], ALU.mult, [BT1[tb], BU[ch]], [BZ[ch]])
                if sub < 4:
                    continue
                for s in range(4):
                    pts = []
                    for half in range(2):
                        pt, bp = self.ps()
                        for ec in range(16):
                            self.MM(pt[:, :], Z[:, ec, s * 128:(s + 1) * 128], WO[:, ec, half * 512:(half + 1) * 512], ec == 0, ec == 15,
                                    [BZ[ec], Bwo], [bp])
                        pts.append((pt, bp))
                    rb = s % 2
                    self.epilogue(i, g, c, t + s, pts, XT[:, s, :], Bxt, R1[:, rb, :], R1[:, rb, :], BR[rb], last)

    def _vg(self, sf, s):
        return self.VGa[:, s, :]

    def _vh(self, sbb, s):
        return self.VHa[:, s, :]

    def layer_b_consts(self, j):
        if not hasattr(self, "negcat"):
            self.negcat = self.sb("negcat", [128, 2, 256], F32)
            self.cw = self.sb("cw", [128, 160], F32)
            self.ngt = self.sb("ngt", [128, 256], F32)
            self.gsc = self.sb("gsc", [128, 4], F32)
            self.pmask = self.sb("pmaskt", [128, 4], F32)
            self.lvs = self.sb("lvst", [128, 128], F32)
            self.Bbc = Buf("bconst")
            zer = self.slabf[:, 0:128]
            Bz = Buf("zer")
            self.MSET("pool", zer, 0.0, [Bz])
            for d_, (pat, cm) in enumerate((([[1, 128]], -1), ([[-1, 128]], 1))):
                for kk, cmp in enumerate((ALU.is_ge, ALU.is_gt)):
                    self.P.op("pool", (lambda o, pat, cm, cmp: lambda e: e.affine_select(
                        out=o, in_=zer, pattern=pat, compare_op=cmp, fill=NEG, base=0, channel_multiplier=cm))(
                        self.negcat[:, d_, kk * 128:(kk + 1) * 128], pat, cm, cmp), [Bz], [self.Bbc])
        self.DMA("sp", self.pmask[:], self.pmask_d[:, :], [], [self.Bbc])
        self.DMA("sp", self.lvs[:], self.lvs_d[:, :], [], [self.Bbc])
        self.DMA("sp", self.cw[:], self.b_convT[:, j * 160:(j + 1) * 160], [], [self.Bbc])
        self.DMA("sp", self.ngt[:], self.b_norm_g[j:j + 1, :].partition_broadcast(128), [], [self.Bbc])
        self.DMA("sp", self.gsc[:, 0:1], self.b_dtbP[j], [], [self.Bbc])
        self.DMA("sp", self.gsc[:, 2:3], self.b_alogP[j], [], [self.Bbc])
        self.ACT(self.gsc[:, 3:4], self.gsc[:, 2:3], AF.Exp, [self.Bbc], [self.Bbc])
        self.TS("dve", self.gsc[:, 1:2], self.gsc[:, 3:4], -1.0, None, ALU.mult, None, [self.Bbc], [self.Bbc])

    def layer_b(self, i):
        j = i // 2
        last = (i == self.depth - 1)
        sf, sbb = self.slabf, self.slabb
        wor = self.sb_wo[:].rearrange("p a b -> p (a b)")
        self.layer_b_consts(j)
        ident, identb = self.ident, self.identb
        FEAT = sf[:, 0:4096]
        Bfeat = Buf("feat")
        WABF = sf[:, 9728:10752].rearrange("p (k n) -> p k n", n=128)
        WAB = sbb[:, 47616:48640].rearrange("p (k n) -> p k n", n=128)
        Bwab = Buf("wab")
        self.MSET("dve", WABF, 0.0, [Bwab])
        for q4, c0 in enumerate((0, 32, 64, 96)):
            self.DMA("sp", WABF[:, :, c0:c0 + 8],
                     self.b_w_in[j, :, 6144 + q4 * 8:6144 + (q4 + 1) * 8].rearrange("(k p) n -> p k n", p=128), [], [Bwab])
        self.CP("dve", WAB, WABF, [Bwab], [Bwab])
        import os
        kb = int(os.environ.get("KB", "99"))
        self.kb = kb
        for g, G in enumerate(self.groups):
            self.P.barrier()
            if kb >= 1:
                self.b_phase0(i, j, g, G, FEAT, Bfeat, WAB, Bwab)
            self.P.barrier()
            if kb >= 2:
                self.b_heads(i, j, g, G, FEAT, Bfeat, wor)
            self.P.barrier()
            if kb >= 5:
                self.b_outproj(i, j, g, G, last)

    def b_phase0(self, i, j, g, G, FEAT, Bfeat, WAB, Bwab):
        sf, sbb = self.slabf, self.slabb
        c = G["cond"]
        XT = sf[:, 4096:8192].rearrange("p (s d) -> p s d", d=1024)
        T1 = sf[:, 8192:8704]
        T2 = sf[:, 8704:9216]
        T3 = sf[:, 9216:9728]
        MASK = sf[:, 10752:11264]
        HTb = sbb[:, 0:8192].rearrange("p (b k n) -> p b k n", k=8, n=512)
        Bxt, Bt = Buf("bxt"), Buf("bt")
        Bh = [Buf(), Buf()]
        Bm = Buf("mask")
        self.MSET("dve", MASK, 1.0, [Bm])
        self.MSET("dve", MASK.rearrange("p (a b) -> p a b", b=128)[:, :, 0:1], 0.0, [Bm])
        import os
        kp = int(os.environ.get("KP", "99"))
        for b4 in range(G["ntok"] // 512 if kp >= 1 else 0):
            hb = b4 % 2
            cols = slice(b4 * 512, (b4 + 1) * 512)
            self.load_x_tile(i, g, b4 * 4, XT, Bxt, 4)
            self.make_ht(i, c, XT, Bxt, 4, HTb[:, hb], Bh[hb])
            for hh in range(2):
                self.DMA("sp", self.HTd[g][b4 * 2 + hh].rearrange("p (k n) -> p k n", n=256), HTb[:, hb, :, hh * 256:(hh + 1) * 256],
                         [Bh[hb]], [self.BHT[g][b4 * 4 + hh * 2], self.BHT[g][b4 * 4 + hh * 2 + 1]])
            if kp < 2:
                continue
            pt, bp = self.ps()
            for kc in range(8):
                self.MM(pt[:, :], WAB[:, kc, :], HTb[:, hb, kc, :], kc == 0, kc == 7, [Bwab, Bh[hb]], [bp])
            if kp < 3:
                continue
            self.ACT(T1[0:64, :], pt[0:64, :], AF.Exp, [bp, self.Bbc], [Bt], bias=self.gsc[0:64, 0:1])
            self.ACT(T1[0:64, :], T1[0:64, :], AF.Ln, [Bt, self.Bc], [Bt], bias=self.epsc[0:64, 1:2])
            self.TS("dve", T1[0:64, :], T1[0:64, :], self.gsc[0:64, 1:2], None, ALU.mult, None, [Bt, self.Bbc], [Bt])
            self.ACT(T1[64:128, :], pt[64:128, :], AF.Exp, [bp], [Bt], scale=-1.0)
            self.ACT(T1[64:128, :], T1[64:128, :], AF.Ln, [Bt, self.Bc], [Bt], bias=self.epsc[64:128, 1:2])
            self.TS("dve", T1[64:128, :], T1[64:128, :], -1.0, None, ALU.mult, None, [Bt], [Bt])
            if kp < 4:
                continue
            self.P.op("dve", (lambda o, m, d: lambda e: e.tensor_tensor_scan(out=o, data0=m, data1=d, initial=0.0, op0=ALU.mult, op1=ALU.add))(
                T2, MASK, T1), [Bt, Bm], [Bt])
            self.TT("dve", T3, T1, T2, ALU.subtract, [Bt], [Bt])
            self.TT("dve", T3.rearrange("p (a b) -> p a b", b=128), T3.rearrange("p (a b) -> p a b", b=128),
                    T2.rearrange("p (a b) -> p a b", b=128)[:, :, 127:128].broadcast_to([128, 4, 128]), ALU.add, [Bt], [Bt])
            pm = self.pmask
            self.TS("dve", FEAT[:, cols], T2, pm[:, 0:1], None, ALU.mult, None, [Bt, self.Bbc], [Bfeat])
            self.STT(FEAT[:, cols], T3, pm[:, 1:2], FEAT[:, cols], ALU.mult, ALU.add, [Bt, self.Bbc, Bfeat], [Bfeat])
            self.STT(FEAT[:, cols], T1, pm[:, 2:3], FEAT[:, cols], ALU.mult, ALU.add, [Bt, self.Bbc, Bfeat], [Bfeat])

    def b_heads(self, i, j, g, G, FEAT, Bfeat, wor):
        sf, sbb = self.slabf, self.slabb
        ident, identb = self.ident, self.identb
        n_seq, L, ntok = G["n_seq"], G["L"], G["ntok"]
        nt = ntok // 128
        tps = L // 128
        ACC = sf[:, 4096:4608]
        A_ = sf[:, 4608:5120]
        RN = sf[:, 5120:5632]
        EX8 = sf[:, 5632:7680].rearrange("p (b n) -> p b n", n=256)
        OS2 = sf[:, 7680:8192].rearrange("p (b n) -> p b n", n=256)
        S8 = sf[:, 8192:10240].rearrange("p (b n) -> p b n", n=256)
        Y12 = sf[:, 10240:10752].rearrange("p (b n) -> p b n", n=256)
        SCAL8 = sf[:, 10752:10816].rearrange("p (b n) -> p b n", n=8)
        SELD = sf[:, 10816:11328].rearrange("p (d k n) -> p d k n", k=2, n=128)
        fst = self.stat
        HTb = sbb[:, 0:4096].rearrange("p (b k n) -> p b k n", k=8, n=256)
        WH = sbb[:, 4096:10240].rearrange("p (k n) -> p k n", n=768)
        PRE = sbb[:, 10240:18432].rearrange("p (b n) -> p b n", n=4096)
        SQ = sbb[:, 18432:18944]
        QT = sbb[:, 18944:23040]
        KT = sbb[:, 23040:27136]
        V = sbb[:, 27136:35328].rearrange("p (t n) -> p t n", n=256)
        KTOK = sbb[:, 35328:39424].rearrange("p (t n) -> p t n", n=128)
        ZS = sbb[:, 39424:47616].rearrange("p (t n) -> p t n", n=256)
        VT = sbb[:, 48640:49152]
        YB2 = sbb[:, 49152:49664].rearrange("p (b n) -> p b n", n=256)
        YT2 = sbb[:, 49664:50176].rearrange("p (b n) -> p b n", n=256)
        O = wor[:, 0:nt * 256].rearrange("p (t n) -> p t n", n=256)
        jb = nt * 256
        NSET = 8
        SETW = 2304
        nch = n_seq * 2
        U8 = wor[:, jb:jb + 2048].rearrange("p (b n) -> p b n", n=256)
        b1 = jb + 2048
        VN2 = wor[:, b1:b1 + 512].rearrange("p (b n) -> p b n", n=256)
        VB2 = wor[:, b1 + 512:b1 + 1024].rearrange("p (b n) -> p b n", n=256)
        Sb8 = wor[:, b1 + 1024:b1 + 1024 + nch * 256].rearrange("p (b n) -> p b n", n=256)
        assert b1 + 1024 + nch * 256 <= 16384, (b1, nch)
        sets = []
        for q in range(NSET):
            b0 = q * SETW
            sets.append(dict(
                QKNT=sbb[:, b0:b0 + 256], NTt=sbb[:, b0 + 256:b0 + 384], NA=sbb[:, b0 + 128:b0 + 384],
                DC=sbb[:, b0 + 384:b0 + 896].rearrange("p (b n) -> p b n", n=256),
                MC=sbb[:, b0 + 896:b0 + 1152],
                LC=sbb[:, b0 + 1152:b0 + 1664].rearrange("p (b n) -> p b n", n=256),
                KBG=sbb[:, b0 + 1664:b0 + 1792], WT=sbb[:, b0 + 1792:b0 + 1920],
                QDT=sbb[:, b0 + 1920:b0 + 2048], KD=sbb[:, b0 + 2048:b0 + 2176], EG=sbb[:, b0 + 2176:b0 + 2304],
                qi=q, B=Buf(f"set{q}"), U=U8[:, q, :], EX=EX8[:, q, :], SCAL=SCAL8[:, q, :], BU=Buf(), BQ=Buf(), BL=Buf()))
        Bsel = Buf("sel")
        BVN = [Buf(), Buf()]
        BVB = [Buf(), Buf()]
        BS = [Buf() for _ in range(nch)]
        BO = [Buf() for _ in range(nt)]
        BOS = [Buf(), Buf()]
        BY = [Buf(), Buf()]
        Bh = [Buf(), Buf()]
        Bwh, Bacc, Bsq = Buf("wh"), Buf("acc"), Buf("sq")
        BPRE = [Buf(), Buf()]
        Bq, Bk, Bv, Bkt, Bz, Bvt = Buf("QT"), Buf("KT"), Buf("V"), Buf("KTOK"), Buf("ZS"), Buf("VT")
        Bpsh = [Buf() for _ in range(8)]
        pshc = [0]

        def psh_next():
            pt_, bp_ = self.ps()
            return pt_[:, 0:64].bitcast(BF16), bp_

        wsrc = self.wb_in[j]
        Bwsrc = self.Bw[("b_in", j)]
        cnt = [0]
        for hd in range(NH):
            for (c0, n, o) in ((hd * 128, 128, 0), (1024 + hd * 128, 128, 128), (2048 + hd * 256, 256, 256), (4096 + hd * 256, 256, 512)):
                self.DMA("sp", WH[:, :, o:o + n], wsrc[:, c0:c0 + n].rearrange("(k p) n -> p k n", p=128), [Bwsrc], [Bwh])
            for pas in range(2):
                for b2 in range(ntok // 256):
                    hb = b2 % 2
                    cols = slice(b2 * 256, (b2 + 1) * 256)
                    self.DMA("sp", HTb[:, hb], self.HTd[g][b2].rearrange("p (k n) -> p k n", n=256),
                             [self.BHT[g][b2 * 2], self.BHT[g][b2 * 2 + 1]], [Bh[hb]])
                    for cc in range(2):
                        wo = (pas * 2 + cc) * 128
                        pt, bp = self.ps()
                        for kc in range(8):
                            self.MM(pt[:, 0:256], WH[:, kc, wo:wo + 128], HTb[:, hb, kc, :], kc == 0, kc == 7, [Bwh, Bh[hb]], [bp])
                        self.CP("act", PRE[:, cc, cols], pt[:, 0:256], [bp], [BPRE[cc]])
                    if pas == 0:
                        for s2 in range(2):
                            t = b2 * 2 + s2
                            pt, bp = self.ps()
                            for kc in range(8):
                                self.MM(pt[:, 0:256], HTb[:, hb, kc, s2 * 128:(s2 + 1) * 128], WH[:, kc, 512:768], kc == 0, kc == 7,
                                        [Bwh, Bh[hb]], [bp])
                            self.ACT(ZS[:, t, :], pt[:, 0:256], AF.Silu, [bp], [Bz])
                import os
                kh = int(os.environ.get("KH", "99"))
                for cc in range(2 if kh >= 2 else 0):
                    kind = ("q", "k")[cc] if pas == 0 else "v"
                    cwi = (hd if kind == "q" else 8 + hd) if pas == 0 else 16 + hd * 2 + cc
                    wv = self.cw[:, cwi * 5:(cwi + 1) * 5]
                    for s_ in range(n_seq):
                        for a in range(0, L, 512):
                            bnd = min(a + 512, L)
                            n = bnd - a
                            base = s_ * L
                            self.TS("dve", ACC[:, 0:n], PRE[:, cc, base + a:base + bnd], wv[:, 2:3], None, ALU.mult, None,
                                    [BPRE[cc], self.Bbc], [Bacc])
                            for off in (-2, -1, 1, 2):
                                lo = max(a, -off) if off < 0 else a
                                hi = min(bnd, L - off) if off > 0 else bnd
                                if hi <= lo:
                                    continue
                                self.STT(ACC[:, lo - a:hi - a], PRE[:, cc, base + lo + off:base + hi + off], wv[:, off + 2:off + 3],
                                         ACC[:, lo - a:hi - a], ALU.mult, ALU.add, [BPRE[cc], self.Bbc, Bacc], [Bacc])
                            gcols = slice(base + a, base + bnd)
                            if kh < 3:
                                continue
                            if kind == "v":
                                self.ACT(VT[:, 0:n], ACC[:, 0:n], AF.Silu, [Bacc], [Bvt])
                                for q in range(n // 128):
                                    t = (base + a) // 128 + q
                                    ph, bph = psh_next()
                                    self.TR(ph, VT[:, q * 128:(q + 1) * 128], identb[:], [Bvt, self.Bc], [bph])
                                    self.CP("dve" if q % 2 else "pool" if False else "dve", V[:, t, cc * 128:(cc + 1) * 128], ph, [bph], [Bv])
                            elif kh >= 4:
                                self.ACT(A_[:, 0:n], ACC[:, 0:n], AF.Silu, [Bacc], [Bacc])
                                self.TT("pool", SQ[:, 0:n], A_[:, 0:n], A_[:, 0:n], ALU.mult, [Bacc], [Bsq])
                                pt, bp = self.ps()
                                self.MM(pt[:, 0:n], self.onesb[:], SQ[:, 0:n], True, True, [self.Bc, Bsq], [bp])
                                self.ACT(RN[:, 0:n], pt[:, 0:n], AF.Sqrt, [bp, self.Bc], [Bsq], bias=self.epsc[:, 0:1])
                                self.P.op("dve", (lambda o: lambda e: e.reciprocal(out=o, in_=o))(RN[:, 0:n]), [Bsq], [Bsq])
                                dst, Bd = (QT, Bq) if kind == "q" else (KT, Bk)
                                self.STT(dst[:, gcols], A_[:, 0:n], (DK ** -0.5) if kind == "q" else 1.0, RN[:, 0:n], ALU.mult, ALU.mult,
                                         [Bacc, Bsq], [Bd])
                                if kind == "k" and kh >= 5:
                                    for q in range(n // 128):
                                        t = (base + a) // 128 + q
                                        ph, bph = psh_next()
                                        self.TR(ph, KT[:, t * 128:(t + 1) * 128], identb[:], [Bk, self.Bc], [bph])
                                        self.CP("dve", KTOK[:, t, :], ph, [bph], [Bkt])
            if g == 0 and hd == 0:
                self.dump(FEAT[:, 0:512], 0, 512, [Bfeat])
                self.dump(QT[:, 0:512], 512, 512, [Bq])
                self.dump(KT[:, 0:512], 1024, 512, [Bk])
                self.dump(V[:, 0, :], 1536, 256, [Bv])
                self.dump(V[:, 1, :], 1792, 256, [Bv])
                self.dump(KTOK[:, 0, :], 2048, 128, [Bkt])
                self.dump(ZS[:, 0, :], 2176, 256, [Bz])
            if self.kb < 3:
                continue
            for s_ in range(n_seq):
                for d_ in range(2):
                    ch = s_ * 2 + d_
                    if g == 1:
                        self.DMA("sp", S8[:, ch, :], self.s0[j, d_, hd], [], [BS[ch]])
                    else:
                        self.MSET("dve", S8[:, ch, :], 0.0, [BS[ch]])
                    self.CP("act", Sb8[:, ch, :], S8[:, ch, :], [BS[ch]], [BS[ch]])
            self.P.barrier()
            for d_ in range(2):
                r = d_ * 32 + hd
                self.CP("dve", SELD[:, d_, 0, :], ident[:, r:r + 1].broadcast_to([128, 128]), [self.Bc], [Bsel])
                self.TT("dve", SELD[:, d_, 1, :], SELD[:, d_, 0, :], ident[:, 64 + r:64 + r + 1].broadcast_to([128, 128]), ALU.add,
                        [self.Bc, Bsel], [Bsel])
            visited = set()
            done = set()

            def visit(t, d_, ch, step, st_):
                r = d_ * 32 + hd
                tc = slice(t * 128, (t + 1) * 128)
                B_ = st_["B"]
                SC = st_["SCAL"]
                bank, bbk = self.psb[st_["qi"]], self.Bps[st_["qi"]]
                SEL1, SEL2 = SELD[:, d_, 0, :], SELD[:, d_, 1, :]
                self.TR(bank[:, 0:128], FEAT[:, tc], ident[:], [Bfeat, self.Bc], [bbk])
                yield
                self.CP("dve", SC[:, 0:1], bank[:, r:r + 1], [bbk], [B_])
                self.CP("dve", SC[:, 1:2], bank[:, 64 + r:64 + r + 1], [bbk], [B_])
                self.MM(bank[:, 0:128], SEL1, FEAT[:, tc], True, True, [Bsel, Bfeat], [bbk])
                self.MM(bank[:, 128:256], SEL2, FEAT[:, tc], True, True, [Bsel, Bfeat], [bbk])
                colT = t * 128 + (127 if d_ == 0 else 0)
                self.MM(bank[:, 256:257], SEL1, FEAT[:, colT:colT + 1], True, True, [Bsel, Bfeat], [bbk])
                yield
                self.CP("dve", SC[:, 2:3], bank[:, 256:257], [bbk], [B_])
                self.STT(st_["EX"], bank[:, 0:256], SC[:, 0:1], self.negcat[:, d_, :], ALU.subtract, ALU.add, [bbk, B_, self.Bbc], [B_])
                self.ACT(st_["EG"], bank[:, 0:128], AF.Exp, [bbk], [B_])
                yield
                self.ACT(SC[:, 3:4], SC[:, 1:2], AF.Exp, [B_], [B_])
                self.ACT(SC[:, 4:5], SC[:, 0:1], AF.Exp, [B_], [B_], bias=SC[:, 1:2])
                self.ACT(SC[:, 5:6], SC[:, 0:1], AF.Exp, [B_], [B_], bias=SC[:, 2:3], scale=-1.0)
                self.ACT(SC[:, 6:7], SC[:, 2:3], AF.Exp, [B_], [B_])
                self.ACT(st_["EX"], st_["EX"], AF.Exp, [B_], [B_])
                self.MM(bank[:, 0:128], KT[:, tc], QT[:, tc], True, True, [Bk, Bq], [bbk])
                self.MM(bank[:, 128:256], KT[:, tc], KT[:, tc], True, True, [Bk], [bbk])
                yield
                self.TT("dve", st_["QKNT"], bank[:, 0:256], st_["EX"], ALU.mult, [bbk, B_], [B_])
                self.TT("pool", st_["QDT"], QT[:, tc], st_["EG"], ALU.mult, [Bq, B_], [st_["BQ"]])
                self.ACT(st_["KD"], KTOK[:, t, :], AF.Copy, [Bkt, B_], [st_["BQ"]], scale=SC[:, 5:6])
                self.ACT(st_["KBG"], KTOK[:, t, :], AF.Copy, [Bkt, B_], [st_["BQ"]], scale=SC[:, 4:5])
                yield
                N = st_["QKNT"][:, 128:256]
                ph = bank[:, 0:64].bitcast(BF16)
                self.TR(ph, N, identb[:], [B_, self.Bc], [bbk])
                yield
                self.CP("act", st_["NTt"], ph, [bbk], [B_])
                yield
                Am = st_["NTt"]
                LVS = self.lvs[:]
                BL = st_["BL"]

                NA = st_["NA"].rearrange("p (b n) -> p b n", n=128)
                LVS2 = self.lvs[:].unsqueeze(1).broadcast_to([128, 2, 128])
                ID2 = identb[:].unsqueeze(1).broadcast_to([128, 2, 128])

                def mk_l(k, lb):
                    self.STT(st_["LC"][:, lb, :].rearrange("p (b n) -> p b n", n=128), LVS2, float(k), NA, ALU.is_equal, ALU.mult,
                             [self.Bbc, B_], [BL])
                mk_l(0, 0)
                self.TT("dve", st_["DC"][:, 0, :].rearrange("p (b n) -> p b n", n=128), ID2,
                        st_["LC"][:, 0, :].rearrange("p (b n) -> p b n", n=128), ALU.subtract, [self.Bc, BL], [B_])
                dcur = 0
                for k in range(1, 7):
                    lb = k % 2
                    mk_l(k, lb)
                    yield
                    Dt_, D_ = st_["DC"][:, dcur, 0:128], st_["DC"][:, dcur, 128:256]
                    self.MM(bank[:, 0:128], st_["LC"][:, lb, 128:256], Dt_, True, True, [BL, B_], [bbk])
                    self.MM(bank[:, 128:256], st_["LC"][:, lb, 0:128], D_, True, True, [BL, B_], [bbk])
                    yield
                    self.CP("act", st_["MC"], bank[:, 0:256], [bbk], [B_])
                    yield
                    self.MM(bank[:, 0:128], D_, st_["MC"][:, 0:128], True, True, [B_], [bbk])
                    self.MM(bank[:, 128:256], Dt_, st_["MC"][:, 128:256], True, True, [B_], [bbk])
                    yield
                    self.TT("dve", st_["DC"][:, 1 - dcur, :], st_["DC"][:, dcur, :], bank[:, 0:256], ALU.subtract, [B_, bbk], [B_])
                    dcur = 1 - dcur
                    yield
                TTm = st_["DC"][:, dcur, 0:128]
                vb = cnt[0] % 2
                cnt[0] += 1
                self.ACT(VB2[:, vb, :], V[:, t, :], AF.Copy, [Bv, B_], [BVB[vb]], scale=SC[:, 3:4])
                self.MM(bank[:, 0:256], TTm, VB2[:, vb, :], True, True, [B_, BVB[vb]], [bbk])
                self.MM(bank[:, 256:384], st_["KBG"], TTm, True, True, [B_, st_["BQ"]], [bbk])
                yield
                self.CP("act", st_["U"], bank[:, 0:256], [bbk], [st_["BU"]])
                self.CP("dve", st_["WT"], bank[:, 256:384], [bbk], [st_["BU"]])
                yield
                while step > 0 and (ch, step - 1) not in done:
                    yield
                self.MM(bank[:, 0:256], st_["WT"], Sb8[:, ch, :], True, True, [st_["BU"], BS[ch]], [bbk])
                yield
                vn = cnt[0] % 2
                cnt[0] += 1
                self.TT("dve", VN2[:, vn, :], st_["U"], bank[:, 0:256], ALU.subtract, [st_["BU"], bbk], [BVN[vn]])
                self.MM(bank[:, 0:256], st_["QDT"], Sb8[:, ch, :], True, False, [st_["BQ"], BS[ch]], [bbk])
                self.MM(bank[:, 0:256], st_["QKNT"][:, 0:128], VN2[:, vn, :], False, True, [B_, BVN[vn]], [bbk])
                second = t in visited
                visited.add(t)
                if not second:
                    self.CP("act", O[:, t, :], bank[:, 0:256], [bbk], [BO[t]])
                else:
                    ob = cnt[0] % 2
                    cnt[0] += 1
                    OSb = OS2[:, ob, :]
                    self.TT("dve", OSb, bank[:, 0:256], O[:, t, :], ALU.add, [bbk, BO[t]], [BOS[ob]])
                self.MM(bank[:, 0:256], st_["KD"], VN2[:, vn, :], True, True, [st_["BQ"], BVN[vn]], [bbk])
                self.STT(S8[:, ch, :], S8[:, ch, :], SC[:, 6:7], bank[:, 0:256], ALU.mult, ALU.add, [BS[ch], B_, bbk], [BS[ch]])
                self.CP("act", Sb8[:, ch, :], S8[:, ch, :], [BS[ch]], [BS[ch]])
                done.add((ch, step))
                if second:
                    st6 = fst[:, 0:6]
                    mv = fst[:, 6:8]
                    ms = fst[:, 8:9]
                    self.P.op("dve", (lambda a_: lambda e: e.bn_stats(out=st6, in_=a_))(OSb), [BOS[ob]], [self.Bstat])
                    self.P.op("dve", lambda e: e.bn_aggr(out=mv, in_=st6), [self.Bstat], [self.Bstat])
                    self.STT(ms, mv[:, 0:1], mv[:, 0:1], mv[:, 1:2], ALU.mult, ALU.add, [self.Bstat], [self.Bstat])
                    self.ACT(ms, ms, AF.Sqrt, [self.Bstat, self.Bc], [self.Bstat], bias=self.epsc[:, 0:1])
                    self.P.op("dve", lambda e: e.reciprocal(out=ms, in_=ms), [self.Bstat], [self.Bstat])
                    self.STT(Y12[:, ob, :], OSb, ms, self.ngt[:], ALU.mult, ALU.mult, [BOS[ob], self.Bstat, self.Bbc], [BY[ob]])
                    self.TT("pool", YB2[:, ob, :], Y12[:, ob, :], ZS[:, t, :], ALU.mult, [BY[ob], Bz], [BY[ob]])
                    for e2 in range(2):
                        ph2 = bank[:, e2 * 64:(e2 + 1) * 64].bitcast(BF16)
                        self.TR(ph2, YB2[:, ob, e2 * 128:(e2 + 1) * 128], identb[:], [BY[ob], self.Bc], [bbk])
                        self.CP("dve", YT2[:, ob, e2 * 128:(e2 + 1) * 128], ph2, [bbk], [BY[ob]])
                    self.DMA("sp", self.YTd[g][hd * 2:hd * 2 + 2, :, tc].rearrange("e p n -> p e n"),
                             YT2[:, ob, :].rearrange("p (e n) -> p e n", n=128), [BY[ob]], [self.BYT[g][hd][t]])

            jobs = []
            for step in range(tps if self.kb >= 4 else 0):
                for s_ in range(n_seq):
                    for d_ in range(2):
                        t = s_ * tps + (step if d_ == 0 else tps - 1 - step)
                        jobs.append((t, d_, s_ * 2 + d_, step))
            free = list(range(NSET))
            active = []
            ji = 0
            while ji < len(jobs) or active:
                while ji < len(jobs) and free:
                    q = free.pop(0)
                    t, d_, ch, step = jobs[ji]
                    ji += 1
                    active.append((visit(t, d_, ch, step, sets[q]), q))
                nxt = []
                for gen, q in active:
                    try:
                        next(gen)
                        nxt.append((gen, q))
                    except StopIteration:
                        free.append(q)
                active = nxt
            self.P.barrier()
            if g == 0:
                for s_ in range(n_seq):
                    for d_ in range(2):
                        ch = s_ * 2 + d_
                        self.DMA("sp", self.ns[s_, j, d_, hd], S8[:, ch, :], [BS[ch]], [self.BNS])

    def b_outproj(self, i, j, g, G, last):
        sf, sbb = self.slabf, self.slabb
        c = G["cond"]
        WO = self.sb_wo[:]
        Bwo = Buf("wo")
        self.DMA("sp", WO, self.wb_out[j].rearrange("(k p) n -> p k n", p=128), [self.Bw[("b_out", j)]], [Bwo])
        XT2 = sf[:, 4096:6144].rearrange("p (b n) -> p b n", n=1024)
        R12 = sf[:, 6144:8192].rearrange("p (b n) -> p b n", n=1024)
        YT16 = sbb[:, 0:4096].rearrange("p (b e n) -> p b e n", e=16, n=128)
        Bx = [Buf(), Buf()]
        Br = [Buf(), Buf()]
        Byt = [Buf(), Buf()]
        for t in range(G["ntok"] // 128):
            b = t % 2
            tc = slice(t * 128, (t + 1) * 128)
            self.DMA("sp", XT2[:, b, :], self.X[g][tc, :], [self.BX[g][t]], [Bx[b]])
            self.DMA("sp", YT16[:, b], self.YTd[g][:, :, tc].rearrange("e p n -> p e n"), [self.BYT[g][h][t] for h in range(NH)], [Byt[b]])
            pts = []
            for half in range(2):
                pt, bp = self.ps()
                for ec in range(16):
                    self.MM(pt[:, :], YT16[:, b, ec, :], WO[:, ec, half * 512:(half + 1) * 512], ec == 0, ec == 15, [Byt[b], Bwo], [bp])
                pts.append((pt, bp))
            self.epilogue(i, g, c, t, pts, XT2[:, b, :], Bx[b], R12[:, b, :], R12[:, b, :], Br[b], last)


def build_program(NP, LP, LS, depth=4, dbg=False):
    import os
    dbg = dbg or bool(os.environ.get('KDBG'))
    b = Builder(NP, LP, LS, depth, dbg)
    b.wst = b.sb("wst", [128, 1024], BF16)
    b.BVG4 = [Buf() for _ in range(4)]
    b.BVH4 = [Buf() for _ in range(4)]
    return b.build()


def _pos_table(n_tokens, grid_w=64):
    def sincos(pos, dim):
        omega = (1.0 / (np.float32(10000.0) ** (np.arange(dim // 2, dtype=np.float32) / np.float32(dim // 2)))).astype(np.float32)
        ang = pos.astype(np.float32)[:, None] * omega[None, :]
        return np.concatenate([np.sin(ang), np.cos(ang)], axis=-1).astype(np.float32)
    rows = n_tokens // grid_w
    er = sincos(np.arange(rows), D // 2)
    ec = sincos(np.arange(grid_w), D // 2)
    emb = np.concatenate([np.broadcast_to(er[:, None, :], (rows, grid_w, D // 2)),
                          np.broadcast_to(ec[None, :, :], (rows, grid_w, D // 2))], axis=-1)
    return np.ascontiguousarray(emb.reshape(rows * grid_w, D), dtype=np.float32)


def prep_shared(inp, depth):
    nA, nB = (depth + 1) // 2, depth // 2
    f = lambda a: np.ascontiguousarray(a, dtype=np.float32)
    out = {}
    out["w_ada"] = f(inp["w_ada"][:depth])
    out["b_adaT"] = f(inp["b_ada"][:depth].reshape(depth, 24, 128).transpose(2, 0, 1).reshape(128, depth * 24))
    out["ln_g"] = f(inp["ln_g"][:depth])
    out["ln_b"] = f(inp["ln_b"][:depth])
    out["a_w_in"] = f(inp["a_w_in"][:nA])
    out["a_ln_gT"] = f(inp["a_ln_g"][:nA].reshape(nA, 16, 128).transpose(2, 0, 1).reshape(128, nA * 16))
    out["a_ln_bT"] = f(inp["a_ln_b"][:nA].reshape(nA, 16, 128).transpose(2, 0, 1).reshape(128, nA * 16))
    out["a_w_sT"] = f(inp["a_w_s"][:nA].transpose(0, 3, 1, 2).reshape(nA, 128, 8 * 128))
    out["a_b_s"] = f(inp["a_b_s"][:nA].reshape(nA, 8 * 128))
    out["a_w_out"] = f(inp["a_w_out"][:nA])
    nb = max(nB, 1)
    out["b_w_in"] = f(inp["b_w_in"][:nb])
    out["b_convT"] = f(inp["b_conv_w"][:nb].reshape(nb, 5, 32, 128).transpose(3, 0, 2, 1).reshape(128, nb * 32 * 5))
    al = np.zeros((nb, 128, 1), np.float32)
    db = np.zeros((nb, 128, 1), np.float32)
    for j in range(nb):
        for d_ in range(2):
            al[j, d_ * 32:d_ * 32 + 8, 0] = inp["b_A_log"][j, d_]
            db[j, d_ * 32:d_ * 32 + 8, 0] = inp["b_dt_bias"][j, d_]
    out["b_alogP"] = al
    out["b_dtbP"] = db
    out["b_norm_g"] = f(inp["b_norm_g"][:nb])
    pm = np.zeros((128, 4), np.float32)
    pm[0:32, 0] = 1.0
    pm[32:64, 1] = 1.0
    pm[64:128, 2] = 1.0
    out["pmask"] = pm
    ii = np.arange(128)
    xr = ii[:, None] ^ ii[None, :]
    lv = np.full((128, 128), -1.0, np.float32)
    nz = xr > 0
    lv[nz] = np.floor(np.log2(xr[nz])).astype(np.float32)
    out["lvs"] = lv
    out["b_w_out"] = f(inp["b_w_out"][:nb])
    return out


def prep_core(inp, shared, core, NP, LP, LS, depth, n_per_group):
    nb = max(depth // 2, 1)
    b = core // n_per_group
    f = lambda a: np.ascontiguousarray(a, dtype=np.float32)
    m = dict(shared)
    m["xp"] = f(inp["x_prompt"][core * NP:(core + 1) * NP].reshape(NP * LP, D))
    m["xs"] = f(inp["x_sample"][b])
    m["pos"] = _pos_table(LS)
    cond2 = np.stack([inp["c_ctx"], inp["c"][b]], axis=0)
    m["condT"] = f(cond2.reshape(2, 8, 128).transpose(2, 1, 0).reshape(128, 16))
    m["s0"] = f(inp["state_delta"][b][:nb])
    return m


_CACHE = {}


def kernel(**inputs):
    inp = {k: np.asarray(v) for k, v in inputs.items()}
    NP, LP, LS, depth = 4, 256, 4096, 4
    key = (NP, LP, LS, depth)
    if key not in _CACHE:
        _CACHE[key] = build_program(NP, LP, LS, depth)
    nc = _CACHE[key]
    shared = prep_shared(inp, depth)
    in_maps = [prep_core(inp, shared, c, NP, LP, LS, depth, 4) for c in range(8)]
    res = run_bass_kernel_spmd(nc, in_maps, core_ids=list(range(8)))
    r = res.results
    y_prompt = np.concatenate([r[c]["yp"].reshape(NP, LP, D) for c in range(8)], axis=0)
    y_sample = np.stack([r[0]["ys"], r[4]["ys"]], axis=0)
    ns = np.concatenate([r[c]["ns"] for c in range(8)], axis=0)
    return (y_prompt.astype(np.float32), y_sample.astype(np.float32), ns.astype(np.float32))
```

```python
import numpy as np
from contextlib import ExitStack
import concourse.bass as bass
import concourse.mybir as mybir
from concourse.bass_utils import run_bass_kernel_spmd

F32 = mybir.dt.float32
BF16 = mybir.dt.bfloat16
AF = mybir.ActivationFunctionType
ALU = mybir.AluOpType

D = 1024
E = 2048
KW = 1024
DK = 128
DV = 256
NH = 8
BW = 6176
ALPHA = (2.0 * 4) ** 0.25
EPS = 1e-6
NEG = -1.0e30


class Buf:
    __slots__ = ("name", "w", "r", "x")

    def __init__(self, name="", x=False):
        self.name = name
        self.w = None
        self.r = {}
        self.x = x


class Prog:
    ENGS = ("pe", "dve", "act", "pool", "sp")
    NSLOT = 8
    SAME_ENG_SKIP = 12

    def __init__(self):
        self.ops = {e: [] for e in self.ENGS}
        self.cnt = {e: 0 for e in self.ENGS}
        self.waited = {e: {} for e in self.ENGS}
        self.dcnt = {e: 0 for e in self.ENGS}

    def _deps(self, eng, reads, writes):
        deps = {}

        def add(p):
            if p is None:
                return
            k = p[0]
            if k not in deps or deps[k][1] < p[1]:
                deps[k] = p
        for b in reads:
            add(b.w)
        for b in writes:
            add(b.w)
            for p in b.r.values():
                add(p)
        waits = []
        for k, p in deps.items():
            val = p[1]
            if k == eng and self.cnt[eng] - p[3] > self.SAME_ENG_SKIP:
                continue
            if self.waited[eng].get(k, 0) >= val:
                continue
            self.waited[eng][k] = val
            waits.append((k, val))
        return waits

    def _record(self, reads, writes, prod):
        k = prod[0]
        for b in reads:
            if k not in b.r or b.r[k][1] < prod[1]:
                b.r[k] = prod
        for b in writes:
            b.w = prod
            b.r = {}

    def op(self, eng, fn, reads=(), writes=()):
        xs = [b for b in reads if b.x]
        if xs:
            writes = list(writes) + xs
        waits = self._deps(eng, reads, writes)
        idx = self.cnt[eng]
        self.cnt[eng] = idx + 1
        self.ops[eng].append((waits, fn, (eng, 1)))
        self._record(reads, writes, (eng, idx + 1, eng, idx))

    def dma(self, q, fn, reads=(), writes=()):
        waits = self._deps(q, reads, writes)
        i = self.dcnt[q]
        self.dcnt[q] = i + 1
        k = ("dma", q, i % self.NSLOT)
        prev = 16 * (i // self.NSLOT)
        if prev > 0 and self.waited[q].get(k, 0) < prev:
            self.waited[q][k] = prev
            waits.append((k, prev))
        self.ops[q].append((waits, fn, (k, 16)))
        self._record(reads, writes, (k, prev + 16, None, None))

    def barrier(self):
        tgt = [(e, self.cnt[e]) for e in self.ENGS if self.cnt[e] > 0]
        for q in self.ENGS:
            n = self.dcnt[q]
            for slot in range(min(n, self.NSLOT)):
                tgt.append((("dma", q, slot), 16 * ((n - 1 - slot) // self.NSLOT + 1)))
        for e in self.ENGS:
            waits = []
            for k, v in tgt:
                if k == e:
                    continue
                if self.waited[e].get(k, 0) >= v:
                    continue
                self.waited[e][k] = v
                waits.append((k, v))
            if waits:
                self.ops[e].append((waits, None, None))

    def replay(self, nc):
        semkeys = list(self.ENGS)
        for q in self.ENGS:
            for s in range(min(self.dcnt[q], self.NSLOT)):
                semkeys.append(("dma", q, s))
        with ExitStack() as st:
            sems = {}
            for k in semkeys:
                nm = k if isinstance(k, str) else f"d_{k[1]}_{k[2]}"
                sems[k] = st.enter_context(nc.semaphore("s_" + nm))
            block = st.enter_context(nc.Block())
            engmap = {"pe": "tensor", "dve": "vector", "act": "scalar", "pool": "gpsimd", "sp": "sync"}

            def mk(ename):
                oplist = self.ops[ename]

                def body(e):
                    for waits, fn, inc in oplist:
                        for k, v in waits:
                            e.wait_ge(sems[k], v)
                        if fn is not None:
                            fn(e).then_inc(sems[inc[0]], inc[1])
                return body

            for ename in self.ENGS:
                if self.ops[ename]:
                    getattr(block, engmap[ename])(mk(ename))


class Builder:
    def __init__(self, NP, LP, LS, depth=4, dbg=False):
        self.NP, self.LP, self.LS, self.depth = NP, LP, LS, depth
        self.nc = nc = bass.Bass("TRN2", target_bir_lowering=False)
        self.P = Prog()
        self.st = ExitStack()
        self.groups = [dict(n_seq=NP, L=LP, cond=0, ntok=NP * LP), dict(n_seq=1, L=LS, cond=1, ntok=LS)]
        nA = (depth + 1) // 2
        nB = depth // 2
        self.nA, self.nB = nA, nB

        def din(name, shape, dt=F32):
            return nc.dram_tensor(name, list(shape), dt, kind="ExternalInput").ap()

        def dout(name, shape, dt=F32):
            return nc.dram_tensor(name, list(shape), dt, kind="ExternalOutput").ap()

        def dint(name, shape, dt=F32):
            return nc.dram_tensor(name, list(shape), dt, kind="Internal").ap()

        self.xin = [din("xp", [NP * LP, D]), din("xs", [LS, D])]
        self.pos = din("pos", [LS, D])
        self.condT = din("condT", [128, 16])
        self.s0 = din("s0", [max(nB, 1), 2, NH, DK, DV])
        self.w_ada = din("w_ada", [depth, D, 3 * D])
        self.b_adaT = din("b_adaT", [128, depth * 24])
        self.ln_g = din("ln_g", [depth, D])
        self.ln_b = din("ln_b", [depth, D])
        self.a_w_in = din("a_w_in", [nA, D, 3 * E])
        self.a_ln_gT = din("a_ln_gT", [128, nA * 16])
        self.a_ln_bT = din("a_ln_bT", [128, nA * 16])
        self.a_w_sT = din("a_w_sT", [nA, 128, 8 * 128])
        self.a_b_s = din("a_b_s", [nA, 8 * 128])
        self.a_w_out = din("a_w_out", [nA, E, D])
        self.b_w_in = din("b_w_in", [max(nB, 1), D, BW])
        self.b_convT = din("b_convT", [128, max(nB, 1) * 32 * 5])
        self.b_alogP = din("b_alogP", [max(nB, 1), 128, 1])
        self.b_dtbP = din("b_dtbP", [max(nB, 1), 128, 1])
        self.b_norm_g = din("b_norm_g", [max(nB, 1), DV])
        self.pmask_d = din("pmask", [128, 4])
        self.lvs_d = din("lvs", [128, 128])
        self.b_w_out = din("b_w_out", [max(nB, 1), E, D])
        self.yout = [dout("yp", [NP * LP, D]), dout("ys", [LS, D])]
        self.ns = dout("ns", [NP, max(nB, 1), 2, NH, DK, DV])
        self.X = [dint("X0", [NP * LP, D]), dint("X1", [LS, D])]
        self.wa_in = dint("wa_in", [nA, D, 3 * E], BF16)
        self.wa_out = dint("wa_out", [nA, E, D], BF16)
        self.wb_in = dint("wb_in", [max(nB, 1), D, BW], BF16)
        self.wb_out = dint("wb_out", [max(nB, 1), E, D], BF16)
        self.HTd = [dint("HTd0", [NP * LP // 256, 128, 2048], BF16), dint("HTd1", [LS // 256, 128, 2048], BF16)]
        self.YTd = [dint("YTd0", [16, 128, NP * LP], BF16), dint("YTd1", [16, 128, LS], BF16)]
        self.dbg = None
        if dbg:
            self.dbg = dout("dbg", [128, 4096])
        self.Bw = {}
        self.BX = [[Buf(f"X{g}_{t}") for t in range(self.groups[g]["ntok"] // 128)] for g in range(2)]
        self.BHT = [[Buf() for _ in range(self.groups[g]["ntok"] // 128)] for g in range(2)]
        self.BYT = [[[Buf() for _ in range(self.groups[g]["ntok"] // 128)] for _h in range(NH)] for g in range(2)]
        self.BNS = Buf("ns")
        self.BYO = Buf("yout")
        self._ps_i = 0

    def sb(self, name, shape, dt):
        return self.st.enter_context(self.nc.sbuf_tensor(name, list(shape), dt))

    def MM(self, out, lhsT, rhs, start, stop, R, W):
        self.P.op("pe", lambda e: e.matmul(out, lhsT=lhsT, rhs=rhs, start=start, stop=stop), R, W)

    def TR(self, out, in_, ident, R, W):
        self.P.op("pe", lambda e: e.transpose(out, in_, ident), R, W)

    def ACT(self, out, in_, func, R, W, bias=None, scale=None, accum=None):
        kw = {}
        if bias is not None:
            kw["bias"] = bias
        if scale is not None:
            kw["scale"] = scale
        if accum is not None:
            kw["accum_out"] = accum
        self.P.op("act", lambda e: e.activation(out=out, in_=in_, func=func, **kw), R, W)

    def TS(self, eng, out, in0, s1, s2, op0, op1, R, W):
        if op1 is None:
            self.P.op(eng, lambda e: e.tensor_scalar(out=out, in0=in0, scalar1=s1, scalar2=None, op0=op0), R, W)
        else:
            self.P.op(eng, lambda e: e.tensor_scalar(out=out, in0=in0, scalar1=s1, scalar2=s2, op0=op0, op1=op1), R, W)

    def STT(self, out, in0, scalar, in1, op0, op1, R, W):
        self.P.op("dve", lambda e: e.scalar_tensor_tensor(out=out, in0=in0, scalar=scalar, in1=in1, op0=op0, op1=op1), R, W)

    def TT(self, eng, out, in0, in1, op, R, W):
        self.P.op(eng, lambda e: e.tensor_tensor(out=out, in0=in0, in1=in1, op=op), R, W)

    def CP(self, eng, out, in_, R, W):
        if eng == "act":
            self.ACT(out, in_, AF.Copy, R, W)
        else:
            self.P.op(eng, lambda e: e.tensor_copy(out=out, in_=in_), R, W)

    def MSET(self, eng, ap, val, W):
        self.P.op(eng, lambda e: e.memset(ap, val), (), W)

    def DMA(self, q, out, in_, R, W):
        self.P.dma(q, lambda e: e.dma_start(out=out, in_=in_), R, W)

    def dump(self, src, col0, n, R):
        if self.dbg is None:
            return
        if not hasattr(self, "dbgst"):
            self.dbgst = self.sb("dbgst", [128, 512], F32)
            self.Bdbg = Buf("dbg")
        self.CP("act", self.dbgst[:, 0:n], src, R, [self.Bdbg])
        self.DMA("sp", self.dbg[:, col0:col0 + n], self.dbgst[:, 0:n], [self.Bdbg], [])

    def ps(self):
        i = self._ps_i
        self._ps_i = (i + 1) % len(self.psb)
        return self.psb[i], self.Bps[i]

    def build(self):
        nc = self.nc
        sb = self.sb
        self.psb = [self.st.enter_context(nc.psum_tensor(f"ps{i}", [128, 512], F32)) for i in range(8)]
        self.Bps = [Buf(f"ps{i}", x=True) for i in range(8)]
        self.ident = sb("ident", [128, 128], F32)
        self.identb = sb("identb", [128, 128], BF16)
        self.onesf = sb("onesf", [128, 128], F32)
        self.onesb = sb("onesb", [128, 128], BF16)
        self.Bc = Buf("consts")
        self.MSET("pool", self.onesf[:], 1.0, [self.Bc])
        self.P.op("pool", lambda e: e.affine_select(out=self.ident[:], in_=self.onesf[:], pattern=[[-1, 128]],
                                                     compare_op=ALU.is_equal, fill=0.0, base=0, channel_multiplier=1),
                  [self.Bc], [self.Bc])
        self.CP("dve", self.identb[:], self.ident[:], [self.Bc], [self.Bc])
        self.CP("dve", self.onesb[:], self.onesf[:], [self.Bc], [self.Bc])
        self.slabf = sb("slabf", [128, 12 * 1024], F32)
        self.slabb = sb("slabb", [128, 49 * 1024], BF16)
        self.sb_wo = sb("WO", [128, 16, 1024], BF16)
        self.MOD = sb("MOD", [128, self.depth * 48], F32)
        self.SC1 = sb("SC1", [128, self.depth * 16], F32)
        self.GATE = sb("GATE", [128, 2 * D], F32)
        self.LNG = sb("LNG", [128, D], F32)
        self.LNB = sb("LNB", [128, D], F32)
        self.BGATE = Buf("gate")
        self.BLN = Buf("ln")
        self.BMOD = Buf("mod")

        import os
        stage = int(os.environ.get("KSTAGE", "99"))
        if stage >= 1:
            self.weight_casts()
        self.small_consts()
        if stage >= 2:
            self.prologue()
            self.P.barrier()
        if stage >= 3:
            self.adaln()
        for i in range(self.depth):
            if stage < 4:
                break
            self.P.barrier()
            self.layer_consts(i)
            self.P.barrier()
            if stage < 5:
                break
            if i % 2 == 0:
                self.layer_a(i)
            else:
                self.layer_b(i)
        self.P.barrier()
        self.P.replay(nc)
        self.st.close()
        return nc

    def weight_casts(self):
        for nm, src, dst, n, rows in (("a_in", self.a_w_in, self.wa_in, self.nA, D), ("a_out", self.a_w_out, self.wa_out, self.nA, E),
                                      ("b_in", self.b_w_in, self.wb_in, self.nB, D), ("b_out", self.b_w_out, self.wb_out, self.nB, E)):
            for l in range(n):
                b = Buf(f"w_{nm}{l}")
                self.Bw[(nm, l)] = b
                for r0 in range(0, rows, 256):
                    self.DMA("pool", dst[l, r0:r0 + 256, :], src[l, r0:r0 + 256, :], [], [b])

    def adaln(self):
        P = self.P
        sf = self.slabf
        cT = sf[:, 0:16]
        sT = sf[:, 16:32]
        bT = sf[:, 32:32 + self.depth * 24]
        W = sf[:, 1024:1024 + 8192].rearrange("p (k n) -> p k n", n=1024)
        Bs = Buf("ada_s")
        Bb = Buf("ada_b")
        BW_ = Buf("adaw")
        self.DMA("sp", cT, self.condT[:, :], [], [Bs])
        self.DMA("sp", bT, self.b_adaT[:, :], [], [Bb])
        self.ACT(sT, cT, AF.Silu, [Bs], [Bs])
        for i in range(self.depth):
            pt, bp = self.ps()
            for part in range(3):
                self.DMA("sp", W, self.w_ada[i, :, part * 1024:(part + 1) * 1024].rearrange("(k p) n -> p k n", p=128), [], [BW_])
                for c8 in range(8):
                    ch = part * 8 + c8
                    for kc in range(8):
                        self.MM(pt[:, ch * 2:ch * 2 + 2], W[:, kc, c8 * 128:(c8 + 1) * 128], sT[:, kc * 2:kc * 2 + 2],
                                kc == 0, kc == 7, [BW_, Bs], [bp])
            mod = self.MOD[:, i * 48:(i + 1) * 48].rearrange("p (ch c) -> p ch c", c=2)
            self.TT("dve", mod, pt[:, 0:48].rearrange("p (ch c) -> p ch c", c=2),
                    bT[:, i * 24:(i + 1) * 24].unsqueeze(2).broadcast_to([128, 24, 2]), ALU.add, [bp, Bb], [self.BMOD])
            self.TS("dve", self.SC1[:, i * 16:(i + 1) * 16], self.MOD[:, i * 48 + 16:i * 48 + 32], 1.0, None, ALU.add, None,
                    [self.BMOD], [self.BMOD])
        if self.dbg is not None:
            self.DMA("sp", self.dbg[:, 0:48 * self.depth], self.MOD[:, :], [self.BMOD], [])
            self.DMA("sp", self.dbg[:, 512:512 + 16 * self.depth], self.SC1[:, :], [self.BMOD], [])

    def shift_ap(self, i, kc, c):
        o = i * 48 + kc * 2 + c
        return self.MOD[:, o:o + 1]

    def scale1_ap(self, i, kc, c):
        o = i * 16 + kc * 2 + c
        return self.SC1[:, o:o + 1]

    def layer_consts(self, i):
        gb = self.slabf[:, 0:128]
        Bg = Buf("gb")
        for c in range(2):
            for half in range(2):
                pt, bp = self.ps()
                for q in range(4):
                    kc = half * 4 + q
                    o = i * 48 + (16 + kc) * 2 + c
                    self.CP("dve", gb, self.MOD[:, o:o + 1].broadcast_to([128, 128]), [self.BMOD], [Bg])
                    self.MM(pt[:, q * 128:(q + 1) * 128], gb, self.ident[:], True, True, [Bg, self.Bc], [bp])
                self.CP("act", self.GATE[:, c * D + half * 512:c * D + (half + 1) * 512], pt[:, :], [bp], [self.BGATE])
        self.DMA("sp", self.LNG[:], self.ln_g[i:i + 1, :].partition_broadcast(128), [], [self.BLN])
        self.DMA("sp", self.LNB[:], self.ln_b[i:i + 1, :].partition_broadcast(128), [], [self.BLN])

    def prologue(self):
        sf = self.slabf
        A = sf[:, 0:4096].rearrange("p (s d) -> p s d", d=1024)
        Bq = sf[:, 4096:8192].rearrange("p (s d) -> p s d", d=1024)
        Ba, Bb = Buf("pa"), Buf("pb")
        for t4 in range(self.LS // 512):
            rows = slice(t4 * 512, (t4 + 1) * 512)
            self.DMA("sp", A, self.xin[1][rows, :].rearrange("(s p) d -> p s d", p=128), [], [Ba])
            self.DMA("sp", Bq, self.pos[rows, :].rearrange("(s p) d -> p s d", p=128), [], [Bb])
            self.TT("dve", A, A, Bq, ALU.add, [Ba, Bb], [Ba])
            self.DMA("pool", self.X[1][rows, :].rearrange("(s p) d -> p s d", p=128), A, [Ba],
                     [self.BX[1][t4 * 4 + s] for s in range(4)])

    def load_x_tile(self, i, g, t, XT, Bx, nsub):
        first = (i == 0 and g == 0)
        src = self.xin[g] if first else self.X[g]
        rows = slice(t * 128, (t + nsub) * 128)
        R = [] if first else [self.BX[g][t + s] for s in range(nsub)]
        self.DMA("sp", XT, src[rows, :].rearrange("(s p) d -> p s d", p=128), R, [Bx])

    def make_ht(self, i, c, XT, Bx, nsub, HT, Bh):
        for kc in range(8):
            pt, bp = self.ps()
            for s in range(nsub):
                self.TR(pt[:, s * 128:(s + 1) * 128], XT[:, s, kc * 128:(kc + 1) * 128], self.ident[:], [Bx, self.Bc], [bp])
            self.TS("dve", HT[:, kc, 0:nsub * 128], pt[:, 0:nsub * 128], self.scale1_ap(i, kc, c), self.shift_ap(i, kc, c),
                    ALU.mult, ALU.add, [bp, self.BMOD], [Bh])

    def epilogue(self, i, g, c, t, pts, xrow, Bx, R1, XN, Br, last):
        import os
        kep = int(os.environ.get("KEP", "99"))
        if kep < 1:
            return
        for half in range(2):
            pt, bp = pts[half]
            hs = slice(half * 512, (half + 1) * 512)
            self.TT("dve", R1[:, hs], pt[:, :], self.GATE[:, c * D + half * 512:c * D + (half + 1) * 512], ALU.mult,
                    [bp, self.BGATE], [Br])
        self.STT(R1, xrow, ALPHA, R1, ALU.mult, ALU.add, [Bx, Br], [Br])
        if kep < 2:
            return
        st6 = self.stat[:, 0:12]
        mv = self.stat[:, 12:14]
        rs = self.stat[:, 14:15]
        for half in range(2):
            self.P.op("dve", (lambda o, a: lambda e: e.bn_stats(out=o, in_=a))(st6[:, half * 6:(half + 1) * 6], R1[:, half * 512:(half + 1) * 512]),
                      [Br], [self.Bstat])
        self.P.op("dve", lambda e: e.bn_aggr(out=mv, in_=st6), [self.Bstat], [self.Bstat])
        self.ACT(rs, mv[:, 1:2], AF.Sqrt, [self.Bstat], [self.Bstat], bias=self.epsc[:, 0:1])
        self.P.op("dve", lambda e: e.reciprocal(out=rs, in_=rs), [self.Bstat], [self.Bstat])
        if kep < 3:
            return
        self.TS("dve", XN, R1, mv[:, 0:1], rs, ALU.subtract, ALU.mult, [Br, self.Bstat], [Br])
        if kep < 4:
            return
        self.TT("pool", XN, XN, self.LNG[:], ALU.mult, [Br, self.BLN], [Br])
        self.TT("pool", XN, XN, self.LNB[:], ALU.add, [Br, self.BLN], [Br])
        if kep < 5:
            return
        if last:
            self.DMA("sp", self.yout[g][t * 128:(t + 1) * 128, :], XN, [Br], [self.BYO])
        else:
            self.DMA("sp", self.X[g][t * 128:(t + 1) * 128, :], XN, [Br], [self.BX[g][t]])

    def small_consts(self):
        if hasattr(self, "stat"):
            return
        self.stat = self.sb("stat", [128, 16], F32)
        self.Bstat = Buf("stat")
        self.epsc = self.sb("epsc", [128, 2], F32)
        self.MSET("dve", self.epsc[:, 0:1], EPS, [self.Bc])
        self.MSET("dve", self.epsc[:, 1:2], 1.0, [self.Bc])

    def layer_a(self, i):
        self.small_consts()
        l = i // 2
        last = (i == self.depth - 1)
        sf, sbb = self.slabf, self.slabb
        XT = sf[:, 0:4096].rearrange("p (s d) -> p s d", d=1024)
        BIAS = sf[:, 4096:6144].rearrange("p (c q) -> p c q", q=128)
        R1 = sf[:, 6144:8192].rearrange("p (b n) -> p b n", n=1024)
        T1 = sf[:, 8192:9216].rearrange("p (b n) -> p b n", n=512)
        WSF = sf[:, 9216:10240]
        BSB = sf[:, 10240:11264].rearrange("p (g q) -> p g q", q=128)
        RS = sf[:, 11264:11392]
        GLN = sf[:, 11392:11408]
        BLNv = sf[:, 11408:11424]
        vst = sf[:, 11424:11424 + 64]
        HT = sbb[:, 0:4096].rearrange("p (k n) -> p k n", n=512)
        U = sbb[:, 4096:12288].rearrange("p (c n) -> p c n", n=512)
        Z = sbb[:, 12288:20480].rearrange("p (c n) -> p c n", n=512)
        VGa = sbb[:, 20480:28672].rearrange("p (b n) -> p b n", n=2048)
        VHa = sbb[:, 28672:36864].rearrange("p (b n) -> p b n", n=2048)
        WB = sbb[:, 36864:49152].rearrange("p (b k n) -> p b k n", k=8, n=512)
        WST = self.wst[:].rearrange("p (g q) -> p g q", q=128)
        WO = self.sb_wo[:]
        self.VGa, self.VHa = VGa, VHa
        Bxt, Bht, Bwst, Bbias, Bwo = Buf("XT"), Buf("HT"), Buf("WST"), Buf("BIAS"), Buf("WO")
        BU = [Buf() for _ in range(16)]
        BZ = [Buf() for _ in range(16)]
        BWB = [Buf(), Buf(), Buf()]
        BR = [Buf(), Buf()]
        BT1 = [Buf(), Buf()]
        Bsm = Buf("small")
        self.DMA("sp", WSF, self.a_w_sT[l, :, :], [], [Bwst])
        self.CP("dve", WST.rearrange("p g q -> p (g q)"), WSF, [Bwst], [Bwst])
        self.DMA("sp", GLN, self.a_ln_gT[:, l * 16:(l + 1) * 16], [], [Bsm])
        self.DMA("sp", BLNv, self.a_ln_bT[:, l * 16:(l + 1) * 16], [], [Bsm])
        self.DMA("sp", BSB.rearrange("p g q -> p (g q)"), self.a_b_s[l:l + 1, :].partition_broadcast(128), [], [Bsm])
        for gq in range(8):
            pt, bp = self.ps()
            self.MM(pt[:, 0:128], self.onesb[:], WST[:, gq, :], True, True, [self.Bc, Bwst], [bp])
            self.CP("act", RS, pt[:, 0:128], [bp], [Bsm])
            for cc in range(2):
                ch = gq * 2 + cc
                self.STT(BIAS[:, ch, :], RS, BLNv[:, ch:ch + 1], BSB[:, gq, :], ALU.mult, ALU.add, [Bsm], [Bbias])
        import os
        sub = int(os.environ.get("KSUB", "99"))
        if sub < 1:
            return
        self.DMA("sp", WO, self.wa_out[l].rearrange("(k p) n -> p k n", p=128), [self.Bw[("a_out", l)]], [Bwo])
        wsrc = self.wa_in[l]
        wcnt = [0]

        def load_w(col0):
            b = wcnt[0] % 3
            wcnt[0] += 1
            self.DMA("sp", WB[:, b, :, :], wsrc[:, col0:col0 + 512].rearrange("(k p) n -> p k n", p=128),
                     [self.Bw[("a_in", l)]], [BWB[b]])
            return b

        for g, G in enumerate(self.groups):
            c = G["cond"]
            for t4 in range(G["ntok"] // 512):
                t = t4 * 4
                self.load_x_tile(i, g, t, XT, Bxt, 4)
                self.make_ht(i, c, XT, Bxt, 4, HT, Bht)
                if sub < 2:
                    continue
                blocks = [("v", 2048 + b * 512, b) for b in range(4)] + [("u", b * 512, b) for b in range(4)] + \
                         [("z", 4096 + b * 512, b) for b in range(4)]
                nxt = load_w(blocks[0][1])
                for bi, (kind, col0, b4) in enumerate(blocks):
                    wb = nxt
                    if bi + 1 < len(blocks):
                        nxt = load_w(blocks[bi + 1][1])
                    if kind == "v":
                        for s in range(4):
                            pt, bp = self.ps()
                            for kc in range(8):
                                self.MM(pt[:, :], HT[:, kc, s * 128:(s + 1) * 128], WB[:, wb, kc, :], kc == 0, kc == 7,
                                        [Bht, BWB[wb]], [bp])
                            self.ACT(self._vg(sf, s)[:, b4 * 512:(b4 + 1) * 512],
                                     pt[:, :], AF.Gelu_apprx_tanh, [bp], [self.BVG4[s]])
                    else:
                        dst, Bd, fn = (U, BU, AF.Gelu_apprx_tanh) if kind == "u" else (Z, BZ, AF.Silu)
                        for cc in range(4):
                            ch = b4 * 4 + cc
                            pt, bp = self.ps()
                            for kc in range(8):
                                self.MM(pt[:, :], WB[:, wb, kc, cc * 128:(cc + 1) * 128], HT[:, kc, :], kc == 0, kc == 7,
                                        [Bht, BWB[wb]], [bp])
                            self.ACT(dst[:, ch, :], pt[:, :], fn, [bp], [Bd[ch]])
                    if kind == "v" and b4 == 3:
                        for s in range(4):
                            vg = self._vg(sf, s)
                            st = vst[:, 0:24]
                            mv = vst[:, 24:26]
                            rs = vst[:, 26:27]
                            for q in range(4):
                                self.P.op("dve", (lambda o, a: lambda e: e.bn_stats(out=o, in_=a))(st[:, q * 6:(q + 1) * 6], vg[:, q * 512:(q + 1) * 512]),
                                          [self.BVG4[s]], [Bsm])
                            self.P.op("dve", lambda e: e.bn_aggr(out=mv, in_=st), [Bsm], [Bsm])
                            self.ACT(rs, mv[:, 1:2], AF.Sqrt, [Bsm], [Bsm], bias=self.epsc[:, 0:1])
                            self.P.op("dve", lambda e: e.reciprocal(out=rs, in_=rs), [Bsm], [Bsm])
                            self.TS("dve", self._vh(sbb, s), vg, mv[:, 0:1], rs, ALU.subtract, ALU.mult, [self.BVG4[s], Bsm], [self.BVH4[s]])
                if sub < 3:
                    continue
                for ch in range(16):
                    self.TT("pool", U[:, ch, :], U[:, ch, :], Z[:, ch, :], ALU.mult, [BU[ch], BZ[ch]], [BU[ch]])
                for ch in range(16):
                    pt, bp = self.ps()
                    for s in range(4):
                        self.MM(pt[:, s * 128:(s + 1) * 128], self._vh(sbb, s)[:, ch * 128:(ch + 1) * 128], WST[:, ch // 2, :], True, True,
                                [self.BVH4[s], Bwst], [bp])
                    tb = ch % 2
                    self.STT(T1[:, tb, :].rearrange("p (s q) -> p s q", q=128), pt[:, :].rearrange("p (s q) -> p s q", q=128),
                             GLN[:, ch:ch + 1], BIAS[:, ch, :].unsqueeze(1).broadcast_to([128, 4, 128]), ALU.mult, ALU.add,
                             [bp, Bsm, Bbias], [BT1[tb]])
                    self.TT("pool", Z[:, ch, :], T1[:, tb, :], U[:, ch, :], ALU.mult, [BT1[tb], BU[ch]], [BZ[ch]])
                if sub < 4:
                    continue
                for s in range(4):
                    pts = []
                    for half in range(2):
                        pt, bp = self.ps()
                        for ec in range(16):
                            self.MM(pt[:, :], Z[:, ec, s * 128:(s + 1) * 128], WO[:, ec, half * 512:(half + 1) * 512], ec == 0, ec == 15,
                                    [BZ[ec], Bwo], [bp])
                        pts.append((pt, bp))
                    rb = s % 2
                    self.epilogue(i, g, c, t + s, pts, XT[:, s, :], Bxt, R1[:, rb, :], R1[:, rb, :], BR[rb], last)

    def _vg(self, sf, s):
        return self.VGa[:, s, :]

    def _vh(self, sbb, s):
        return self.VHa[:, s, :]

    def layer_b_consts(self, j):
        if not hasattr(self, "negcat"):
            self.negcat = self.sb("negcat", [128, 2, 256], F32)
            self.cw = self.sb("cw", [128, 160], F32)
            self.ngt = self.sb("ngt", [128, 256], F32)
            self.gsc = self.sb("gsc", [128, 4], F32)
            self.pmask = self.sb("pmaskt", [128, 4], F32)
            self.lvs = self.sb("lvst", [128, 128], F32)
            self.Bbc = Buf("bconst")
            zer = self.slabf[:, 0:128]
            Bz = Buf("zer")
            self.MSET("pool", zer, 0.0, [Bz])
            for d_, (pat, cm) in enumerate((([[1, 128]], -1), ([[-1, 128]], 1))):
                for kk, cmp in enumerate((ALU.is_ge, ALU.is_gt)):
                    self.P.op("pool", (lambda o, pat, cm, cmp: lambda e: e.affine_select(
                        out=o, in_=zer, pattern=pat, compare_op=cmp, fill=NEG, base=0, channel_multiplier=cm))(
                        self.negcat[:, d_, kk * 128:(kk + 1) * 128], pat, cm, cmp), [Bz], [self.Bbc])
        self.DMA("sp", self.pmask[:], self.pmask_d[:, :], [], [self.Bbc])
        self.DMA("sp", self.lvs[:], self.lvs_d[:, :], [], [self.Bbc])
        self.DMA("sp", self.cw[:], self.b_convT[:, j * 160:(j + 1) * 160], [], [self.Bbc])
        self.DMA("sp", self.ngt[:], self.b_norm_g[j:j + 1, :].partition_broadcast(128), [], [self.Bbc])
        self.DMA("sp", self.gsc[:, 0:1], self.b_dtbP[j], [], [self.Bbc])
        self.DMA("sp", self.gsc[:, 2:3], self.b_alogP[j], [], [self.Bbc])
        self.ACT(self.gsc[:, 3:4], self.gsc[:, 2:3], AF.Exp, [self.Bbc], [self.Bbc])
        self.TS("dve", self.gsc[:, 1:2], self.gsc[:, 3:4], -1.0, None, ALU.mult, None, [self.Bbc], [self.Bbc])

    def layer_b(self, i):
        j = i // 2
        last = (i == self.depth - 1)
        sf, sbb = self.slabf, self.slabb
        wor = self.sb_wo[:].rearrange("p a b -> p (a b)")
        self.layer_b_consts(j)
        ident, identb = self.ident, self.identb
        FEAT = sf[:, 0:4096]
        Bfeat = Buf("feat")
        WABF = sf[:, 9728:10752].rearrange("p (k n) -> p k n", n=128)
        WAB = sbb[:, 47616:48640].rearrange("p (k n) -> p k n", n=128)
        Bwab = Buf("wab")
        self.MSET("dve", WABF, 0.0, [Bwab])
        for q4, c0 in enumerate((0, 32, 64, 96)):
            self.DMA("sp", WABF[:, :, c0:c0 + 8],
                     self.b_w_in[j, :, 6144 + q4 * 8:6144 + (q4 + 1) * 8].rearrange("(k p) n -> p k n", p=128), [], [Bwab])
        self.CP("dve", WAB, WABF, [Bwab], [Bwab])
        import os
        kb = int(os.environ.get("KB", "99"))
        self.kb = kb
        for g, G in enumerate(self.groups):
            self.P.barrier()
            if kb >= 1:
                self.b_phase0(i, j, g, G, FEAT, Bfeat, WAB, Bwab)
            self.P.barrier()
            if kb >= 2:
                self.b_heads(i, j, g, G, FEAT, Bfeat, wor)
            self.P.barrier()
            if kb >= 5:
                self.b_outproj(i, j, g, G, last)

    def b_phase0(self, i, j, g, G, FEAT, Bfeat, WAB, Bwab):
        sf, sbb = self.slabf, self.slabb
        c = G["cond"]
        XT = sf[:, 4096:8192].rearrange("p (s d) -> p s d", d=1024)
        T1 = sf[:, 8192:8704]
        T2 = sf[:, 8704:9216]
        T3 = sf[:, 9216:9728]
        MASK = sf[:, 10752:11264]
        HTb = sbb[:, 0:8192].rearrange("p (b k n) -> p b k n", k=8, n=512)
        Bxt, Bt = Buf("bxt"), Buf("bt")
        Bh = [Buf(), Buf()]
        Bm = Buf("mask")
        self.Bft = Buf("ftall")
        self.MSET("dve", MASK, 1.0, [Bm])
        self.MSET("dve", MASK.rearrange("p (a b) -> p a b", b=128)[:, :, 0:1], 0.0, [Bm])
        import os
        kp = int(os.environ.get("KP", "99"))
        for b4 in range(G["ntok"] // 512 if kp >= 1 else 0):
            hb = b4 % 2
            cols = slice(b4 * 512, (b4 + 1) * 512)
            self.load_x_tile(i, g, b4 * 4, XT, Bxt, 4)
            self.make_ht(i, c, XT, Bxt, 4, HTb[:, hb], Bh[hb])
            for hh in range(2):
                self.DMA("sp", self.HTd[g][b4 * 2 + hh].rearrange("p (k n) -> p k n", n=256), HTb[:, hb, :, hh * 256:(hh + 1) * 256],
                         [Bh[hb]], [self.BHT[g][b4 * 4 + hh * 2], self.BHT[g][b4 * 4 + hh * 2 + 1]])
            if kp < 2:
                continue
            pt, bp = self.ps()
            for kc in range(8):
                self.MM(pt[:, :], WAB[:, kc, :], HTb[:, hb, kc, :], kc == 0, kc == 7, [Bwab, Bh[hb]], [bp])
            if kp < 3:
                continue
            self.ACT(T1[0:64, :], pt[0:64, :], AF.Exp, [bp, self.Bbc], [Bt], bias=self.gsc[0:64, 0:1])
            self.ACT(T1[0:64, :], T1[0:64, :], AF.Ln, [Bt, self.Bc], [Bt], bias=self.epsc[0:64, 1:2])
            self.TS("dve", T1[0:64, :], T1[0:64, :], self.gsc[0:64, 1:2], None, ALU.mult, None, [Bt, self.Bbc], [Bt])
            self.ACT(T1[64:128, :], pt[64:128, :], AF.Exp, [bp], [Bt], scale=-1.0)
            self.ACT(T1[64:128, :], T1[64:128, :], AF.Ln, [Bt, self.Bc], [Bt], bias=self.epsc[64:128, 1:2])
            self.TS("dve", T1[64:128, :], T1[64:128, :], -1.0, None, ALU.mult, None, [Bt], [Bt])
            if kp < 4:
                continue
            self.P.op("dve", (lambda o, m, d: lambda e: e.tensor_tensor_scan(out=o, data0=m, data1=d, initial=0.0, op0=ALU.mult, op1=ALU.add))(
                T2, MASK, T1), [Bt, Bm], [Bt])
            self.TT("dve", T3, T1, T2, ALU.subtract, [Bt], [Bt])
            self.TT("dve", T3.rearrange("p (a b) -> p a b", b=128), T3.rearrange("p (a b) -> p a b", b=128),
                    T2.rearrange("p (a b) -> p a b", b=128)[:, :, 127:128].broadcast_to([128, 4, 128]), ALU.add, [Bt], [Bt])
            pm = self.pmask
            self.TS("dve", FEAT[:, cols], T2, pm[:, 0:1], None, ALU.mult, None, [Bt, self.Bbc], [Bfeat])
            self.STT(FEAT[:, cols], T3, pm[:, 1:2], FEAT[:, cols], ALU.mult, ALU.add, [Bt, self.Bbc, Bfeat], [Bfeat])
            self.STT(FEAT[:, cols], T1, pm[:, 2:3], FEAT[:, cols], ALU.mult, ALU.add, [Bt, self.Bbc, Bfeat], [Bfeat])
            FTall = sf[:, 11264:12288].rearrange("p (t n) -> p t n", n=32)
            ptf, bpf_ = self.ps()
            for s_ in range(4):
                self.TR(ptf[:, s_ * 128:(s_ + 1) * 128], FEAT[:, b4 * 512 + s_ * 128:b4 * 512 + (s_ + 1) * 128], self.ident[:],
                        [Bfeat, self.Bc], [bpf_])
            for s_ in range(4):
                self.CP("dve", FTall[:, b4 * 4 + s_, :].rearrange("p (a b) -> p a b", b=8),
                        ptf[:, s_ * 128:(s_ + 1) * 128].rearrange("p (a b) -> p a b", b=32)[:, :, 0:8], [bpf_], [self.Bft])

    def b_heads(self, i, j, g, G, FEAT, Bfeat, wor):
        sf, sbb = self.slabf, self.slabb
        ident, identb = self.ident, self.identb
        n_seq, L, ntok = G["n_seq"], G["L"], G["ntok"]
        nt = ntok // 128
        tps = L // 128
        ACC = sf[:, 4096:4608]
        A_ = sf[:, 4608:5120]
        RN = sf[:, 5120:5632]
        EX8 = sf[:, 5632:7680].rearrange("p (b n) -> p b n", n=256)
        OS2 = sf[:, 7680:8192].rearrange("p (b n) -> p b n", n=256)
        S8 = sf[:, 8192:10240].rearrange("p (b n) -> p b n", n=256)
        Y12 = sf[:, 10240:10752].rearrange("p (b n) -> p b n", n=256)
        if not hasattr(self, "scal8"):
            self.scal8 = self.sb("scal8", [128, 64], F32)
        SCAL8 = self.scal8[:].rearrange("p (b n) -> p b n", n=8)
        SELD = sf[:, 10752:11264].rearrange("p (d k n) -> p d k n", k=2, n=128)
        FTall = sf[:, 11264:12288].rearrange("p (t n) -> p t n", n=32)
        fst = self.stat
        HTb = sbb[:, 0:4096].rearrange("p (b k n) -> p b k n", k=8, n=256)
        WH = sbb[:, 4096:10240].rearrange("p (k n) -> p k n", n=768)
        PRE = sbb[:, 10240:18432].rearrange("p (b n) -> p b n", n=4096)
        SQ = sbb[:, 18432:18944]
        QT = sbb[:, 18944:23040]
        KT = sbb[:, 23040:27136]
        V = sbb[:, 27136:35328].rearrange("p (t n) -> p t n", n=256)
        KTOK = sbb[:, 35328:39424].rearrange("p (t n) -> p t n", n=128)
        ZS = sbb[:, 39424:47616].rearrange("p (t n) -> p t n", n=256)
        VT = sbb[:, 48640:49152]
        YB2 = sbb[:, 49152:49664].rearrange("p (b n) -> p b n", n=256)
        YT2 = sbb[:, 49664:50176].rearrange("p (b n) -> p b n", n=256)
        O = wor[:, 0:nt * 256].rearrange("p (t n) -> p t n", n=256)
        jb = nt * 256
        NSET = 8
        SETW = 2304
        nch = n_seq * 2
        U8 = wor[:, jb:jb + 2048].rearrange("p (b n) -> p b n", n=256)
        b1 = jb + 2048
        VN2 = wor[:, b1:b1 + 512].rearrange("p (b n) -> p b n", n=256)
        VB2 = wor[:, b1 + 512:b1 + 1024].rearrange("p (b n) -> p b n", n=256)
        Sb8 = wor[:, b1 + 1024:b1 + 1024 + nch * 256].rearrange("p (b n) -> p b n", n=256)
        assert b1 + 1024 + nch * 256 <= 16384, (b1, nch)
        sets = []
        for q in range(NSET):
            b0 = q * SETW
            sets.append(dict(
                QKNT=sbb[:, b0:b0 + 256], NTt=sbb[:, b0 + 256:b0 + 384], NA=sbb[:, b0 + 128:b0 + 384],
                DC=sbb[:, b0 + 384:b0 + 896].rearrange("p (b n) -> p b n", n=256),
                MC=sbb[:, b0 + 896:b0 + 1152],
                LC=sbb[:, b0 + 1152:b0 + 1664].rearrange("p (b n) -> p b n", n=256),
                KBG=sbb[:, b0 + 1664:b0 + 1792], WT=sbb[:, b0 + 1792:b0 + 1920],
                QDT=sbb[:, b0 + 1920:b0 + 2048], KD=sbb[:, b0 + 2048:b0 + 2176], EG=sbb[:, b0 + 2176:b0 + 2304],
                qi=q, B=Buf(f"set{q}"), U=U8[:, q, :], EX=EX8[:, q, :], SCAL=SCAL8[:, q, :], BU=Buf(), BQ=Buf(), BL=Buf()))
        Bsel = Buf("sel")
        BVN = [Buf(), Buf()]
        BVB = [Buf(), Buf()]
        BS = [Buf() for _ in range(nch)]
        BO = [Buf() for _ in range(nt)]
        BOS = [Buf(), Buf()]
        BY = [Buf(), Buf()]
        Bh = [Buf(), Buf()]
        Bwh, Bacc, Bsq = Buf("wh"), Buf("acc"), Buf("sq")
        BPRE = [Buf(), Buf()]
        Bq, Bk, Bv, Bkt, Bz, Bvt = Buf("QT"), Buf("KT"), Buf("V"), Buf("KTOK"), Buf("ZS"), Buf("VT")
        Bpsh = [Buf() for _ in range(8)]
        pshc = [0]

        def psh_next():
            pt_, bp_ = self.ps()
            return pt_[:, 0:64].bitcast(BF16), bp_

        wsrc = self.wb_in[j]
        Bwsrc = self.Bw[("b_in", j)]
        cnt = [0]
        for hd in range(NH):
            for (c0, n, o) in ((hd * 128, 128, 0), (1024 + hd * 128, 128, 128), (2048 + hd * 256, 256, 256), (4096 + hd * 256, 256, 512)):
                self.DMA("sp", WH[:, :, o:o + n], wsrc[:, c0:c0 + n].rearrange("(k p) n -> p k n", p=128), [Bwsrc], [Bwh])
            for pas in range(2):
                for b2 in range(ntok // 256):
                    hb = b2 % 2
                    cols = slice(b2 * 256, (b2 + 1) * 256)
                    self.DMA("sp", HTb[:, hb], self.HTd[g][b2].rearrange("p (k n) -> p k n", n=256),
                             [self.BHT[g][b2 * 2], self.BHT[g][b2 * 2 + 1]], [Bh[hb]])
                    for cc in range(2):
                        wo = (pas * 2 + cc) * 128
                        pt, bp = self.ps()
                        for kc in range(8):
                            self.MM(pt[:, 0:256], WH[:, kc, wo:wo + 128], HTb[:, hb, kc, :], kc == 0, kc == 7, [Bwh, Bh[hb]], [bp])
                        self.CP("act", PRE[:, cc, cols], pt[:, 0:256], [bp], [BPRE[cc]])
                    if pas == 0:
                        for s2 in range(2):
                            t = b2 * 2 + s2
                            pt, bp = self.ps()
                            for kc in range(8):
                                self.MM(pt[:, 0:256], HTb[:, hb, kc, s2 * 128:(s2 + 1) * 128], WH[:, kc, 512:768], kc == 0, kc == 7,
                                        [Bwh, Bh[hb]], [bp])
                            self.ACT(ZS[:, t, :], pt[:, 0:256], AF.Silu, [bp], [Bz])
                import os
                kh = int(os.environ.get("KH", "99"))
                for cc in range(2 if kh >= 2 else 0):
                    kind = ("q", "k")[cc] if pas == 0 else "v"
                    cwi = (hd if kind == "q" else 8 + hd) if pas == 0 else 16 + hd * 2 + cc
                    wv = self.cw[:, cwi * 5:(cwi + 1) * 5]
                    for s_ in range(n_seq):
                        for a in range(0, L, 512):
                            bnd = min(a + 512, L)
                            n = bnd - a
                            base = s_ * L
                            self.TS("dve", ACC[:, 0:n], PRE[:, cc, base + a:base + bnd], wv[:, 2:3], None, ALU.mult, None,
                                    [BPRE[cc], self.Bbc], [Bacc])
                            for off in (-2, -1, 1, 2):
                                lo = max(a, -off) if off < 0 else a
                                hi = min(bnd, L - off) if off > 0 else bnd
                                if hi <= lo:
                                    continue
                                self.STT(ACC[:, lo - a:hi - a], PRE[:, cc, base + lo + off:base + hi + off], wv[:, off + 2:off + 3],
                                         ACC[:, lo - a:hi - a], ALU.mult, ALU.add, [BPRE[cc], self.Bbc, Bacc], [Bacc])
                            gcols = slice(base + a, base + bnd)
                            if kh < 3:
                                continue
                            if kind == "v":
                                self.ACT(VT[:, 0:n], ACC[:, 0:n], AF.Silu, [Bacc], [Bvt])
                                for q in range(n // 128):
                                    t = (base + a) // 128 + q
                                    ph, bph = psh_next()
                                    self.TR(ph, VT[:, q * 128:(q + 1) * 128], identb[:], [Bvt, self.Bc], [bph])
                                    self.CP("dve" if q % 2 else "pool" if False else "dve", V[:, t, cc * 128:(cc + 1) * 128], ph, [bph], [Bv])
                            elif kh >= 4:
                                self.ACT(A_[:, 0:n], ACC[:, 0:n], AF.Silu, [Bacc], [Bacc])
                                self.TT("pool", SQ[:, 0:n], A_[:, 0:n], A_[:, 0:n], ALU.mult, [Bacc], [Bsq])
                                pt, bp = self.ps()
                                self.MM(pt[:, 0:n], self.onesb[:], SQ[:, 0:n], True, True, [self.Bc, Bsq], [bp])
                                self.ACT(RN[:, 0:n], pt[:, 0:n], AF.Sqrt, [bp, self.Bc], [Bsq], bias=self.epsc[:, 0:1])
                                self.P.op("dve", (lambda o: lambda e: e.reciprocal(out=o, in_=o))(RN[:, 0:n]), [Bsq], [Bsq])
                                dst, Bd = (QT, Bq) if kind == "q" else (KT, Bk)
                                self.STT(dst[:, gcols], A_[:, 0:n], (DK ** -0.5) if kind == "q" else 1.0, RN[:, 0:n], ALU.mult, ALU.mult,
                                         [Bacc, Bsq], [Bd])
                                if kind == "k" and kh >= 5:
                                    for q in range(n // 128):
                                        t = (base + a) // 128 + q
                                        ph, bph = psh_next()
                                        self.TR(ph, KT[:, t * 128:(t + 1) * 128], identb[:], [Bk, self.Bc], [bph])
                                        self.CP("dve", KTOK[:, t, :], ph, [bph], [Bkt])
            if g == 0 and hd == 0:
                self.dump(FEAT[:, 0:512], 0, 512, [Bfeat])
                self.dump(QT[:, 0:512], 512, 512, [Bq])
                self.dump(KT[:, 0:512], 1024, 512, [Bk])
                self.dump(V[:, 0, :], 1536, 256, [Bv])
                self.dump(V[:, 1, :], 1792, 256, [Bv])
                self.dump(KTOK[:, 0, :], 2048, 128, [Bkt])
                self.dump(ZS[:, 0, :], 2176, 256, [Bz])
            if self.kb < 3:
                continue
            for s_ in range(n_seq):
                for d_ in range(2):
                    ch = s_ * 2 + d_
                    if g == 1:
                        self.DMA("sp", S8[:, ch, :], self.s0[j, d_, hd], [], [BS[ch]])
                    else:
                        self.MSET("dve", S8[:, ch, :], 0.0, [BS[ch]])
                    self.CP("act", Sb8[:, ch, :], S8[:, ch, :], [BS[ch]], [BS[ch]])
            self.P.barrier()
            for d_ in range(2):
                r = d_ * 32 + hd
                self.CP("dve", SELD[:, d_, 0, :], ident[:, r:r + 1].broadcast_to([128, 128]), [self.Bc], [Bsel])
                self.TT("dve", SELD[:, d_, 1, :], SELD[:, d_, 0, :], ident[:, 64 + r:64 + r + 1].broadcast_to([128, 128]), ALU.add,
                        [self.Bc, Bsel], [Bsel])
            visited = set()
            done = set()

            def visit(t, d_, ch, step, st_):
                r = d_ * 32 + hd
                tc = slice(t * 128, (t + 1) * 128)
                B_ = st_["B"]
                SC = st_["SCAL"]
                bank, bbk = self.psb[st_["qi"]], self.Bps[st_["qi"]]
                SEL1, SEL2 = SELD[:, d_, 0, :], SELD[:, d_, 1, :]
                GCJ = FTall[:, t, d_ * 8 + hd:d_ * 8 + hd + 1]
                LBJ = FTall[:, t, 16 + d_ * 8 + hd:16 + d_ * 8 + hd + 1]
                self.MM(bank[:, 0:128], SEL1, FEAT[:, tc], True, True, [Bsel, Bfeat], [bbk])
                self.MM(bank[:, 128:256], SEL2, FEAT[:, tc], True, True, [Bsel, Bfeat], [bbk])
                colT = t * 128 + (127 if d_ == 0 else 0)
                self.MM(bank[:, 256:257], SEL1, FEAT[:, colT:colT + 1], True, True, [Bsel, Bfeat], [bbk])
                yield
                self.CP("dve", SC[:, 2:3], bank[:, 256:257], [bbk], [B_])
                self.STT(st_["EX"], bank[:, 0:256], GCJ, self.negcat[:, d_, :], ALU.subtract, ALU.add, [bbk, B_, self.Bbc, self.Bft], [B_])
                self.ACT(st_["EG"], bank[:, 0:128], AF.Exp, [bbk], [B_])
                yield
                self.ACT(SC[:, 3:4], LBJ, AF.Exp, [B_, self.Bft], [B_])
                self.ACT(SC[:, 4:5], GCJ, AF.Exp, [B_, self.Bft], [B_], bias=LBJ)
                self.ACT(SC[:, 5:6], GCJ, AF.Exp, [B_, self.Bft], [B_], bias=SC[:, 2:3], scale=-1.0)
                self.ACT(SC[:, 6:7], SC[:, 2:3], AF.Exp, [B_], [B_])
                self.ACT(st_["EX"], st_["EX"], AF.Exp, [B_], [B_])
                self.MM(bank[:, 0:128], KT[:, tc], QT[:, tc], True, True, [Bk, Bq], [bbk])
                self.MM(bank[:, 128:256], KT[:, tc], KT[:, tc], True, True, [Bk], [bbk])
                yield
                self.TT("dve", st_["QKNT"], bank[:, 0:256], st_["EX"], ALU.mult, [bbk, B_], [B_])
                self.TT("pool", st_["QDT"], QT[:, tc], st_["EG"], ALU.mult, [Bq, B_], [st_["BQ"]])
                self.ACT(st_["KD"], KTOK[:, t, :], AF.Copy, [Bkt, B_], [st_["BQ"]], scale=SC[:, 5:6])
                self.ACT(st_["KBG"], KTOK[:, t, :], AF.Copy, [Bkt, B_], [st_["BQ"]], scale=SC[:, 4:5])
                yield
                N = st_["QKNT"][:, 128:256]
                ph = bank[:, 0:64].bitcast(BF16)
                self.TR(ph, N, identb[:], [B_, self.Bc], [bbk])
                yield
                self.CP("act", st_["NTt"], ph, [bbk], [B_])
                yield
                Am = st_["NTt"]
                LVS = self.lvs[:]
                BL = st_["BL"]

                NA = st_["NA"].rearrange("p (b n) -> p b n", n=128)
                LVS2 = self.lvs[:].unsqueeze(1).broadcast_to([128, 2, 128])
                ID2 = identb[:].unsqueeze(1).broadcast_to([128, 2, 128])

                def mk_l(k, lb):
                    self.STT(st_["LC"][:, lb, :].rearrange("p (b n) -> p b n", n=128), LVS2, float(k), NA, ALU.is_equal, ALU.mult,
                             [self.Bbc, B_], [BL])
                mk_l(0, 0)
                self.TT("dve", st_["DC"][:, 0, :].rearrange("p (b n) -> p b n", n=128), ID2,
                        st_["LC"][:, 0, :].rearrange("p (b n) -> p b n", n=128), ALU.subtract, [self.Bc, BL], [B_])
                dcur = 0
                for k in range(1, 7):
                    lb = k % 2
                    mk_l(k, lb)
                    yield
                    Dt_, D_ = st_["DC"][:, dcur, 0:128], st_["DC"][:, dcur, 128:256]
                    self.MM(bank[:, 0:128], st_["LC"][:, lb, 128:256], Dt_, True, True, [BL, B_], [bbk])
                    self.MM(bank[:, 128:256], st_["LC"][:, lb, 0:128], D_, True, True, [BL, B_], [bbk])
                    yield
                    self.CP("act", st_["MC"], bank[:, 0:256], [bbk], [B_])
                    yield
                    self.MM(bank[:, 0:128], D_, st_["MC"][:, 0:128], True, True, [B_], [bbk])
                    self.MM(bank[:, 128:256], Dt_, st_["MC"][:, 128:256], True, True, [B_], [bbk])
                    yield
                    self.TT("dve", st_["DC"][:, 1 - dcur, :], st_["DC"][:, dcur, :], bank[:, 0:256], ALU.subtract, [B_, bbk], [B_])
                    dcur = 1 - dcur
                    yield
                TTm = st_["DC"][:, dcur, 0:128]
                vb = cnt[0] % 2
                cnt[0] += 1
                self.ACT(VB2[:, vb, :], V[:, t, :], AF.Copy, [Bv, B_], [BVB[vb]], scale=SC[:, 3:4])
                self.MM(bank[:, 0:256], TTm, VB2[:, vb, :], True, True, [B_, BVB[vb]], [bbk])
                self.MM(bank[:, 256:384], st_["KBG"], TTm, True, True, [B_, st_["BQ"]], [bbk])
                yield
                self.CP("act", st_["U"], bank[:, 0:256], [bbk], [st_["BU"]])
                self.CP("dve", st_["WT"], bank[:, 256:384], [bbk], [st_["BU"]])
                yield
                while step > 0 and (ch, step - 1) not in done:
                    yield
                self.MM(bank[:, 0:256], st_["WT"], Sb8[:, ch, :], True, True, [st_["BU"], BS[ch]], [bbk])
                yield
                vn = cnt[0] % 2
                cnt[0] += 1
                self.TT("dve", VN2[:, vn, :], st_["U"], bank[:, 0:256], ALU.subtract, [st_["BU"], bbk], [BVN[vn]])
                self.MM(bank[:, 0:256], st_["QDT"], Sb8[:, ch, :], True, False, [st_["BQ"], BS[ch]], [bbk])
                self.MM(bank[:, 0:256], st_["QKNT"][:, 0:128], VN2[:, vn, :], False, True, [B_, BVN[vn]], [bbk])
                second = t in visited
                visited.add(t)
                if not second:
                    self.CP("act", O[:, t, :], bank[:, 0:256], [bbk], [BO[t]])
                else:
                    ob = cnt[0] % 2
                    cnt[0] += 1
                    OSb = OS2[:, ob, :]
                    self.TT("dve", OSb, bank[:, 0:256], O[:, t, :], ALU.add, [bbk, BO[t]], [BOS[ob]])
                self.MM(bank[:, 0:256], st_["KD"], VN2[:, vn, :], True, True, [st_["BQ"], BVN[vn]], [bbk])
                self.STT(S8[:, ch, :], S8[:, ch, :], SC[:, 6:7], bank[:, 0:256], ALU.mult, ALU.add, [BS[ch], B_, bbk], [BS[ch]])
                self.CP("act", Sb8[:, ch, :], S8[:, ch, :], [BS[ch]], [BS[ch]])
                done.add((ch, step))
                if second:
                    st6 = fst[:, 0:6]
                    mv = fst[:, 6:8]
                    ms = fst[:, 8:9]
                    self.P.op("dve", (lambda a_: lambda e: e.bn_stats(out=st6, in_=a_))(OSb), [BOS[ob]], [self.Bstat])
                    self.P.op("dve", lambda e: e.bn_aggr(out=mv, in_=st6), [self.Bstat], [self.Bstat])
                    self.STT(ms, mv[:, 0:1], mv[:, 0:1], mv[:, 1:2], ALU.mult, ALU.add, [self.Bstat], [self.Bstat])
                    self.ACT(ms, ms, AF.Sqrt, [self.Bstat, self.Bc], [self.Bstat], bias=self.epsc[:, 0:1])
                    self.P.op("dve", lambda e: e.reciprocal(out=ms, in_=ms), [self.Bstat], [self.Bstat])
                    self.STT(Y12[:, ob, :], OSb, ms, self.ngt[:], ALU.mult, ALU.mult, [BOS[ob], self.Bstat, self.Bbc], [BY[ob]])
                    self.TT("pool", YB2[:, ob, :], Y12[:, ob, :], ZS[:, t, :], ALU.mult, [BY[ob], Bz], [BY[ob]])
                    for e2 in range(2):
                        ph2 = bank[:, e2 * 64:(e2 + 1) * 64].bitcast(BF16)
                        self.TR(ph2, YB2[:, ob, e2 * 128:(e2 + 1) * 128], identb[:], [BY[ob], self.Bc], [bbk])
                        self.CP("dve", YT2[:, ob, e2 * 128:(e2 + 1) * 128], ph2, [bbk], [BY[ob]])
                    self.DMA("sp", self.YTd[g][hd * 2:hd * 2 + 2, :, tc].rearrange("e p n -> p e n"),
                             YT2[:, ob, :].rearrange("p (e n) -> p e n", n=128), [BY[ob]], [self.BYT[g][hd][t]])

            jobs = []
            for step in range(tps if self.kb >= 4 else 0):
                for s_ in range(n_seq):
                    for d_ in range(2):
                        t = s_ * tps + (step if d_ == 0 else tps - 1 - step)
                        jobs.append((t, d_, s_ * 2 + d_, step))
            free = list(range(NSET))
            active = []
            ji = 0
            while ji < len(jobs) or active:
                while ji < len(jobs) and free:
                    q = free.pop(0)
                    t, d_, ch, step = jobs[ji]
                    ji += 1
                    active.append((visit(t, d_, ch, step, sets[q]), q))
                nxt = []
                for gen, q in active:
                    try:
                        next(gen)
                        nxt.append((gen, q))
                    except StopIteration:
                        free.append(q)
                active = nxt
            self.P.barrier()
            if g == 0:
                for s_ in range(n_seq):
                    for d_ in range(2):
                        ch = s_ * 2 + d_
                        self.DMA("sp", self.ns[s_, j, d_, hd], S8[:, ch, :], [BS[ch]], [self.BNS])

    def b_outproj(self, i, j, g, G, last):
        sf, sbb = self.slabf, self.slabb
        c = G["cond"]
        WO = self.sb_wo[:]
        Bwo = Buf("wo")
        self.DMA("sp", WO, self.wb_out[j].rearrange("(k p) n -> p k n", p=128), [self.Bw[("b_out", j)]], [Bwo])
        XT2 = sf[:, 4096:6144].rearrange("p (b n) -> p b n", n=1024)
        R12 = sf[:, 6144:8192].rearrange("p (b n) -> p b n", n=1024)
        YT16 = sbb[:, 0:4096].rearrange("p (b e n) -> p b e n", e=16, n=128)
        Bx = [Buf(), Buf()]
        Br = [Buf(), Buf()]
        Byt = [Buf(), Buf()]
        for t in range(G["ntok"] // 128):
            b = t % 2
            tc = slice(t * 128, (t + 1) * 128)
            self.DMA("sp", XT2[:, b, :], self.X[g][tc, :], [self.BX[g][t]], [Bx[b]])
            self.DMA("sp", YT16[:, b], self.YTd[g][:, :, tc].rearrange("e p n -> p e n"), [self.BYT[g][h][t] for h in range(NH)], [Byt[b]])
            pts = []
            for half in range(2):
                pt, bp = self.ps()
                for ec in range(16):
                    self.MM(pt[:, :], YT16[:, b, ec, :], WO[:, ec, half * 512:(half + 1) * 512], ec == 0, ec == 15, [Byt[b], Bwo], [bp])
                pts.append((pt, bp))
            self.epilogue(i, g, c, t, pts, XT2[:, b, :], Bx[b], R12[:, b, :], R12[:, b, :], Br[b], last)


def build_program(NP, LP, LS, depth=4, dbg=False):
    import os
    dbg = dbg or bool(os.environ.get('KDBG'))
    b = Builder(NP, LP, LS, depth, dbg)
    b.wst = b.sb("wst", [128, 1024], BF16)
    b.BVG4 = [Buf() for _ in range(4)]
    b.BVH4 = [Buf() for _ in range(4)]
    return b.build()


def _pos_table(n_tokens, grid_w=64):
    def sincos(pos, dim):
        omega = (1.0 / (np.float32(10000.0) ** (np.arange(dim // 2, dtype=np.float32) / np.float32(dim // 2)))).astype(np.float32)
        ang = pos.astype(np.float32)[:, None] * omega[None, :]
        return np.concatenate([np.sin(ang), np.cos(ang)], axis=-1).astype(np.float32)
    rows = n_tokens // grid_w
    er = sincos(np.arange(rows), D // 2)
    ec = sincos(np.arange(grid_w), D // 2)
    emb = np.concatenate([np.broadcast_to(er[:, None, :], (rows, grid_w, D // 2)),
                          np.broadcast_to(ec[None, :, :], (rows, grid_w, D // 2))], axis=-1)
    return np.ascontiguousarray(emb.reshape(rows * grid_w, D), dtype=np.float32)


def prep_shared(inp, depth):
    nA, nB = (depth + 1) // 2, depth // 2
    f = lambda a: np.ascontiguousarray(a, dtype=np.float32)
    out = {}
    out["w_ada"] = f(inp["w_ada"][:depth])
    out["b_adaT"] = f(inp["b_ada"][:depth].reshape(depth, 24, 128).transpose(2, 0, 1).reshape(128, depth * 24))
    out["ln_g"] = f(inp["ln_g"][:depth])
    out["ln_b"] = f(inp["ln_b"][:depth])
    out["a_w_in"] = f(inp["a_w_in"][:nA])
    out["a_ln_gT"] = f(inp["a_ln_g"][:nA].reshape(nA, 16, 128).transpose(2, 0, 1).reshape(128, nA * 16))
    out["a_ln_bT"] = f(inp["a_ln_b"][:nA].reshape(nA, 16, 128).transpose(2, 0, 1).reshape(128, nA * 16))
    out["a_w_sT"] = f(inp["a_w_s"][:nA].transpose(0, 3, 1, 2).reshape(nA, 128, 8 * 128))
    out["a_b_s"] = f(inp["a_b_s"][:nA].reshape(nA, 8 * 128))
    out["a_w_out"] = f(inp["a_w_out"][:nA])
    nb = max(nB, 1)
    out["b_w_in"] = f(inp["b_w_in"][:nb])
    out["b_convT"] = f(inp["b_conv_w"][:nb].reshape(nb, 5, 32, 128).transpose(3, 0, 2, 1).reshape(128, nb * 32 * 5))
    al = np.zeros((nb, 128, 1), np.float32)
    db = np.zeros((nb, 128, 1), np.float32)
    for j in range(nb):
        for d_ in range(2):
            al[j, d_ * 32:d_ * 32 + 8, 0] = inp["b_A_log"][j, d_]
            db[j, d_ * 32:d_ * 32 + 8, 0] = inp["b_dt_bias"][j, d_]
    out["b_alogP"] = al
    out["b_dtbP"] = db
    out["b_norm_g"] = f(inp["b_norm_g"][:nb])
    pm = np.zeros((128, 4), np.float32)
    pm[0:32, 0] = 1.0
    pm[32:64, 1] = 1.0
    pm[64:128, 2] = 1.0
    out["pmask"] = pm
    ii = np.arange(128)
    xr = ii[:, None] ^ ii[None, :]
    lv = np.full((128, 128), -1.0, np.float32)
    nz = xr > 0
    lv[nz] = np.floor(np.log2(xr[nz])).astype(np.float32)
    out["lvs"] = lv
    out["b_w_out"] = f(inp["b_w_out"][:nb])
    return out


def prep_core(inp, shared, core, NP, LP, LS, depth, n_per_group):
    nb = max(depth // 2, 1)
    b = core // n_per_group
    f = lambda a: np.ascontiguousarray(a, dtype=np.float32)
    m = dict(shared)
    m["xp"] = f(inp["x_prompt"][core * NP:(core + 1) * NP].reshape(NP * LP, D))
    m["xs"] = f(inp["x_sample"][b])
    m["pos"] = _pos_table(LS)
    cond2 = np.stack([inp["c_ctx"], inp["c"][b]], axis=0)
    m["condT"] = f(cond2.reshape(2, 8, 128).transpose(2, 1, 0).reshape(128, 16))
    m["s0"] = f(inp["state_delta"][b][:nb])
    return m


_CACHE = {}


def kernel(**inputs):
    inp = {k: np.asarray(v) for k, v in inputs.items()}
    NP, LP, LS, depth = 4, 256, 4096, 4
    key = (NP, LP, LS, depth)
    if key not in _CACHE:
        _CACHE[key] = build_program(NP, LP, LS, depth)
    nc = _CACHE[key]
    shared = prep_shared(inp, depth)
    in_maps = [prep_core(inp, shared, c, NP, LP, LS, depth, 4) for c in range(8)]
    res = run_bass_kernel_spmd(nc, in_maps, core_ids=list(range(8)))
    r = res.results
    y_prompt = np.concatenate([r[c]["yp"].reshape(NP, LP, D) for c in range(8)], axis=0)
    y_sample = np.stack([r[0]["ys"], r[4]["ys"]], axis=0)
    ns = np.concatenate([r[c]["ns"] for c in range(8)], axis=0)
    return (y_prompt.astype(np.float32), y_sample.astype(np.float32), ns.astype(np.float32))
```

```python
import numpy as np
from contextlib import ExitStack
import concourse.bass as bass
import concourse.mybir as mybir
from concourse.bass_utils import run_bass_kernel_spmd

F32 = mybir.dt.float32
BF16 = mybir.dt.bfloat16
AF = mybir.ActivationFunctionType
ALU = mybir.AluOpType

D = 1024
E = 2048
KW = 1024
DK = 128
DV = 256
NH = 8
BW = 6176
ALPHA = (2.0 * 4) ** 0.25
EPS = 1e-6
NEG = -1.0e30


class Buf:
    __slots__ = ("name", "w", "r", "x")

    def __init__(self, name="", x=False):
        self.name = name
        self.w = None
        self.r = {}
        self.x = x


class Prog:
    ENGS = ("pe", "dve", "act", "pool", "sp")
    NSLOT = 8
    SAME_ENG_SKIP = 12

    def __init__(self):
        self.ops = {e: [] for e in self.ENGS}
        self.cnt = {e: 0 for e in self.ENGS}
        self.waited = {e: {} for e in self.ENGS}
        self.dcnt = {e: 0 for e in self.ENGS}

    def _deps(self, eng, reads, writes):
        deps = {}

        def add(p):
            if p is None:
                return
            k = p[0]
            if k not in deps or deps[k][1] < p[1]:
                deps[k] = p
        for b in reads:
            add(b.w)
        for b in writes:
            add(b.w)
            for p in b.r.values():
                add(p)
        waits = []
        for k, p in deps.items():
            val = p[1]
            if k == eng and (eng == "pe" or self.cnt[eng] - p[3] > self.SAME_ENG_SKIP):
                continue
            if self.waited[eng].get(k, 0) >= val:
                continue
            self.waited[eng][k] = val
            waits.append((k, val))
        return waits

    def _record(self, reads, writes, prod):
        k = prod[0]
        for b in reads:
            if k not in b.r or b.r[k][1] < prod[1]:
                b.r[k] = prod
        for b in writes:
            b.w = prod
            b.r = {}

    def op(self, eng, fn, reads=(), writes=()):
        xs = [b for b in reads if b.x]
        if xs:
            writes = list(writes) + xs
        waits = self._deps(eng, reads, writes)
        idx = self.cnt[eng]
        self.cnt[eng] = idx + 1
        self.ops[eng].append((waits, fn, (eng, 1)))
        self._record(reads, writes, (eng, idx + 1, eng, idx))

    def dma(self, q, fn, reads=(), writes=()):
        waits = self._deps(q, reads, writes)
        i = self.dcnt[q]
        self.dcnt[q] = i + 1
        k = ("dma", q, i % self.NSLOT)
        prev = 16 * (i // self.NSLOT)
        if prev > 0 and self.waited[q].get(k, 0) < prev:
            self.waited[q][k] = prev
            waits.append((k, prev))
        self.ops[q].append((waits, fn, (k, 16)))
        self._record(reads, writes, (k, prev + 16, None, None))

    def barrier(self):
        tgt = [(e, self.cnt[e]) for e in self.ENGS if self.cnt[e] > 0]
        for q in self.ENGS:
            n = self.dcnt[q]
            for slot in range(min(n, self.NSLOT)):
                tgt.append((("dma", q, slot), 16 * ((n - 1 - slot) // self.NSLOT + 1)))
        for e in self.ENGS:
            waits = []
            for k, v in tgt:
                if k == e:
                    continue
                if self.waited[e].get(k, 0) >= v:
                    continue
                self.waited[e][k] = v
                waits.append((k, v))
            if waits:
                self.ops[e].append((waits, None, None))

    def replay(self, nc):
        semkeys = list(self.ENGS)
        for q in self.ENGS:
            for s in range(min(self.dcnt[q], self.NSLOT)):
                semkeys.append(("dma", q, s))
        with ExitStack() as st:
            sems = {}
            for k in semkeys:
                nm = k if isinstance(k, str) else f"d_{k[1]}_{k[2]}"
                sems[k] = st.enter_context(nc.semaphore("s_" + nm))
            block = st.enter_context(nc.Block())
            engmap = {"pe": "tensor", "dve": "vector", "act": "scalar", "pool": "gpsimd", "sp": "sync"}

            def mk(ename):
                oplist = self.ops[ename]

                def body(e):
                    for waits, fn, inc in oplist:
                        for k, v in waits:
                            e.wait_ge(sems[k], v)
                        if fn is not None:
                            fn(e).then_inc(sems[inc[0]], inc[1])
                return body

            for ename in self.ENGS:
                if self.ops[ename]:
                    getattr(block, engmap[ename])(mk(ename))


class Builder:
    def __init__(self, NP, LP, LS, depth=4, dbg=False):
        self.NP, self.LP, self.LS, self.depth = NP, LP, LS, depth
        self.nc = nc = bass.Bass("TRN2", target_bir_lowering=False)
        self.P = Prog()
        self.st = ExitStack()
        self.groups = [dict(n_seq=NP, L=LP, cond=0, ntok=NP * LP), dict(n_seq=1, L=LS, cond=1, ntok=LS)]
        nA = (depth + 1) // 2
        nB = depth // 2
        self.nA, self.nB = nA, nB

        def din(name, shape, dt=F32):
            return nc.dram_tensor(name, list(shape), dt, kind="ExternalInput").ap()

        def dout(name, shape, dt=F32):
            return nc.dram_tensor(name, list(shape), dt, kind="ExternalOutput").ap()

        def dint(name, shape, dt=F32):
            return nc.dram_tensor(name, list(shape), dt, kind="Internal").ap()

        self.xin = [din("xp", [NP * LP, D]), din("xs", [LS, D])]
        self.pos = din("pos", [LS, D])
        self.condT = din("condT", [128, 16])
        self.s0 = din("s0", [max(nB, 1), 2, NH, DK, DV])
        self.w_ada = din("w_ada", [depth, D, 3 * D])
        self.b_adaT = din("b_adaT", [128, depth * 24])
        self.ln_g = din("ln_g", [depth, D])
        self.ln_b = din("ln_b", [depth, D])
        self.a_w_in = din("a_w_in", [nA, D, 3 * E])
        self.a_ln_gT = din("a_ln_gT", [128, nA * 16])
        self.a_ln_bT = din("a_ln_bT", [128, nA * 16])
        self.a_w_sT = din("a_w_sT", [nA, 128, 8 * 128])
        self.a_b_s = din("a_b_s", [nA, 8 * 128])
        self.a_w_out = din("a_w_out", [nA, E, D])
        self.b_w_in = din("b_w_in", [max(nB, 1), D, BW])
        self.b_convT = din("b_convT", [128, max(nB, 1) * 32 * 5])
        self.b_alogP = din("b_alogP", [max(nB, 1), 128, 1])
        self.b_dtbP = din("b_dtbP", [max(nB, 1), 128, 1])
        self.b_norm_g = din("b_norm_g", [max(nB, 1), DV])
        self.pmask_d = din("pmask", [128, 4])
        self.lvs_d = din("lvs", [128, 128])
        self.b_w_out = din("b_w_out", [max(nB, 1), E, D])
        self.yout = [dout("yp", [NP * LP, D]), dout("ys", [LS, D])]
        self.ns = dout("ns", [NP, max(nB, 1), 2, NH, DK, DV])
        self.X = [dint("X0", [NP * LP, D]), dint("X1", [LS, D])]
        self.wa_in = dint("wa_in", [nA, D, 3 * E], BF16)
        self.wa_out = dint("wa_out", [nA, E, D], BF16)
        self.wb_in = dint("wb_in", [max(nB, 1), D, BW], BF16)
        self.wb_out = dint("wb_out", [max(nB, 1), E, D], BF16)
        self.HTd = [dint("HTd0", [NP * LP // 256, 128, 2048], BF16), dint("HTd1", [LS // 256, 128, 2048], BF16)]
        self.YTd = [dint("YTd0", [16, 128, NP * LP], BF16), dint("YTd1", [16, 128, LS], BF16)]
        self.dbg = None
        if dbg:
            self.dbg = dout("dbg", [128, 4096])
        self.Bw = {}
        self.BX = [[Buf(f"X{g}_{t}") for t in range(self.groups[g]["ntok"] // 128)] for g in range(2)]
        self.BHT = [[Buf() for _ in range(self.groups[g]["ntok"] // 128)] for g in range(2)]
        self.BYT = [[[Buf() for _ in range(self.groups[g]["ntok"] // 128)] for _h in range(NH)] for g in range(2)]
        self.BNS = Buf("ns")
        self.BYO = Buf("yout")
        self._ps_i = 0

    def sb(self, name, shape, dt):
        return self.st.enter_context(self.nc.sbuf_tensor(name, list(shape), dt))

    def MM(self, out, lhsT, rhs, start, stop, R, W):
        self.P.op("pe", lambda e: e.matmul(out, lhsT=lhsT, rhs=rhs, start=start, stop=stop), R, W)

    def TR(self, out, in_, ident, R, W):
        self.P.op("pe", lambda e: e.transpose(out, in_, ident), R, W)

    def ACT(self, out, in_, func, R, W, bias=None, scale=None, accum=None):
        kw = {}
        if bias is not None:
            kw["bias"] = bias
        if scale is not None:
            kw["scale"] = scale
        if accum is not None:
            kw["accum_out"] = accum
        self.P.op("act", lambda e: e.activation(out=out, in_=in_, func=func, **kw), R, W)

    def TS(self, eng, out, in0, s1, s2, op0, op1, R, W):
        if op1 is None:
            self.P.op(eng, lambda e: e.tensor_scalar(out=out, in0=in0, scalar1=s1, scalar2=None, op0=op0), R, W)
        else:
            self.P.op(eng, lambda e: e.tensor_scalar(out=out, in0=in0, scalar1=s1, scalar2=s2, op0=op0, op1=op1), R, W)

    def STT(self, out, in0, scalar, in1, op0, op1, R, W):
        self.P.op("dve", lambda e: e.scalar_tensor_tensor(out=out, in0=in0, scalar=scalar, in1=in1, op0=op0, op1=op1), R, W)

    def TT(self, eng, out, in0, in1, op, R, W):
        self.P.op(eng, lambda e: e.tensor_tensor(out=out, in0=in0, in1=in1, op=op), R, W)

    def CP(self, eng, out, in_, R, W):
        if eng == "act":
            self.ACT(out, in_, AF.Copy, R, W)
        else:
            self.P.op(eng, lambda e: e.tensor_copy(out=out, in_=in_), R, W)

    def MSET(self, eng, ap, val, W):
        self.P.op(eng, lambda e: e.memset(ap, val), (), W)

    def DMA(self, q, out, in_, R, W):
        self.P.dma(q, lambda e: e.dma_start(out=out, in_=in_), R, W)

    def dump(self, src, col0, n, R):
        if self.dbg is None:
            return
        if not hasattr(self, "dbgst"):
            self.dbgst = self.sb("dbgst", [128, 512], F32)
            self.Bdbg = Buf("dbg")
        self.CP("act", self.dbgst[:, 0:n], src, R, [self.Bdbg])
        self.DMA("sp", self.dbg[:, col0:col0 + n], self.dbgst[:, 0:n], [self.Bdbg], [])

    def ps(self):
        i = self._ps_i
        self._ps_i = (i + 1) % len(self.psb)
        return self.psb[i], self.Bps[i]

    def build(self):
        nc = self.nc
        sb = self.sb
        self.psb = [self.st.enter_context(nc.psum_tensor(f"ps{i}", [128, 512], F32)) for i in range(8)]
        self.Bps = [Buf(f"ps{i}", x=True) for i in range(8)]
        self.ident = sb("ident", [128, 128], F32)
        self.identb = sb("identb", [128, 128], BF16)
        self.onesf = sb("onesf", [128, 128], F32)
        self.onesb = sb("onesb", [128, 128], BF16)
        self.Bc = Buf("consts")
        self.MSET("pool", self.onesf[:], 1.0, [self.Bc])
        self.P.op("pool", lambda e: e.affine_select(out=self.ident[:], in_=self.onesf[:], pattern=[[-1, 128]],
                                                     compare_op=ALU.is_equal, fill=0.0, base=0, channel_multiplier=1),
                  [self.Bc], [self.Bc])
        self.CP("dve", self.identb[:], self.ident[:], [self.Bc], [self.Bc])
        self.CP("dve", self.onesb[:], self.onesf[:], [self.Bc], [self.Bc])
        self.slabf = sb("slabf", [128, 12 * 1024], F32)
        self.slabb = sb("slabb", [128, 49 * 1024], BF16)
        self.sb_wo = sb("WO", [128, 16, 1024], BF16)
        self.MOD = sb("MOD", [128, self.depth * 48], F32)
        self.SC1 = sb("SC1", [128, self.depth * 16], F32)
        self.GATE = sb("GATE", [128, 2 * D], F32)
        self.LNG = sb("LNG", [128, D], F32)
        self.LNB = sb("LNB", [128, D], F32)
        self.BGATE = Buf("gate")
        self.BLN = Buf("ln")
        self.BMOD = Buf("mod")

        import os
        stage = int(os.environ.get("KSTAGE", "99"))
        if stage >= 1:
            self.weight_casts()
        self.small_consts()
        if stage >= 2:
            self.prologue()
            self.P.barrier()
        if stage >= 3:
            self.adaln()
        for i in range(self.depth):
            if stage < 4:
                break
            self.P.barrier()
            self.layer_consts(i)
            self.P.barrier()
            if stage < 5:
                break
            if i % 2 == 0:
                self.layer_a(i)
            else:
                self.layer_b(i)
        self.P.barrier()
        self.P.replay(nc)
        self.st.close()
        return nc

    def weight_casts(self):
        for nm, src, dst, n, rows in (("a_in", self.a_w_in, self.wa_in, self.nA, D), ("a_out", self.a_w_out, self.wa_out, self.nA, E),
                                      ("b_in", self.b_w_in, self.wb_in, self.nB, D), ("b_out", self.b_w_out, self.wb_out, self.nB, E)):
            for l in range(n):
                b = Buf(f"w_{nm}{l}")
                self.Bw[(nm, l)] = b
                for r0 in range(0, rows, 256):
                    self.DMA("pool", dst[l, r0:r0 + 256, :], src[l, r0:r0 + 256, :], [], [b])

    def adaln(self):
        P = self.P
        sf = self.slabf
        cT = sf[:, 0:16]
        sT = sf[:, 16:32]
        bT = sf[:, 32:32 + self.depth * 24]
        W = sf[:, 1024:1024 + 8192].rearrange("p (k n) -> p k n", n=1024)
        Bs = Buf("ada_s")
        Bb = Buf("ada_b")
        BW_ = Buf("adaw")
        self.DMA("sp", cT, self.condT[:, :], [], [Bs])
        self.DMA("sp", bT, self.b_adaT[:, :], [], [Bb])
        self.ACT(sT, cT, AF.Silu, [Bs], [Bs])
        for i in range(self.depth):
            pt, bp = self.ps()
            for part in range(3):
                self.DMA("sp", W, self.w_ada[i, :, part * 1024:(part + 1) * 1024].rearrange("(k p) n -> p k n", p=128), [], [BW_])
                for c8 in range(8):
                    ch = part * 8 + c8
                    for kc in range(8):
                        self.MM(pt[:, ch * 2:ch * 2 + 2], W[:, kc, c8 * 128:(c8 + 1) * 128], sT[:, kc * 2:kc * 2 + 2],
                                kc == 0, kc == 7, [BW_, Bs], [bp])
            mod = self.MOD[:, i * 48:(i + 1) * 48].rearrange("p (ch c) -> p ch c", c=2)
            self.TT("dve", mod, pt[:, 0:48].rearrange("p (ch c) -> p ch c", c=2),
                    bT[:, i * 24:(i + 1) * 24].unsqueeze(2).broadcast_to([128, 24, 2]), ALU.add, [bp, Bb], [self.BMOD])
            self.TS("dve", self.SC1[:, i * 16:(i + 1) * 16], self.MOD[:, i * 48 + 16:i * 48 + 32], 1.0, None, ALU.add, None,
                    [self.BMOD], [self.BMOD])
        if self.dbg is not None:
            self.DMA("sp", self.dbg[:, 0:48 * self.depth], self.MOD[:, :], [self.BMOD], [])
            self.DMA("sp", self.dbg[:, 512:512 + 16 * self.depth], self.SC1[:, :], [self.BMOD], [])

    def shift_ap(self, i, kc, c):
        o = i * 48 + kc * 2 + c
        return self.MOD[:, o:o + 1]

    def scale1_ap(self, i, kc, c):
        o = i * 16 + kc * 2 + c
        return self.SC1[:, o:o + 1]

    def layer_consts(self, i):
        gb = self.slabf[:, 0:128]
        Bg = Buf("gb")
        for c in range(2):
            for half in range(2):
                pt, bp = self.ps()
                for q in range(4):
                    kc = half * 4 + q
                    o = i * 48 + (16 + kc) * 2 + c
                    self.CP("dve", gb, self.MOD[:, o:o + 1].broadcast_to([128, 128]), [self.BMOD], [Bg])
                    self.MM(pt[:, q * 128:(q + 1) * 128], gb, self.ident[:], True, True, [Bg, self.Bc], [bp])
                self.CP("act", self.GATE[:, c * D + half * 512:c * D + (half + 1) * 512], pt[:, :], [bp], [self.BGATE])
        self.DMA("sp", self.LNG[:], self.ln_g[i:i + 1, :].partition_broadcast(128), [], [self.BLN])
        self.DMA("sp", self.LNB[:], self.ln_b[i:i + 1, :].partition_broadcast(128), [], [self.BLN])

    def prologue(self):
        sf = self.slabf
        A = sf[:, 0:4096].rearrange("p (s d) -> p s d", d=1024)
        Bq = sf[:, 4096:8192].rearrange("p (s d) -> p s d", d=1024)
        Ba, Bb = Buf("pa"), Buf("pb")
        for t4 in range(self.LS // 512):
            rows = slice(t4 * 512, (t4 + 1) * 512)
            self.DMA("sp", A, self.xin[1][rows, :].rearrange("(s p) d -> p s d", p=128), [], [Ba])
            self.DMA("sp", Bq, self.pos[rows, :].rearrange("(s p) d -> p s d", p=128), [], [Bb])
            self.TT("dve", A, A, Bq, ALU.add, [Ba, Bb], [Ba])
            self.DMA("pool", self.X[1][rows, :].rearrange("(s p) d -> p s d", p=128), A, [Ba],
                     [self.BX[1][t4 * 4 + s] for s in range(4)])

    def load_x_tile(self, i, g, t, XT, Bx, nsub):
        first = (i == 0 and g == 0)
        src = self.xin[g] if first else self.X[g]
        rows = slice(t * 128, (t + nsub) * 128)
        R = [] if first else [self.BX[g][t + s] for s in range(nsub)]
        self.DMA("sp", XT, src[rows, :].rearrange("(s p) d -> p s d", p=128), R, [Bx])

    def make_ht(self, i, c, XT, Bx, nsub, HT, Bh):
        for kc in range(8):
            pt, bp = self.ps()
            for s in range(nsub):
                self.TR(pt[:, s * 128:(s + 1) * 128], XT[:, s, kc * 128:(kc + 1) * 128], self.ident[:], [Bx, self.Bc], [bp])
            self.TS("dve", HT[:, kc, 0:nsub * 128], pt[:, 0:nsub * 128], self.scale1_ap(i, kc, c), self.shift_ap(i, kc, c),
                    ALU.mult, ALU.add, [bp, self.BMOD], [Bh])

    def epilogue(self, i, g, c, t, pts, xrow, Bx, R1, XN, Br, last):
        import os
        kep = int(os.environ.get("KEP", "99"))
        if kep < 1:
            return
        for half in range(2):
            pt, bp = pts[half]
            hs = slice(half * 512, (half + 1) * 512)
            self.TT("dve", R1[:, hs], pt[:, :], self.GATE[:, c * D + half * 512:c * D + (half + 1) * 512], ALU.mult,
                    [bp, self.BGATE], [Br])
        self.STT(R1, xrow, ALPHA, R1, ALU.mult, ALU.add, [Bx, Br], [Br])
        if kep < 2:
            return
        st6 = self.stat[:, 0:12]
        mv = self.stat[:, 12:14]
        rs = self.stat[:, 14:15]
        for half in range(2):
            self.P.op("dve", (lambda o, a: lambda e: e.bn_stats(out=o, in_=a))(st6[:, half * 6:(half + 1) * 6], R1[:, half * 512:(half + 1) * 512]),
                      [Br], [self.Bstat])
        self.P.op("dve", lambda e: e.bn_aggr(out=mv, in_=st6), [self.Bstat], [self.Bstat])
        self.ACT(rs, mv[:, 1:2], AF.Sqrt, [self.Bstat], [self.Bstat], bias=self.epsc[:, 0:1])
        self.P.op("dve", lambda e: e.reciprocal(out=rs, in_=rs), [self.Bstat], [self.Bstat])
        if kep < 3:
            return
        self.TS("dve", XN, R1, mv[:, 0:1], rs, ALU.subtract, ALU.mult, [Br, self.Bstat], [Br])
        if kep < 4:
            return
        self.TT("pool", XN, XN, self.LNG[:], ALU.mult, [Br, self.BLN], [Br])
        self.TT("pool", XN, XN, self.LNB[:], ALU.add, [Br, self.BLN], [Br])
        if kep < 5:
            return
        if last:
            self.DMA("sp", self.yout[g][t * 128:(t + 1) * 128, :], XN, [Br], [self.BYO])
        else:
            self.DMA("sp", self.X[g][t * 128:(t + 1) * 128, :], XN, [Br], [self.BX[g][t]])

    def small_consts(self):
        if hasattr(self, "stat"):
            return
        self.stat = self.sb("stat", [128, 16], F32)
        self.Bstat = Buf("stat")
        self.epsc = self.sb("epsc", [128, 2], F32)
        self.MSET("dve", self.epsc[:, 0:1], EPS, [self.Bc])
        self.MSET("dve", self.epsc[:, 1:2], 1.0, [self.Bc])

    def layer_a(self, i):
        self.small_consts()
        l = i // 2
        last = (i == self.depth - 1)
        sf, sbb = self.slabf, self.slabb
        XT = sf[:, 0:4096].rearrange("p (s d) -> p s d", d=1024)
        BIAS = sf[:, 4096:6144].rearrange("p (c q) -> p c q", q=128)
        R1 = sf[:, 6144:8192].rearrange("p (b n) -> p b n", n=1024)
        T1 = sf[:, 8192:9216].rearrange("p (b n) -> p b n", n=512)
        WSF = sf[:, 9216:10240]
        BSB = sf[:, 10240:11264].rearrange("p (g q) -> p g q", q=128)
        RS = sf[:, 11264:11392]
        GLN = sf[:, 11392:11408]
        BLNv = sf[:, 11408:11424]
        vst = sf[:, 11424:11424 + 64]
        HT = sbb[:, 0:4096].rearrange("p (k n) -> p k n", n=512)
        U = sbb[:, 4096:12288].rearrange("p (c n) -> p c n", n=512)
        Z = sbb[:, 12288:20480].rearrange("p (c n) -> p c n", n=512)
        VGa = sbb[:, 20480:28672].rearrange("p (b n) -> p b n", n=2048)
        VHa = sbb[:, 28672:36864].rearrange("p (b n) -> p b n", n=2048)
        WB = sbb[:, 36864:49152].rearrange("p (b k n) -> p b k n", k=8, n=512)
        WST = self.wst[:].rearrange("p (g q) -> p g q", q=128)
        WO = self.sb_wo[:]
        self.VGa, self.VHa = VGa, VHa
        Bxt, Bht, Bwst, Bbias, Bwo = Buf("XT"), Buf("HT"), Buf("WST"), Buf("BIAS"), Buf("WO")
        BU = [Buf() for _ in range(16)]
        BZ = [Buf() for _ in range(16)]
        BWB = [Buf(), Buf(), Buf()]
        BR = [Buf(), Buf()]
        BT1 = [Buf(), Buf()]
        Bsm = Buf("small")
        self.DMA("sp", WSF, self.a_w_sT[l, :, :], [], [Bwst])
        self.CP("dve", WST.rearrange("p g q -> p (g q)"), WSF, [Bwst], [Bwst])
        self.DMA("sp", GLN, self.a_ln_gT[:, l * 16:(l + 1) * 16], [], [Bsm])
        self.DMA("sp", BLNv, self.a_ln_bT[:, l * 16:(l + 1) * 16], [], [Bsm])
        self.DMA("sp", BSB.rearrange("p g q -> p (g q)"), self.a_b_s[l:l + 1, :].partition_broadcast(128), [], [Bsm])
        for gq in range(8):
            pt, bp = self.ps()
            self.MM(pt[:, 0:128], self.onesb[:], WST[:, gq, :], True, True, [self.Bc, Bwst], [bp])
            self.CP("act", RS, pt[:, 0:128], [bp], [Bsm])
            for cc in range(2):
                ch = gq * 2 + cc
                self.STT(BIAS[:, ch, :], RS, BLNv[:, ch:ch + 1], BSB[:, gq, :], ALU.mult, ALU.add, [Bsm], [Bbias])
        import os
        sub = int(os.environ.get("KSUB", "99"))
        if sub < 1:
            return
        self.DMA("sp", WO, self.wa_out[l].rearrange("(k p) n -> p k n", p=128), [self.Bw[("a_out", l)]], [Bwo])
        wsrc = self.wa_in[l]
        wcnt = [0]

        def load_w(col0):
            b = wcnt[0] % 3
            wcnt[0] += 1
            self.DMA("sp", WB[:, b, :, :], wsrc[:, col0:col0 + 512].rearrange("(k p) n -> p k n", p=128),
                     [self.Bw[("a_in", l)]], [BWB[b]])
            return b

        for g, G in enumerate(self.groups):
            c = G["cond"]
            for t4 in range(G["ntok"] // 512):
                t = t4 * 4
                self.load_x_tile(i, g, t, XT, Bxt, 4)
                self.make_ht(i, c, XT, Bxt, 4, HT, Bht)
                if sub < 2:
                    continue
                blocks = [("v", 2048 + b * 512, b) for b in range(4)] + [("u", b * 512, b) for b in range(4)] + \
                         [("z", 4096 + b * 512, b) for b in range(4)]
                nxt = load_w(blocks[0][1])
                for bi, (kind, col0, b4) in enumerate(blocks):
                    wb = nxt
                    if bi + 1 < len(blocks):
                        nxt = load_w(blocks[bi + 1][1])
                    if kind == "v":
                        for s in range(4):
                            pt, bp = self.ps()
                            for kc in range(8):
                                self.MM(pt[:, :], HT[:, kc, s * 128:(s + 1) * 128], WB[:, wb, kc, :], kc == 0, kc == 7,
                                        [Bht, BWB[wb]], [bp])
                            self.ACT(self._vg(sf, s)[:, b4 * 512:(b4 + 1) * 512],
                                     pt[:, :], AF.Gelu_apprx_tanh, [bp], [self.BVG4[s]])
                    else:
                        dst, Bd, fn = (U, BU, AF.Gelu_apprx_tanh) if kind == "u" else (Z, BZ, AF.Silu)
                        for cc in range(4):
                            ch = b4 * 4 + cc
                            pt, bp = self.ps()
                            for kc in range(8):
                                self.MM(pt[:, :], WB[:, wb, kc, cc * 128:(cc + 1) * 128], HT[:, kc, :], kc == 0, kc == 7,
                                        [Bht, BWB[wb]], [bp])
                            self.ACT(dst[:, ch, :], pt[:, :], fn, [bp], [Bd[ch]])
                    if kind == "v" and b4 == 3:
                        for s in range(4):
                            vg = self._vg(sf, s)
                            st = vst[:, 0:24]
                            mv = vst[:, 24:26]
                            rs = vst[:, 26:27]
                            for q in range(4):
                                self.P.op("dve", (lambda o, a: lambda e: e.bn_stats(out=o, in_=a))(st[:, q * 6:(q + 1) * 6], vg[:, q * 512:(q + 1) * 512]),
                                          [self.BVG4[s]], [Bsm])
                            self.P.op("dve", lambda e: e.bn_aggr(out=mv, in_=st), [Bsm], [Bsm])
                            self.ACT(rs, mv[:, 1:2], AF.Sqrt, [Bsm], [Bsm], bias=self.epsc[:, 0:1])
                            self.P.op("dve", lambda e: e.reciprocal(out=rs, in_=rs), [Bsm], [Bsm])
                            self.TS("dve", self._vh(sbb, s), vg, mv[:, 0:1], rs, ALU.subtract, ALU.mult, [self.BVG4[s], Bsm], [self.BVH4[s]])
                if sub < 3:
                    continue
                for ch in range(16):
                    self.TT("pool", U[:, ch, :], U[:, ch, :], Z[:, ch, :], ALU.mult, [BU[ch], BZ[ch]], [BU[ch]])
                for ch in range(16):
                    pt, bp = self.ps()
                    for s in range(4):
                        self.MM(pt[:, s * 128:(s + 1) * 128], self._vh(sbb, s)[:, ch * 128:(ch + 1) * 128], WST[:, ch // 2, :], True, True,
                                [self.BVH4[s], Bwst], [bp])
                    tb = ch % 2
                    self.STT(T1[:, tb, :].rearrange("p (s q) -> p s q", q=128), pt[:, :].rearrange("p (s q) -> p s q", q=128),
                             GLN[:, ch:ch + 1], BIAS[:, ch, :].unsqueeze(1).broadcast_to([128, 4, 128]), ALU.mult, ALU.add,
                             [bp, Bsm, Bbias], [BT1[tb]])
                    self.TT("pool", Z[:, ch, :], T1[:, tb, :], U[:, ch, :], ALU.mult, [BT1[tb], BU[ch]], [BZ[ch]])
                if sub < 4:
                    continue
                for s in range(4):
                    pts = []
                    for half in range(2):
                        pt, bp = self.ps()
                        for ec in range(16):
                            self.MM(pt[:, :], Z[:, ec, s * 128:(s + 1) * 128], WO[:, ec, half * 512:(half + 1) * 512], ec == 0, ec == 15,
                                    [BZ[ec], Bwo], [bp])
                        pts.append((pt, bp))
                    rb = s % 2
                    self.epilogue(i, g, c, t + s, pts, XT[:, s, :], Bxt, R1[:, rb, :], R1[:, rb, :], BR[rb], last)

    def _vg(self, sf, s):
        return self.VGa[:, s, :]

    def _vh(self, sbb, s):
        return self.VHa[:, s, :]

    def layer_b_consts(self, j):
        if not hasattr(self, "negcat"):
            self.negcat = self.sb("negcat", [128, 2, 256], F32)
            self.cw = self.sb("cw", [128, 160], F32)
            self.ngt = self.sb("ngt", [128, 256], F32)
            self.gsc = self.sb("gsc", [128, 4], F32)
            self.pmask = self.sb("pmaskt", [128, 4], F32)
            self.lvs = self.sb("lvst", [128, 128], F32)
            self.Bbc = Buf("bconst")
            zer = self.slabf[:, 0:128]
            Bz = Buf("zer")
            self.MSET("pool", zer, 0.0, [Bz])
            for d_, (pat, cm) in enumerate((([[1, 128]], -1), ([[-1, 128]], 1))):
                for kk, cmp in enumerate((ALU.is_ge, ALU.is_gt)):
                    self.P.op("pool", (lambda o, pat, cm, cmp: lambda e: e.affine_select(
                        out=o, in_=zer, pattern=pat, compare_op=cmp, fill=NEG, base=0, channel_multiplier=cm))(
                        self.negcat[:, d_, kk * 128:(kk + 1) * 128], pat, cm, cmp), [Bz], [self.Bbc])
        self.DMA("sp", self.pmask[:], self.pmask_d[:, :], [], [self.Bbc])
        self.DMA("sp", self.lvs[:], self.lvs_d[:, :], [], [self.Bbc])
        self.DMA("sp", self.cw[:], self.b_convT[:, j * 160:(j + 1) * 160], [], [self.Bbc])
        self.DMA("sp", self.ngt[:], self.b_norm_g[j:j + 1, :].partition_broadcast(128), [], [self.Bbc])
        self.DMA("sp", self.gsc[:, 0:1], self.b_dtbP[j], [], [self.Bbc])
        self.DMA("sp", self.gsc[:, 2:3], self.b_alogP[j], [], [self.Bbc])
        self.ACT(self.gsc[:, 3:4], self.gsc[:, 2:3], AF.Exp, [self.Bbc], [self.Bbc])
        self.TS("dve", self.gsc[:, 1:2], self.gsc[:, 3:4], -1.0, None, ALU.mult, None, [self.Bbc], [self.Bbc])

    def layer_b(self, i):
        j = i // 2
        last = (i == self.depth - 1)
        sf, sbb = self.slabf, self.slabb
        wor = self.sb_wo[:].rearrange("p a b -> p (a b)")
        self.layer_b_consts(j)
        ident, identb = self.ident, self.identb
        FEAT = sf[:, 0:4096]
        Bfeat = Buf("feat")
        WABF = sf[:, 9728:10752].rearrange("p (k n) -> p k n", n=128)
        WAB = sbb[:, 47616:48640].rearrange("p (k n) -> p k n", n=128)
        Bwab = Buf("wab")
        self.MSET("dve", WABF, 0.0, [Bwab])
        for q4, c0 in enumerate((0, 32, 64, 96)):
            self.DMA("sp", WABF[:, :, c0:c0 + 8],
                     self.b_w_in[j, :, 6144 + q4 * 8:6144 + (q4 + 1) * 8].rearrange("(k p) n -> p k n", p=128), [], [Bwab])
        self.CP("dve", WAB, WABF, [Bwab], [Bwab])
        import os
        kb = int(os.environ.get("KB", "99"))
        self.kb = kb
        for g, G in enumerate(self.groups):
            self.P.barrier()
            if kb >= 1:
                self.b_phase0(i, j, g, G, FEAT, Bfeat, WAB, Bwab)
            self.P.barrier()
            if kb >= 2:
                self.b_heads(i, j, g, G, FEAT, Bfeat, wor)
            self.P.barrier()
            if kb >= 5:
                self.b_outproj(i, j, g, G, last)

    def b_phase0(self, i, j, g, G, FEAT, Bfeat, WAB, Bwab):
        sf, sbb = self.slabf, self.slabb
        c = G["cond"]
        XT = sf[:, 4096:8192].rearrange("p (s d) -> p s d", d=1024)
        T1 = sf[:, 8192:8704]
        T2 = sf[:, 8704:9216]
        T3 = sf[:, 9216:9728]
        MASK = sf[:, 10752:11264]
        HTb = sbb[:, 0:8192].rearrange("p (b k n) -> p b k n", k=8, n=512)
        Bxt, Bt = Buf("bxt"), Buf("bt")
        Bh = [Buf(), Buf()]
        Bm = Buf("mask")
        self.Bft = Buf("ftall")
        self.MSET("dve", MASK, 1.0, [Bm])
        self.MSET("dve", MASK.rearrange("p (a b) -> p a b", b=128)[:, :, 0:1], 0.0, [Bm])
        import os
        kp = int(os.environ.get("KP", "99"))
        for b4 in range(G["ntok"] // 512 if kp >= 1 else 0):
            hb = b4 % 2
            cols = slice(b4 * 512, (b4 + 1) * 512)
            self.load_x_tile(i, g, b4 * 4, XT, Bxt, 4)
            self.make_ht(i, c, XT, Bxt, 4, HTb[:, hb], Bh[hb])
            for hh in range(2):
                self.DMA("sp", self.HTd[g][b4 * 2 + hh].rearrange("p (k n) -> p k n", n=256), HTb[:, hb, :, hh * 256:(hh + 1) * 256],
                         [Bh[hb]], [self.BHT[g][b4 * 4 + hh * 2], self.BHT[g][b4 * 4 + hh * 2 + 1]])
            if kp < 2:
                continue
            pt, bp = self.ps()
            for kc in range(8):
                self.MM(pt[:, :], WAB[:, kc, :], HTb[:, hb, kc, :], kc == 0, kc == 7, [Bwab, Bh[hb]], [bp])
            if kp < 3:
                continue
            self.ACT(T1[0:64, :], pt[0:64, :], AF.Exp, [bp, self.Bbc], [Bt], bias=self.gsc[0:64, 0:1])
            self.ACT(T1[0:64, :], T1[0:64, :], AF.Ln, [Bt, self.Bc], [Bt], bias=self.epsc[0:64, 1:2])
            self.TS("dve", T1[0:64, :], T1[0:64, :], self.gsc[0:64, 1:2], None, ALU.mult, None, [Bt, self.Bbc], [Bt])
            self.ACT(T1[64:128, :], pt[64:128, :], AF.Exp, [bp], [Bt], scale=-1.0)
            self.ACT(T1[64:128, :], T1[64:128, :], AF.Ln, [Bt, self.Bc], [Bt], bias=self.epsc[64:128, 1:2])
            self.TS("dve", T1[64:128, :], T1[64:128, :], -1.0, None, ALU.mult, None, [Bt], [Bt])
            if kp < 4:
                continue
            self.P.op("dve", (lambda o, m, d: lambda e: e.tensor_tensor_scan(out=o, data0=m, data1=d, initial=0.0, op0=ALU.mult, op1=ALU.add))(
                T2, MASK, T1), [Bt, Bm], [Bt])
            self.TT("dve", T3, T1, T2, ALU.subtract, [Bt], [Bt])
            self.TT("dve", T3.rearrange("p (a b) -> p a b", b=128), T3.rearrange("p (a b) -> p a b", b=128),
                    T2.rearrange("p (a b) -> p a b", b=128)[:, :, 127:128].broadcast_to([128, 4, 128]), ALU.add, [Bt], [Bt])
            pm = self.pmask
            self.TS("dve", FEAT[:, cols], T2, pm[:, 0:1], None, ALU.mult, None, [Bt, self.Bbc], [Bfeat])
            self.STT(FEAT[:, cols], T3, pm[:, 1:2], FEAT[:, cols], ALU.mult, ALU.add, [Bt, self.Bbc, Bfeat], [Bfeat])
            self.STT(FEAT[:, cols], T1, pm[:, 2:3], FEAT[:, cols], ALU.mult, ALU.add, [Bt, self.Bbc, Bfeat], [Bfeat])
            FTall = sf[:, 11264:12288].rearrange("p (t n) -> p t n", n=32)
            ptf, bpf_ = self.ps()
            for s_ in range(4):
                self.TR(ptf[:, s_ * 128:(s_ + 1) * 128], FEAT[:, b4 * 512 + s_ * 128:b4 * 512 + (s_ + 1) * 128], self.ident[:],
                        [Bfeat, self.Bc], [bpf_])
            for s_ in range(4):
                self.CP("dve", FTall[:, b4 * 4 + s_, :].rearrange("p (a b) -> p a b", b=8),
                        ptf[:, s_ * 128:(s_ + 1) * 128].rearrange("p (a b) -> p a b", b=32)[:, :, 0:8], [bpf_], [self.Bft])

    def b_heads(self, i, j, g, G, FEAT, Bfeat, wor):
        sf, sbb = self.slabf, self.slabb
        ident, identb = self.ident, self.identb
        n_seq, L, ntok = G["n_seq"], G["L"], G["ntok"]
        nt = ntok // 128
        tps = L // 128
        ACC = sf[:, 4096:4608]
        A_ = sf[:, 4608:5120]
        RN = sf[:, 5120:5632]
        EX8 = sf[:, 5632:7680].rearrange("p (b n) -> p b n", n=256)
        OS2 = sf[:, 7680:8192].rearrange("p (b n) -> p b n", n=256)
        S8 = sf[:, 8192:10240].rearrange("p (b n) -> p b n", n=256)
        Y12 = sf[:, 10240:10752].rearrange("p (b n) -> p b n", n=256)
        if not hasattr(self, "scal8"):
            self.scal8 = self.sb("scal8", [128, 64], F32)
        SCAL8 = self.scal8[:].rearrange("p (b n) -> p b n", n=8)
        SELD = sf[:, 10752:11264].rearrange("p (d k n) -> p d k n", k=2, n=128)
        FTall = sf[:, 11264:12288].rearrange("p (t n) -> p t n", n=32)
        fst = self.stat
        HTb = sbb[:, 0:4096].rearrange("p (b k n) -> p b k n", k=8, n=256)
        WH = sbb[:, 4096:10240].rearrange("p (k n) -> p k n", n=768)
        PRE = sbb[:, 10240:18432].rearrange("p (b n) -> p b n", n=4096)
        SQ = sbb[:, 18432:18944]
        QT = sbb[:, 18944:23040]
        KT = sbb[:, 23040:27136]
        V = sbb[:, 27136:35328].rearrange("p (t n) -> p t n", n=256)
        KTOK = sbb[:, 35328:39424].rearrange("p (t n) -> p t n", n=128)
        ZS = sbb[:, 39424:47616].rearrange("p (t n) -> p t n", n=256)
        VT = sbb[:, 48640:49152]
        YB2 = sbb[:, 49152:49664].rearrange("p (b n) -> p b n", n=256)
        YT2 = sbb[:, 49664:50176].rearrange("p (b n) -> p b n", n=256)
        O = wor[:, 0:nt * 256].rearrange("p (t n) -> p t n", n=256)
        jb = nt * 256
        NSET = 8
        SETW = 2304
        nch = n_seq * 2
        U8 = wor[:, jb:jb + 2048].rearrange("p (b n) -> p b n", n=256)
        b1 = jb + 2048
        VN2 = wor[:, b1:b1 + 512].rearrange("p (b n) -> p b n", n=256)
        VB2 = wor[:, b1 + 512:b1 + 1024].rearrange("p (b n) -> p b n", n=256)
        Sb8 = wor[:, b1 + 1024:b1 + 1024 + nch * 256].rearrange("p (b n) -> p b n", n=256)
        assert b1 + 1024 + nch * 256 <= 16384, (b1, nch)
        sets = []
        for q in range(NSET):
            b0 = q * SETW
            sets.append(dict(
                QKNT=sbb[:, b0:b0 + 256], NTt=sbb[:, b0 + 256:b0 + 384], NA=sbb[:, b0 + 128:b0 + 384],
                DC=sbb[:, b0 + 384:b0 + 896].rearrange("p (b n) -> p b n", n=256),
                MC=sbb[:, b0 + 896:b0 + 1152],
                LC=sbb[:, b0 + 1152:b0 + 1664].rearrange("p (b n) -> p b n", n=256),
                KBG=sbb[:, b0 + 1664:b0 + 1792], WT=sbb[:, b0 + 1792:b0 + 1920],
                QDT=sbb[:, b0 + 1920:b0 + 2048], KD=sbb[:, b0 + 2048:b0 + 2176], EG=sbb[:, b0 + 2176:b0 + 2304],
                qi=q, B=Buf(f"set{q}"), U=U8[:, q, :], EX=EX8[:, q, :], SCAL=SCAL8[:, q, :], BU=Buf(), BQ=Buf(), BL=Buf()))
        Bsel = Buf("sel")
        BVN = [Buf(), Buf()]
        BVB = [Buf(), Buf()]
        BS = [Buf() for _ in range(nch)]
        BO = [Buf() for _ in range(nt)]
        BOS = [Buf(), Buf()]
        BY = [Buf(), Buf()]
        Bh = [Buf(), Buf()]
        Bwh, Bacc, Bsq = Buf("wh"), Buf("acc"), Buf("sq")
        BPRE = [Buf(), Buf()]
        Bq, Bk, Bv, Bkt, Bz, Bvt = Buf("QT"), Buf("KT"), Buf("V"), Buf("KTOK"), Buf("ZS"), Buf("VT")
        Bpsh = [Buf() for _ in range(8)]
        pshc = [0]

        def psh_next():
            pt_, bp_ = self.ps()
            return pt_[:, 0:64].bitcast(BF16), bp_

        wsrc = self.wb_in[j]
        Bwsrc = self.Bw[("b_in", j)]
        cnt = [0]
        for hd in range(NH):
            for (c0, n, o) in ((hd * 128, 128, 0), (1024 + hd * 128, 128, 128), (2048 + hd * 256, 256, 256), (4096 + hd * 256, 256, 512)):
                self.DMA("sp", WH[:, :, o:o + n], wsrc[:, c0:c0 + n].rearrange("(k p) n -> p k n", p=128), [Bwsrc], [Bwh])
            for pas in range(2):
                for b2 in range(ntok // 256):
                    hb = b2 % 2
                    cols = slice(b2 * 256, (b2 + 1) * 256)
                    self.DMA("sp", HTb[:, hb], self.HTd[g][b2].rearrange("p (k n) -> p k n", n=256),
                             [self.BHT[g][b2 * 2], self.BHT[g][b2 * 2 + 1]], [Bh[hb]])
                    for cc in range(2):
                        wo = (pas * 2 + cc) * 128
                        pt, bp = self.ps()
                        for kc in range(8):
                            self.MM(pt[:, 0:256], WH[:, kc, wo:wo + 128], HTb[:, hb, kc, :], kc == 0, kc == 7, [Bwh, Bh[hb]], [bp])
                        self.CP("act", PRE[:, cc, cols], pt[:, 0:256], [bp], [BPRE[cc]])
                    if pas == 0:
                        for s2 in range(2):
                            t = b2 * 2 + s2
                            pt, bp = self.ps()
                            for kc in range(8):
                                self.MM(pt[:, 0:256], HTb[:, hb, kc, s2 * 128:(s2 + 1) * 128], WH[:, kc, 512:768], kc == 0, kc == 7,
                                        [Bwh, Bh[hb]], [bp])
                            self.ACT(ZS[:, t, :], pt[:, 0:256], AF.Silu, [bp], [Bz])
                import os
                kh = int(os.environ.get("KH", "99"))
                for cc in range(2 if kh >= 2 else 0):
                    kind = ("q", "k")[cc] if pas == 0 else "v"
                    cwi = (hd if kind == "q" else 8 + hd) if pas == 0 else 16 + hd * 2 + cc
                    wv = self.cw[:, cwi * 5:(cwi + 1) * 5]
                    for s_ in range(n_seq):
                        for a in range(0, L, 512):
                            bnd = min(a + 512, L)
                            n = bnd - a
                            base = s_ * L
                            self.TS("dve", ACC[:, 0:n], PRE[:, cc, base + a:base + bnd], wv[:, 2:3], None, ALU.mult, None,
                                    [BPRE[cc], self.Bbc], [Bacc])
                            for off in (-2, -1, 1, 2):
                                lo = max(a, -off) if off < 0 else a
                                hi = min(bnd, L - off) if off > 0 else bnd
                                if hi <= lo:
                                    continue
                                self.STT(ACC[:, lo - a:hi - a], PRE[:, cc, base + lo + off:base + hi + off], wv[:, off + 2:off + 3],
                                         ACC[:, lo - a:hi - a], ALU.mult, ALU.add, [BPRE[cc], self.Bbc, Bacc], [Bacc])
                            gcols = slice(base + a, base + bnd)
                            if kh < 3:
                                continue
                            if kind == "v":
                                self.ACT(VT[:, 0:n], ACC[:, 0:n], AF.Silu, [Bacc], [Bvt])
                                for q in range(n // 128):
                                    t = (base + a) // 128 + q
                                    ph, bph = psh_next()
                                    self.TR(ph, VT[:, q * 128:(q + 1) * 128], identb[:], [Bvt, self.Bc], [bph])
                                    self.CP("dve" if q % 2 else "pool" if False else "dve", V[:, t, cc * 128:(cc + 1) * 128], ph, [bph], [Bv])
                            elif kh >= 4:
                                self.ACT(A_[:, 0:n], ACC[:, 0:n], AF.Silu, [Bacc], [Bacc])
                                self.TT("pool", SQ[:, 0:n], A_[:, 0:n], A_[:, 0:n], ALU.mult, [Bacc], [Bsq])
                                pt, bp = self.ps()
                                self.MM(pt[:, 0:n], self.onesb[:], SQ[:, 0:n], True, True, [self.Bc, Bsq], [bp])
                                self.ACT(RN[:, 0:n], pt[:, 0:n], AF.Sqrt, [bp, self.Bc], [Bsq], bias=self.epsc[:, 0:1])
                                self.P.op("dve", (lambda o: lambda e: e.reciprocal(out=o, in_=o))(RN[:, 0:n]), [Bsq], [Bsq])
                                dst, Bd = (QT, Bq) if kind == "q" else (KT, Bk)
                                self.STT(dst[:, gcols], A_[:, 0:n], (DK ** -0.5) if kind == "q" else 1.0, RN[:, 0:n], ALU.mult, ALU.mult,
                                         [Bacc, Bsq], [Bd])
                                if kind == "k" and kh >= 5:
                                    for q in range(n // 128):
                                        t = (base + a) // 128 + q
                                        ph, bph = psh_next()
                                        self.TR(ph, KT[:, t * 128:(t + 1) * 128], identb[:], [Bk, self.Bc], [bph])
                                        self.CP("dve", KTOK[:, t, :], ph, [bph], [Bkt])
            if g == 0 and hd == 0:
                self.dump(FEAT[:, 0:512], 0, 512, [Bfeat])
                self.dump(QT[:, 0:512], 512, 512, [Bq])
                self.dump(KT[:, 0:512], 1024, 512, [Bk])
                self.dump(V[:, 0, :], 1536, 256, [Bv])
                self.dump(V[:, 1, :], 1792, 256, [Bv])
                self.dump(KTOK[:, 0, :], 2048, 128, [Bkt])
                self.dump(ZS[:, 0, :], 2176, 256, [Bz])
            if self.kb < 3:
                continue
            for s_ in range(n_seq):
                for d_ in range(2):
                    ch = s_ * 2 + d_
                    if g == 1:
                        self.DMA("sp", S8[:, ch, :], self.s0[j, d_, hd], [], [BS[ch]])
                    else:
                        self.MSET("dve", S8[:, ch, :], 0.0, [BS[ch]])
                    self.CP("act", Sb8[:, ch, :], S8[:, ch, :], [BS[ch]], [BS[ch]])
            self.P.barrier()
            for d_ in range(2):
                r = d_ * 32 + hd
                self.CP("dve", SELD[:, d_, 0, :], ident[:, r:r + 1].broadcast_to([128, 128]), [self.Bc], [Bsel])
                self.TT("dve", SELD[:, d_, 1, :], SELD[:, d_, 0, :], ident[:, 64 + r:64 + r + 1].broadcast_to([128, 128]), ALU.add,
                        [self.Bc, Bsel], [Bsel])
            visited = set()
            done = set()

            def visit(t, d_, ch, step, st_):
                r = d_ * 32 + hd
                tc = slice(t * 128, (t + 1) * 128)
                B_ = st_["B"]
                SC = st_["SCAL"]
                bank, bbk = self.psb[st_["qi"]], self.Bps[st_["qi"]]
                SEL1, SEL2 = SELD[:, d_, 0, :], SELD[:, d_, 1, :]
                GCJ = FTall[:, t, d_ * 8 + hd:d_ * 8 + hd + 1]
                LBJ = FTall[:, t, 16 + d_ * 8 + hd:16 + d_ * 8 + hd + 1]
                self.MM(bank[:, 0:128], SEL1, FEAT[:, tc], True, True, [Bsel, Bfeat], [bbk])
                self.MM(bank[:, 128:256], SEL2, FEAT[:, tc], True, True, [Bsel, Bfeat], [bbk])
                colT = t * 128 + (127 if d_ == 0 else 0)
                self.MM(bank[:, 256:257], SEL1, FEAT[:, colT:colT + 1], True, True, [Bsel, Bfeat], [bbk])
                yield
                self.CP("dve", SC[:, 2:3], bank[:, 256:257], [bbk], [B_])
                self.STT(st_["EX"], bank[:, 0:256], GCJ, self.negcat[:, d_, :], ALU.subtract, ALU.add, [bbk, B_, self.Bbc, self.Bft], [B_])
                self.ACT(st_["EG"], bank[:, 0:128], AF.Exp, [bbk], [B_])
                yield
                self.ACT(SC[:, 3:4], LBJ, AF.Exp, [B_, self.Bft], [B_])
                self.ACT(SC[:, 4:5], GCJ, AF.Exp, [B_, self.Bft], [B_], bias=LBJ)
                self.ACT(SC[:, 5:6], GCJ, AF.Exp, [B_, self.Bft], [B_], bias=SC[:, 2:3], scale=-1.0)
                self.ACT(SC[:, 6:7], SC[:, 2:3], AF.Exp, [B_], [B_])
                self.ACT(st_["EX"], st_["EX"], AF.Exp, [B_], [B_])
                self.MM(bank[:, 0:128], KT[:, tc], QT[:, tc], True, True, [Bk, Bq], [bbk])
                self.MM(bank[:, 128:256], KT[:, tc], KT[:, tc], True, True, [Bk], [bbk])
                yield
                self.TT("dve", st_["QKNT"], bank[:, 0:256], st_["EX"], ALU.mult, [bbk, B_], [B_])
                self.TT("pool", st_["QDT"], QT[:, tc], st_["EG"], ALU.mult, [Bq, B_], [st_["BQ"]])
                self.ACT(st_["KD"], KTOK[:, t, :], AF.Copy, [Bkt, B_], [st_["BQ"]], scale=SC[:, 5:6])
                self.ACT(st_["KBG"], KTOK[:, t, :], AF.Copy, [Bkt, B_], [st_["BQ"]], scale=SC[:, 4:5])
                yield
                N = st_["QKNT"][:, 128:256]
                ph = bank[:, 0:64].bitcast(BF16)
                self.TR(ph, N, identb[:], [B_, self.Bc], [bbk])
                yield
                self.CP("act", st_["NTt"], ph, [bbk], [B_])
                yield
                Am = st_["NTt"]
                LVS = self.lvs[:]
                BL = st_["BL"]

                NA = st_["NA"].rearrange("p (b n) -> p b n", n=128)
                LVS2 = self.lvs[:].unsqueeze(1).broadcast_to([128, 2, 128])
                ID2 = identb[:].unsqueeze(1).broadcast_to([128, 2, 128])

                def mk_l(k, lb):
                    self.STT(st_["LC"][:, lb, :].rearrange("p (b n) -> p b n", n=128), LVS2, float(k), NA, ALU.is_equal, ALU.mult,
                             [self.Bbc, B_], [BL])
                mk_l(0, 0)
                self.TT("dve", st_["DC"][:, 0, :].rearrange("p (b n) -> p b n", n=128), ID2,
                        st_["LC"][:, 0, :].rearrange("p (b n) -> p b n", n=128), ALU.subtract, [self.Bc, BL], [B_])
                dcur = 0
                for k in range(1, 7):
                    lb = k % 2
                    mk_l(k, lb)
                    yield
                    Dt_, D_ = st_["DC"][:, dcur, 0:128], st_["DC"][:, dcur, 128:256]
                    self.MM(bank[:, 0:128], st_["LC"][:, lb, 128:256], Dt_, True, True, [BL, B_], [bbk])
                    self.MM(bank[:, 128:256], st_["LC"][:, lb, 0:128], D_, True, True, [BL, B_], [bbk])
                    yield
                    self.CP("act", st_["MC"], bank[:, 0:256], [bbk], [B_])
                    yield
                    self.MM(bank[:, 0:128], D_, st_["MC"][:, 0:128], True, True, [B_], [bbk])
                    self.MM(bank[:, 128:256], Dt_, st_["MC"][:, 128:256], True, True, [B_], [bbk])
                    yield
                    self.TT("dve", st_["DC"][:, 1 - dcur, :], st_["DC"][:, dcur, :], bank[:, 0:256], ALU.subtract, [B_, bbk], [B_])
                    dcur = 1 - dcur
                    yield
                TTm = st_["DC"][:, dcur, 0:128]
                vb = cnt[0] % 2
                cnt[0] += 1
                self.ACT(VB2[:, vb, :], V[:, t, :], AF.Copy, [Bv, B_], [BVB[vb]], scale=SC[:, 3:4])
                self.MM(bank[:, 0:256], TTm, VB2[:, vb, :], True, True, [B_, BVB[vb]], [bbk])
                self.MM(bank[:, 256:384], st_["KBG"], TTm, True, True, [B_, st_["BQ"]], [bbk])
                yield
                self.CP("act", st_["U"], bank[:, 0:256], [bbk], [st_["BU"]])
                self.CP("dve", st_["WT"], bank[:, 256:384], [bbk], [st_["BU"]])
                yield
                while step > 0 and (ch, step - 1) not in done:
                    yield
                self.MM(bank[:, 0:256], st_["WT"], Sb8[:, ch, :], True, True, [st_["BU"], BS[ch]], [bbk])
                yield
                vn = cnt[0] % 2
                cnt[0] += 1
                self.TT("dve", VN2[:, vn, :], st_["U"], bank[:, 0:256], ALU.subtract, [st_["BU"], bbk], [BVN[vn]])
                self.MM(bank[:, 0:256], st_["QDT"], Sb8[:, ch, :], True, False, [st_["BQ"], BS[ch]], [bbk])
                self.MM(bank[:, 0:256], st_["QKNT"][:, 0:128], VN2[:, vn, :], False, True, [B_, BVN[vn]], [bbk])
                second = t in visited
                visited.add(t)
                if not second:
                    self.CP("act", O[:, t, :], bank[:, 0:256], [bbk], [BO[t]])
                else:
                    ob = cnt[0] % 2
                    cnt[0] += 1
                    OSb = OS2[:, ob, :]
                    self.TT("dve", OSb, bank[:, 0:256], O[:, t, :], ALU.add, [bbk, BO[t]], [BOS[ob]])
                self.MM(bank[:, 0:256], st_["KD"], VN2[:, vn, :], True, True, [st_["BQ"], BVN[vn]], [bbk])
                self.STT(S8[:, ch, :], S8[:, ch, :], SC[:, 6:7], bank[:, 0:256], ALU.mult, ALU.add, [BS[ch], B_, bbk], [BS[ch]])
                self.CP("act", Sb8[:, ch, :], S8[:, ch, :], [BS[ch]], [BS[ch]])
                done.add((ch, step))
                if second:
                    st6 = fst[:, 0:6]
                    mv = fst[:, 6:8]
                    ms = fst[:, 8:9]
                    self.P.op("dve", (lambda a_: lambda e: e.bn_stats(out=st6, in_=a_))(OSb), [BOS[ob]], [self.Bstat])
                    self.P.op("dve", lambda e: e.bn_aggr(out=mv, in_=st6), [self.Bstat], [self.Bstat])
                    self.STT(ms, mv[:, 0:1], mv[:, 0:1], mv[:, 1:2], ALU.mult, ALU.add, [self.Bstat], [self.Bstat])
                    self.ACT(ms, ms, AF.Sqrt, [self.Bstat, self.Bc], [self.Bstat], bias=self.epsc[:, 0:1])
                    self.P.op("dve", lambda e: e.reciprocal(out=ms, in_=ms), [self.Bstat], [self.Bstat])
                    self.STT(Y12[:, ob, :], OSb, ms, self.ngt[:], ALU.mult, ALU.mult, [BOS[ob], self.Bstat, self.Bbc], [BY[ob]])
                    self.TT("pool", YB2[:, ob, :], Y12[:, ob, :], ZS[:, t, :], ALU.mult, [BY[ob], Bz], [BY[ob]])
                    for e2 in range(2):
                        ph2 = bank[:, e2 * 64:(e2 + 1) * 64].bitcast(BF16)
                        self.TR(ph2, YB2[:, ob, e2 * 128:(e2 + 1) * 128], identb[:], [BY[ob], self.Bc], [bbk])
                        self.CP("dve", YT2[:, ob, e2 * 128:(e2 + 1) * 128], ph2, [bbk], [BY[ob]])
                    self.DMA("sp", self.YTd[g][hd * 2:hd * 2 + 2, :, tc].rearrange("e p n -> p e n"),
                             YT2[:, ob, :].rearrange("p (e n) -> p e n", n=128), [BY[ob]], [self.BYT[g][hd][t]])

            jobs = []
            for step in range(tps if self.kb >= 4 else 0):
                for s_ in range(n_seq):
                    for d_ in range(2):
                        t = s_ * tps + (step if d_ == 0 else tps - 1 - step)
                        jobs.append((t, d_, s_ * 2 + d_, step))
            free = list(range(NSET))
            active = []
            ji = 0
            while ji < len(jobs) or active:
                while ji < len(jobs) and free:
                    q = free.pop(0)
                    t, d_, ch, step = jobs[ji]
                    ji += 1
                    active.append((visit(t, d_, ch, step, sets[q]), q))
                nxt = []
                for gen, q in active:
                    try:
                        next(gen)
                        nxt.append((gen, q))
                    except StopIteration:
                        free.append(q)
                active = nxt
            self.P.barrier()
            if g == 0:
                for s_ in range(n_seq):
                    for d_ in range(2):
                        ch = s_ * 2 + d_
                        self.DMA("sp", self.ns[s_, j, d_, hd], S8[:, ch, :], [BS[ch]], [self.BNS])

    def b_outproj(self, i, j, g, G, last):
        sf, sbb = self.slabf, self.slabb
        c = G["cond"]
        WO = self.sb_wo[:]
        Bwo = Buf("wo")
        self.DMA("sp", WO, self.wb_out[j].rearrange("(k p) n -> p k n", p=128), [self.Bw[("b_out", j)]], [Bwo])
        XT2 = sf[:, 4096:6144].rearrange("p (b n) -> p b n", n=1024)
        R12 = sf[:, 6144:8192].rearrange("p (b n) -> p b n", n=1024)
        YT16 = sbb[:, 0:4096].rearrange("p (b e n) -> p b e n", e=16, n=128)
        Bx = [Buf(), Buf()]
        Br = [Buf(), Buf()]
        Byt = [Buf(), Buf()]
        for t in range(G["ntok"] // 128):
            b = t % 2
            tc = slice(t * 128, (t + 1) * 128)
            self.DMA("sp", XT2[:, b, :], self.X[g][tc, :], [self.BX[g][t]], [Bx[b]])
            self.DMA("sp", YT16[:, b], self.YTd[g][:, :, tc].rearrange("e p n -> p e n"), [self.BYT[g][h][t] for h in range(NH)], [Byt[b]])
            pts = []
            for half in range(2):
                pt, bp = self.ps()
                for ec in range(16):
                    self.MM(pt[:, :], YT16[:, b, ec, :], WO[:, ec, half * 512:(half + 1) * 512], ec == 0, ec == 15, [Byt[b], Bwo], [bp])
                pts.append((pt, bp))
            self.epilogue(i, g, c, t, pts, XT2[:, b, :], Bx[b], R12[:, b, :], R12[:, b, :], Br[b], last)


def build_program(NP, LP, LS, depth=4, dbg=False):
    import os
    dbg = dbg or bool(os.environ.get('KDBG'))
    b = Builder(NP, LP, LS, depth, dbg)
    b.wst = b.sb("wst", [128, 1024], BF16)
    b.BVG4 = [Buf() for _ in range(4)]
    b.BVH4 = [Buf() for _ in range(4)]
    return b.build()


def _pos_table(n_tokens, grid_w=64):
    def sincos(pos, dim):
        omega = (1.0 / (np.float32(10000.0) ** (np.arange(dim // 2, dtype=np.float32) / np.float32(dim // 2)))).astype(np.float32)
        ang = pos.astype(np.float32)[:, None] * omega[None, :]
        return np.concatenate([np.sin(ang), np.cos(ang)], axis=-1).astype(np.float32)
    rows = n_tokens // grid_w
    er = sincos(np.arange(rows), D // 2)
    ec = sincos(np.arange(grid_w), D // 2)
    emb = np.concatenate([np.broadcast_to(er[:, None, :], (rows, grid_w, D // 2)),
                          np.broadcast_to(ec[None, :, :], (rows, grid_w, D // 2))], axis=-1)
    return np.ascontiguousarray(emb.reshape(rows * grid_w, D), dtype=np.float32)


def prep_shared(inp, depth):
    nA, nB = (depth + 1) // 2, depth // 2
    f = lambda a: np.ascontiguousarray(a, dtype=np.float32)
    out = {}
    out["w_ada"] = f(inp["w_ada"][:depth])
    out["b_adaT"] = f(inp["b_ada"][:depth].reshape(depth, 24, 128).transpose(2, 0, 1).reshape(128, depth * 24))
    out["ln_g"] = f(inp["ln_g"][:depth])
    out["ln_b"] = f(inp["ln_b"][:depth])
    out["a_w_in"] = f(inp["a_w_in"][:nA])
    out["a_ln_gT"] = f(inp["a_ln_g"][:nA].reshape(nA, 16, 128).transpose(2, 0, 1).reshape(128, nA * 16))
    out["a_ln_bT"] = f(inp["a_ln_b"][:nA].reshape(nA, 16, 128).transpose(2, 0, 1).reshape(128, nA * 16))
    out["a_w_sT"] = f(inp["a_w_s"][:nA].transpose(0, 3, 1, 2).reshape(nA, 128, 8 * 128))
    out["a_b_s"] = f(inp["a_b_s"][:nA].reshape(nA, 8 * 128))
    out["a_w_out"] = f(inp["a_w_out"][:nA])
    nb = max(nB, 1)
    out["b_w_in"] = f(inp["b_w_in"][:nb])
    out["b_convT"] = f(inp["b_conv_w"][:nb].reshape(nb, 5, 32, 128).transpose(3, 0, 2, 1).reshape(128, nb * 32 * 5))
    al = np.zeros((nb, 128, 1), np.float32)
    db = np.zeros((nb, 128, 1), np.float32)
    for j in range(nb):
        for d_ in range(2):
            al[j, d_ * 32:d_ * 32 + 8, 0] = inp["b_A_log"][j, d_]
            db[j, d_ * 32:d_ * 32 + 8, 0] = inp["b_dt_bias"][j, d_]
    out["b_alogP"] = al
    out["b_dtbP"] = db
    out["b_norm_g"] = f(inp["b_norm_g"][:nb])
    pm = np.zeros((128, 4), np.float32)
    pm[0:32, 0] = 1.0
    pm[32:64, 1] = 1.0
    pm[64:128, 2] = 1.0
    out["pmask"] = pm
    ii = np.arange(128)
    xr = ii[:, None] ^ ii[None, :]
    lv = np.full((128, 128), -1.0, np.float32)
    nz = xr > 0
    lv[nz] = np.floor(np.log2(xr[nz])).astype(np.float32)
    out["lvs"] = lv
    out["b_w_out"] = f(inp["b_w_out"][:nb])
    return out


def prep_core(inp, shared, core, NP, LP, LS, depth, n_per_group):
    nb = max(depth // 2, 1)
    b = core // n_per_group
    f = lambda a: np.ascontiguousarray(a, dtype=np.float32)
    m = dict(shared)
    m["xp"] = f(inp["x_prompt"][core * NP:(core + 1) * NP].reshape(NP * LP, D))
    m["xs"] = f(inp["x_sample"][b])
    m["pos"] = _pos_table(LS)
    cond2 = np.stack([inp["c_ctx"], inp["c"][b]], axis=0)
    m["condT"] = f(cond2.reshape(2, 8, 128).transpose(2, 1, 0).reshape(128, 16))
    m["s0"] = f(inp["state_delta"][b][:nb])
    return m


_CACHE = {}


def kernel(**inputs):
    inp = {k: np.asarray(v) for k, v in inputs.items()}
    NP, LP, LS, depth = 4, 256, 4096, 4
    key = (NP, LP, LS, depth)
    if key not in _CACHE:
        _CACHE[key] = build_program(NP, LP, LS, depth)
    nc = _CACHE[key]
    shared = prep_shared(inp, depth)
    in_maps = [prep_core(inp, shared, c, NP, LP, LS, depth, 4) for c in range(8)]
    res = run_bass_kernel_spmd(nc, in_maps, core_ids=list(range(8)))
    r = res.results
    y_prompt = np.concatenate([r[c]["yp"].reshape(NP, LP, D) for c in range(8)], axis=0)
    y_sample = np.stack([r[0]["ys"], r[4]["ys"]], axis=0)
    ns = np.concatenate([r[c]["ns"] for c in range(8)], axis=0)
    return (y_prompt.astype(np.float32), y_sample.astype(np.float32), ns.astype(np.float32))
```
